# Optimizing a Trainium2 kernel written in Bass

```python
import jax, jax.numpy as jnp
from jax import lax
import numpy as np

D_MODEL = 1024
BATCH = 16
SEQ = 4096
DEPTH = 1
DEC_BATCH = 8
DEC_SEQ = 32
PAST_LEN = 1024

CHUNK = 64
N_HEADS = 16
HEAD_DIM = D_MODEL // N_HEADS
ATT_DIM = N_HEADS * HEAD_DIM
CONV_DIM = D_MODEL
CONV_GROUPS = 16
CONV_WIDTH = 3
D_FF = ((8 * D_MODEL // 3 + 255) // 256) * 256
Q_BLOCK = 128
N_MOD = 6
EPS = 1e-6
SPLITS = [ATT_DIM, 2 * ATT_DIM, 3 * ATT_DIM,
          3 * ATT_DIM + CONV_DIM, 3 * ATT_DIM + 2 * CONV_DIM, 3 * ATT_DIM + 3 * CONV_DIM,
          3 * ATT_DIM + 3 * CONV_DIM + D_MODEL]
IN_DIM = 3 * ATT_DIM + 3 * CONV_DIM + 2 * D_MODEL

kernel_name = "hybrid_stickbreak_shortconv_convffn_step"


def rms_norm(x, gain):
    x32 = x.astype(jnp.float32)
    y = x32 * lax.rsqrt(jnp.mean(x32 * x32, axis=-1, keepdims=True) + EPS)
    return (y * gain.astype(jnp.float32)).astype(x.dtype)


def modulate(xn, shift, scale):
    return xn * (1 + scale[:, None, :]) + shift[:, None, :]


def causal_dwconv(u, prev, w, b):
    t = u.shape[1]
    u_pad = jnp.concatenate([prev.astype(u.dtype), u], axis=1)
    y = sum((w[i] * u_pad[:, i:i + t] for i in range(CONV_WIDTH)), b)
    return y, u_pad[:, u_pad.shape[1] - (CONV_WIDTH - 1):]


def stick_breaking(q, k, v, q_pos, k_pos):
    z = jnp.einsum("bqhd,bkhd->bhqk", q.astype(jnp.float32), k.astype(jnp.float32)) * (HEAD_DIM ** -0.5)
    mask = k_pos[None, :] < q_pos[:, None]
    log_keep = jnp.where(mask, jax.nn.log_sigmoid(-z), 0.0)
    suffix = lax.cumsum(log_keep, axis=3, reverse=True) - log_keep
    a = jnp.where(mask, jnp.exp(jax.nn.log_sigmoid(z) + suffix), 0.0)
    o = jnp.einsum("bhqk,bkhd->bqhd", a, v.astype(jnp.float32))
    return o.astype(v.dtype)


def stick_breaking_blocked(q, k, v):
    b, t, h, d = q.shape
    nb = t // Q_BLOCK
    qb = q.reshape(b, nb, Q_BLOCK, h, d).transpose(1, 0, 2, 3, 4)
    pb = jnp.arange(t).reshape(nb, Q_BLOCK)
    k_pos = jnp.arange(t)
    ob = lax.map(lambda a: stick_breaking(a[0], k, v, a[1], k_pos), (qb, pb))
    return ob.transpose(1, 0, 2, 3, 4).reshape(b, t, h, d)


def trunk_layer(x, c, past_k, past_v, conv_prev, ffn_prev,
                w_ada, b_ada, g_norm1, w_in, w_conv, b_conv, w_branch_a, w_branch_b, w_out,
                g_norm2, w_up, w_fconv, b_fconv, w_down):
    bsz, t, _ = x.shape
    mod = jax.nn.silu(c) @ w_ada + b_ada
    shift1, scale1, gate1, shift2, scale2, gate2 = jnp.split(mod, N_MOD, axis=-1)

    h = modulate(rms_norm(x, g_norm1), shift1, scale1)
    q, k, v, b_gate, c_gate, u_in, g_a, g_b = jnp.split(h @ w_in, SPLITS, axis=-1)
    q = q.reshape(bsz, t, N_HEADS, HEAD_DIM)
    k = k.reshape(bsz, t, N_HEADS, HEAD_DIM)
    v = v.reshape(bsz, t, N_HEADS, HEAD_DIM)
    if past_k is None:
        o_a = stick_breaking_blocked(q, k, v)
    else:
        p = past_k.shape[1]
        k_all = jnp.concatenate([past_k.astype(k.dtype), k], axis=1)
        v_all = jnp.concatenate([past_v.astype(v.dtype), v], axis=1)
        o_a = stick_breaking(q, k_all, v_all, p + jnp.arange(t), jnp.arange(p + t))
    conv_out, new_conv = causal_dwconv(c_gate * u_in, conv_prev, w_conv, b_conv)
    o_b = b_gate * conv_out
    merged = (jax.nn.sigmoid(g_a) * (o_a.reshape(bsz, t, ATT_DIM) @ w_branch_a)
              + jax.nn.sigmoid(g_b) * (o_b @ w_branch_b))
    x = x + gate1[:, None, :] * (merged @ w_out)

    h2 = modulate(rms_norm(x, g_norm2), shift2, scale2)
    up, gate = jnp.split(h2 @ w_up, 2, axis=-1)
    up_c, new_ffn = causal_dwconv(up, ffn_prev, w_fconv, b_fconv)
    x = x + gate2[:, None, :] * ((jax.nn.silu(up_c) * gate) @ w_down)
    return x, k, v, new_conv, new_ffn


def setup_inputs(seed: int = 0) -> dict:
    key = jax.random.key(seed)
    ks = iter(jax.random.split(key, 32))
    f32 = jnp.float32

    def nrm(shape, scale=1.0):
        return jax.random.normal(next(ks), shape, f32) * scale

    def gain(shape):
        return 1.0 + nrm(shape, 0.02)

    return {
        "x_prompt": nrm((BATCH, SEQ, D_MODEL)),
        "x_sample": nrm((DEC_BATCH, DEC_SEQ, D_MODEL)),
        "cache_k": nrm((DEPTH, DEC_BATCH, PAST_LEN, N_HEADS, HEAD_DIM)),
        "cache_v": nrm((DEPTH, DEC_BATCH, PAST_LEN, N_HEADS, HEAD_DIM)),
        "state_conv": nrm((DEPTH, DEC_BATCH, CONV_WIDTH - 1, CONV_DIM)),
        "state_ffn_conv": nrm((DEPTH, DEC_BATCH, CONV_WIDTH - 1, D_FF)),
        "c_prompt": nrm((BATCH, D_MODEL)),
        "c_sample": nrm((DEC_BATCH, D_MODEL)),
        "w_ada": nrm((DEPTH, D_MODEL, N_MOD * D_MODEL), 0.5 * D_MODEL ** -0.5),
        "b_ada": nrm((DEPTH, N_MOD * D_MODEL), 0.01),
        "g_norm1": gain((DEPTH, D_MODEL)),
        "w_in": nrm((DEPTH, D_MODEL, IN_DIM), D_MODEL ** -0.5),
        "w_conv": nrm((DEPTH, CONV_WIDTH, CONV_DIM), CONV_WIDTH ** -0.5),
        "b_conv": nrm((DEPTH, CONV_DIM), 0.01),
        "w_branch_a": nrm((DEPTH, ATT_DIM, D_MODEL), ATT_DIM ** -0.5),
        "w_branch_b": nrm((DEPTH, CONV_DIM, D_MODEL), CONV_DIM ** -0.5),
        "w_out": nrm((DEPTH, D_MODEL, D_MODEL), D_MODEL ** -0.5),
        "g_norm2": gain((DEPTH, D_MODEL)),
        "w_up": nrm((DEPTH, D_MODEL, 2 * D_FF), D_MODEL ** -0.5),
        "w_fconv": nrm((DEPTH, CONV_WIDTH, D_FF), CONV_WIDTH ** -0.5),
        "b_fconv": nrm((DEPTH, D_FF), 0.01),
        "w_down": nrm((DEPTH, D_FF, D_MODEL), D_FF ** -0.5),
        "g_final": gain((D_MODEL,)),
    }


def reference(x_prompt, x_sample, cache_k, cache_v, state_conv, state_ffn_conv, c_prompt, c_sample,
              w_ada, b_ada, g_norm1, w_in, w_conv, b_conv, w_branch_a, w_branch_b, w_out,
              g_norm2, w_up, w_fconv, b_fconv, w_down, g_final):
    bp = x_prompt.shape[0]
    xp, xs = x_prompt, x_sample
    kp, vp, cp, fp, ks_, vs_, cs_, fs_ = [], [], [], [], [], [], [], []
    for l in range(DEPTH):
        lw = (w_ada[l], b_ada[l], g_norm1[l], w_in[l], w_conv[l], b_conv[l], w_branch_a[l],
              w_branch_b[l], w_out[l], g_norm2[l], w_up[l], w_fconv[l], b_fconv[l], w_down[l])
        zero_conv = jnp.zeros((bp, CONV_WIDTH - 1, CONV_DIM), xp.dtype)
        zero_ffn = jnp.zeros((bp, CONV_WIDTH - 1, D_FF), xp.dtype)
        xp, k1, v1, c1, f1 = trunk_layer(xp, c_prompt, None, None, zero_conv, zero_ffn, *lw)
        xs, k2, v2, c2, f2 = trunk_layer(xs, c_sample, cache_k[l], cache_v[l], state_conv[l],
                                         state_ffn_conv[l], *lw)
        kp.append(k1); vp.append(v1); cp.append(c1); fp.append(f1)
        ks_.append(k2); vs_.append(v2); cs_.append(c2); fs_.append(f2)
    y_prompt = rms_norm(xp, g_final)
    y_sample = rms_norm(xs, g_final)
    return (y_prompt, y_sample,
            jnp.stack(kp), jnp.stack(vp), jnp.stack(cp), jnp.stack(fp),
            jnp.stack(ks_), jnp.stack(vs_), jnp.stack(cs_), jnp.stack(fs_))
```

```python
import numpy as np
from contextlib import ExitStack
import concourse.bass as bass
import concourse.mybir as mybir
from concourse.bass_utils import run_bass_kernel_spmd

F32 = mybir.dt.float32
BF16 = mybir.dt.bfloat16
AF = mybir.ActivationFunctionType
ALU = mybir.AluOpType

PE, ACT, DVE, POOL, SP = "pe", "act", "dve", "pool", "sp"
ENGS = [PE, ACT, DVE, POOL, SP]

D = 1024
DFF = 2816
NFF = 22
NH = 16
EPS = 1e-6


class Buf:
    def __init__(self, name):
        self.name = name
        self.last_w = None
        self.readers = []


class Op:
    __slots__ = ("eng", "fn", "deps", "signal", "dma_key", "dma_cnt", "is_dma")

    def __init__(self, eng, fn):
        self.eng = eng
        self.fn = fn
        self.deps = []
        self.signal = 0
        self.dma_key = None
        self.dma_cnt = 0
        self.is_dma = False


class Prog:
    def __init__(self):
        self.ops = {e: [] for e in ENGS}
        self.dma_keys = {}

    def _add_deps(self, op, reads, writes):
        deps = []
        for b in reads:
            if b.last_w is not None:
                deps.append(("raw", b.last_w))
        for b in writes:
            if b.last_w is not None:
                deps.append(("waw", b.last_w))
            for r in b.readers:
                deps.append(("war", r))
        for kind, d in deps:
            if d is op:
                continue
            if d.eng == op.eng and not d.is_dma and not op.is_dma:
                if op.eng == PE:
                    continue
            op.deps.append(d)
        for b in reads:
            b.readers.append(op)
        for b in writes:
            b.last_w = op
            b.readers = []

    def op(self, eng, fn, reads=(), writes=()):
        o = Op(eng, fn)
        self._add_deps(o, reads, writes)
        self.ops[eng].append(o)
        return o

    def dma(self, eng, fn, key, reads=(), writes=(), group=False):
        o = Op(eng, fn)
        o.is_dma = True
        st = self.dma_keys.setdefault(key, [0, None])
        if st[1] is not None and not group:
            o.deps.append(st[1])
        self._add_deps(o, reads, writes)
        st[0] += 16
        st[1] = o
        o.dma_key = key
        o.dma_cnt = st[0]
        self.ops[eng].append(o)
        return o

    def emit(self, nc):
        needed = set()
        for e in ENGS:
            for o in self.ops[e]:
                for d in o.deps:
                    if not d.is_dma:
                        needed.add(id(d))
        for e in ENGS:
            c = 0
            for o in self.ops[e]:
                if not o.is_dma and id(o) in needed:
                    c += 1
                    o.signal = c
        with ExitStack() as st:
            esem = {e: st.enter_context(nc.semaphore("s_" + e)) for e in ENGS}
            dsem = {k: st.enter_context(nc.semaphore("d_%d" % i)) for i, k in enumerate(self.dma_keys)}
            block = st.enter_context(nc.Block())
            prog = self

            def run(eng_name, eng):
                known = {}
                for o in prog.ops[eng_name]:
                    w = {}
                    for d in o.deps:
                        if d.is_dma:
                            k, v = ("d", d.dma_key), d.dma_cnt
                        else:
                            k, v = ("e", d.eng), d.signal
                        if v > w.get(k, 0):
                            w[k] = v
                    for k, v in w.items():
                        if known.get(k, 0) >= v:
                            continue
                        known[k] = v
                        eng.wait_ge(dsem[k[1]] if k[0] == "d" else esem[k[1]], v)
                    ins = o.fn(eng)
                    if o.is_dma:
                        ins.then_inc(dsem[o.dma_key], 16)
                    elif o.signal:
                        ins.then_inc(esem[eng_name], 1)
                if eng_name == SP:
                    for k, stt in prog.dma_keys.items():
                        if known.get(("d", k), 0) < stt[0]:
                            eng.wait_ge(dsem[k], stt[0])

            @block.tensor
            def _(e):
                run(PE, e)

            @block.scalar
            def _(e):
                run(ACT, e)

            @block.vector
            def _(e):
                run(DVE, e)

            @block.gpsimd
            def _(e):
                run(POOL, e)

            @block.sync
            def _(e):
                run(SP, e)


def build(SEQ=4096, NP=2, PAST=1024, SAMP=32):
    nc = bass.Bass("TRN2", target_bir_lowering=False)
    T = 512
    NT = SEQ // T
    TKMAX = max(SEQ, PAST + SAMP)
    NS = NP + 1
    P = Prog()

    def din(name, shape):
        return nc.dram_tensor(name, list(shape), F32, kind="ExternalInput").ap()

    def dout(name, shape):
        return nc.dram_tensor(name, list(shape), F32, kind="ExternalOutput").ap()

    xp = din("xp", [NP * SEQ, D]); xs = din("xs", [SAMP, D])
    ck = din("ck", [PAST, D]); cv = din("cv", [PAST, D])
    sc = din("sc", [2, D]); sf = din("sf", [2, DFF]); cvec = din("cvec", [NS, D])
    w_ada = din("w_ada", [D, 6 * D]); b_ada = din("b_ada", [6 * D]); g1 = din("g1", [D])
    w_in = din("w_in", [D, 8 * D]); w_conv = din("w_conv", [3, D]); b_conv = din("b_conv", [D])
    w_ba = din("w_ba", [D, D]); w_bb = din("w_bb", [D, D]); w_out = din("w_out", [D, D])
    g2 = din("g2", [D]); w_up = din("w_up", [D, 2 * DFF]); w_fconv = din("w_fconv", [3, DFF])
    b_fconv = din("b_fconv", [DFF]); w_down = din("w_down", [DFF, D]); gfin = din("gfin", [D])
    yp = dout("yp", [NP * SEQ, D]); ys = dout("ys", [SAMP, D])
    nkp = dout("nkp", [NP * SEQ, D]); nvp = dout("nvp", [NP * SEQ, D])
    ncp = dout("ncp", [NP * 2, D]); nfp = dout("nfp", [NP * 2, DFF])
    nks = dout("nks", [SAMP, D]); nvs = dout("nvs", [SAMP, D])
    ncs = dout("ncs", [2, D]); nfs = dout("nfs", [2, DFF])

    def dscr(name, shape):
        return nc.dram_tensor(name, list(shape), BF16).ap()

    wb_ada = dscr("wb_ada", [D, 6 * D]); wb_in = dscr("wb_in", [D, 3 * D])
    wb_cv = dscr("wb_cv", [D, 8 * 3 * 128])
    wb_mg = dscr("wb_mg", [D, 8 * 4 * 128])
    wb_out = dscr("wb_out", [D, D])
    wb_up = dscr("wb_up", [D, 2 * DFF])
    wb_dn = dscr("wb_dn", [8, 128, NFF * 128])
    kscr = dscr("kscr", [NS, 8, 128, TKMAX]); vscr = dscr("vscr", [NS, 8, TKMAX, 128])

    with ExitStack() as st:
        def sb(name, shape, dt=F32):
            return st.enter_context(nc.sbuf_tensor(name, list(shape), dt))

        ps = st.enter_context(nc.psum_tensor("ps", [128, 4096], F32))
        bank = [Buf("bank%d" % i) for i in range(8)]

        def bk(i, n=512, p=128):
            return ps[0:p, i * 512:i * 512 + n]

        ident = sb("ident", [128, 128]); tmpc = sb("tmpc", [128, 128])
        ntri = sb("ntri", [128, 128], BF16); nones = sb("nones", [128, 128], BF16)
        ones = sb("ones", [128, 128], BF16); m01 = sb("m01", [128, 128], BF16)
        b_const = Buf("const")
        P.op(POOL, lambda e: e.memset(ident[:], 1.0), writes=[b_const])
        P.op(POOL, lambda e: e.affine_select(out=ident[:], in_=ident[:], pattern=[[-1, 128]], compare_op=ALU.is_equal,
                                            fill=0.0, base=0, channel_multiplier=1), reads=[b_const], writes=[b_const])
        P.op(POOL, lambda e: e.memset(tmpc[:], -1.0), writes=[b_const])
        P.op(POOL, lambda e: e.affine_select(out=tmpc[:], in_=tmpc[:], pattern=[[-1, 128]], compare_op=ALU.is_ge,
                                            fill=0.0, base=0, channel_multiplier=1), reads=[b_const], writes=[b_const])
        P.op(DVE, lambda e: e.tensor_copy(ntri[:], tmpc[:]), reads=[b_const], writes=[b_const])
        b_c2 = Buf("const2")
        tmpd = sb("tmpd", [128, 128])
        P.op(POOL, lambda e: e.memset(tmpd[:], 1.0), writes=[b_c2])
        P.op(POOL, lambda e: e.affine_select(out=tmpd[:], in_=tmpd[:], pattern=[[1, 128]], compare_op=ALU.is_gt,
                                            fill=0.0, base=0, channel_multiplier=-1), reads=[b_c2], writes=[b_c2])
        P.op(DVE, lambda e: e.tensor_copy(m01[:], tmpd[:]), reads=[b_c2], writes=[b_c2])
        zer = sb("zer", [128, 512], BF16)
        P.op(POOL, lambda e: e.memset(zer[:], 0.0), writes=[b_c2])
        P.op(POOL, lambda e: e.memset(nones[:], -1.0), writes=[b_c2])
        P.op(POOL, lambda e: e.memset(ones[:], 1.0), writes=[b_c2])
        b_const_all = [b_const, b_c2]

        b_w = {n: [] for n in ("ada", "in", "cv", "mg", "out", "up", "down")}

        def cast(name, dst, src):
            bw = Buf("w_%s_%d" % (name, len(b_w[name])))
            b_w[name].append(bw)
            P.dma(POOL, lambda e: e.dma_start(out=dst, in_=src), "wcast_" + name, writes=[bw], group=True)

        for r0 in range(0, D, 128):
            cast("ada", wb_ada[r0:r0 + 128, :], w_ada[r0:r0 + 128, :])
        for r0 in range(0, D, 128):
            cast("in", wb_in[r0:r0 + 128, :], w_in[r0:r0 + 128, 0:3 * D])
        for r0 in range(0, D, 128):
            for gi in range(3):
                cast("cv", wb_cv[r0:r0 + 128, :].rearrange("p (m g n) -> p m g n", g=3, n=128)[:, :, gi, :],
                     w_in[r0:r0 + 128, (3 + gi) * D:(4 + gi) * D].rearrange("p (m n) -> p m n", n=128))
        for r0 in range(0, D, 128):
            for gi, srcw in enumerate((w_in[:, 6 * D:7 * D], w_in[:, 7 * D:8 * D], w_ba, w_bb)):
                cast("mg", wb_mg[r0:r0 + 128, :].rearrange("p (m g n) -> p m g n", g=4, n=128)[:, :, gi, :],
                     srcw[r0:r0 + 128, :].rearrange("p (m n) -> p m n", n=128))
        for r0 in range(0, D, 128):
            cast("out", wb_out[r0:r0 + 128, :], w_out[r0:r0 + 128, :])
        for r0 in range(0, D, 128):
            for gi in range(2):
                cast("up", wb_up[r0:r0 + 128, :].rearrange("p (f g n) -> p f g n", g=2, n=128)[:, :, gi, :],
                     w_up[r0:r0 + 128, gi * DFF:(gi + 1) * DFF].rearrange("p (f n) -> p f n", n=128))
        for m in range(8):
            for f0 in range(0, NFF, 11):
                cast("down", wb_dn[m, :, f0 * 128:(f0 + 11) * 128].rearrange("p (f n) -> p f n", n=128),
                     w_down[f0 * 128:(f0 + 11) * 128, m * 128:(m + 1) * 128].rearrange("(f p) n -> p f n", p=128))

        vr = [sb("vr%d" % i, [128, 128]) for i in range(3)]
        vT = [sb("vT%d" % i, [128, 128]) for i in range(3)]
        b_vr = [Buf("vr%d" % i) for i in range(3)]
        b_vT = [Buf("vT%d" % i) for i in range(3)]
        R = {}
        rowp = [0, 0, 0]

        def vload(key, ti, src2d, nrows):
            r0 = rowp[ti]
            R[key] = (ti, r0)
            P.dma(SP, lambda e: e.dma_start(out=vr[ti][r0:r0 + nrows, :], in_=src2d), "vload%d" % ti, writes=[b_vr[ti]], group=True)
            rowp[ti] += nrows

        def v2(ap1d, n):
            return ap1d.rearrange("(a b) -> a b", b=128)

        for s_ in range(NS):
            vload(("c", s_), 0, v2(cvec[s_, :], 8), 8)
        vload("b_ada", 0, v2(b_ada, 48), 48)
        vload("g1", 0, v2(g1, 8), 8); vload("g2", 0, v2(g2, 8), 8); vload("gf", 0, v2(gfin, 8), 8)
        for i in range(3):
            vload(("wc", i), 0, v2(w_conv[i, :], 8), 8)
        vload("bc", 0, v2(b_conv, 8), 8)
        for i in range(3):
            vload(("wf", i), 1, v2(w_fconv[i, :], NFF), NFF)
        vload("bf", 1, v2(b_fconv, NFF), NFF)
        for i in range(2):
            vload(("sc", i), 1, v2(sc[i, :], 8), 8)
        for i in range(2):
            vload(("sf", i), 2, v2(sf[i, :], NFF), NFF)
        for ti in range(3):
            n = rowp[ti]
            P.op(PE, lambda e, ti=ti, n=n: e.transpose(bk(ti, n), vr[ti][0:n, :], ident[0:n, 0:n]),
                 reads=[b_vr[ti]] + b_const_all, writes=[bank[ti]])
            P.op(DVE, lambda e, ti=ti, n=n: e.tensor_copy(vT[ti][:, 0:n], bk(ti, n)), reads=[bank[ti]], writes=[b_vT[ti]])

        def vcol(key, j=0):
            ti, r0 = R[key]
            return vT[ti][:, r0 + j:r0 + j + 1]

        def vcols(key, n):
            ti, r0 = R[key]
            return vT[ti][:, r0:r0 + n]

        NW = 3
        wsl = [sb("wsl%d" % i, [128, 4096], BF16) for i in range(NW)]
        b_wsl = [Buf("wsl%d" % i) for i in range(NW)]
        wctr = [0]

        def wload(pieces, deps):
            i = wctr[0] % NW
            wctr[0] += 1
            if len(pieces) == 1:
                dv0, src0 = pieces[0]
                kh = src0.shape[1] // 2
                pieces = [((lambda t, dv0=dv0: dv0(t)[:, 0:kh, :]), src0[:, 0:kh, :]),
                          ((lambda t, dv0=dv0: dv0(t)[:, kh:, :]), src0[:, kh:, :])]
            for n_, (dv, src) in enumerate(pieces):
                P.dma(SP, lambda e, dv=dv, src=src, i=i: e.dma_start(out=dv(wsl[i]), in_=src), "w%d" % i,
                      reads=deps, writes=[b_wsl[i]], group=(n_ > 0))
            return wsl[i], b_wsl[i]

        def wview_k(t, ncol, kc=8):
            return t[:, 0:kc * ncol].rearrange("p (k n) -> p k n", n=ncol)

        def wsrc(wb, c0, ncol):
            return wb.rearrange("(k p) n -> p k n", p=128)[:, :, c0:c0 + ncol]

        pctr = [0]

        def nextbank():
            i = pctr[0] % 8
            pctr[0] += 1
            return i

        csil = sb("csil", [128, 8, NS], BF16); b_csil = Buf("csil")
        modT = sb("modT", [128, 48, NS]); b_modT = Buf("modT")
        modv = sb("modv", [128, NS, 2, 8]); b_modv = Buf("modv")
        for s_ in range(NS):
            P.op(ACT, lambda e, s_=s_: e.activation(csil[:, :, s_], vcols(("c", s_), 8), AF.Silu), reads=[b_vT[0]], writes=[b_csil])
        mb = nextbank()
        for g in range(12):
            wt, wbuf = wload([(lambda t: wview_k(t, 512), wsrc(wb_ada, g * 512, 512))], b_w["ada"])
            wv = wview_k(wt, 512)
            for j in range(4):
                jj = g * 4 + j
                for kc in range(8):
                    P.op(PE, lambda e, wv=wv, j=j, kc=kc, jj=jj: e.matmul(ps[:, mb * 512 + 4 * jj:mb * 512 + 4 * jj + NS],
                                                                         lhsT=wv[:, kc, j * 128:(j + 1) * 128], rhs=csil[:, kc, :],
                                                                         start=(kc == 0), stop=(kc == 7), skip_group_check=True),
                         reads=[wbuf, b_csil], writes=[bank[mb]])
        for s_ in range(NS):
            P.op(DVE, lambda e, s_=s_: e.tensor_tensor(modT[:, :, s_], ps[:, mb * 512:mb * 512 + 192].rearrange("p (j f) -> p j f", f=4)[:, :, s_],
                                                       vcols("b_ada", 48), ALU.add),
                 reads=[bank[mb], b_vT[0]], writes=[b_modT])
        for s_ in range(NS):
            for kind, sc0, gk in ((0, 8, "g1"), (1, 32, "g2")):
                P.op(DVE, lambda e, s_=s_, kind=kind, sc0=sc0, gk=gk: e.scalar_tensor_tensor(
                    modv[:, s_, kind, :], modT[:, sc0:sc0 + 8, s_], 1.0, vcols(gk, 8), ALU.add, ALU.mult),
                    reads=[b_modT, b_vT[0]], writes=[b_modv])

        def mcol(j, s_):
            return modT[:, j, s_:s_ + 1]

        xin = [sb("xin%d" % i, [128, 1024]) for i in range(2)]; b_xin = [Buf("xin%d" % i) for i in range(2)]
        stg = [sb("stg%d" % i, [128, 512]) for i in range(4)]; b_stg = [Buf("stg%d" % i) for i in range(4)]
        xTs = [sb("xT%d" % i, [128, 8, T]) for i in range(2)]; b_xTs = [Buf("xT%d" % i) for i in range(2)]
        hT = sb("hT", [128, 8, T], BF16); b_hT = Buf("hT")
        qz = [sb("qz%d" % h, [128, 8, T], BF16) for h in range(2)]; b_qz = Buf("qz")
        kTs = sb("kTs", [128, 8, T], BF16); b_kTs = Buf("kTs")
        obT = sb("obT", [128, 8, T], BF16); b_obT = Buf("obT")
        oT = sb("oT", [128, 8, T], BF16); b_oT = Buf("oT")
        mT = sb("mT", [128, 8, T], BF16); b_mT = Buf("mT")
        big = sb("big", [128, 16384], BF16)
        b_big = [Buf("big%d" % i) for i in range(4)]
        kTh = [big[:, 0:4096], big[:, 4096:8192]]
        vh = [big[:, 8192:12288].rearrange("p (k d) -> p k d", d=128), big[:, 12288:16384].rearrange("p (k d) -> p k d", d=128)]
        actT = big[:, 0:NFF * T].rearrange("p (f t) -> p f t", t=T)
        sqs = [sb("sq%d" % i, [128, T], BF16) for i in range(2)]; b_sqs = [Buf("sq%d" % i) for i in range(2)]
        rstd = sb("rstd", [128, T]); b_rstd = Buf("rstd")
        lnt = sb("lnt", [128, T]); b_lnt = Buf("lnt")
        tmpA = [sb("tmpA%d" % i, [128, T]) for i in range(2)]; b_tmpA = [Buf("tmpA%d" % i) for i in range(2)]
        tmpB = [sb("tmpB%d" % i, [128, T]) for i in range(2)]; b_tmpB = [Buf("tmpB%d" % i) for i in range(2)]
        cu = [sb("cu%d" % i, [128, T + 2]) for i in range(2)]; b_cu = [Buf("cu%d" % i) for i in range(2)]
        cuh = sb("cuh", [128, 8, 2]); b_cuh = Buf("cuh")
        uph = sb("uph", [128, NFF, 2]); b_uph = Buf("uph")
        Et = [sb("Et%d" % i, [128, 2, T]) for i in range(2)]; b_E = [Buf("E%d" % i) for i in range(2)]
        Lt = [sb("Lt%d" % i, [128, 2, T], BF16) for i in range(2)]; b_L = [Buf("L%d" % i) for i in range(2)]
        At = [sb("At%d" % i, [128, 2, T], BF16) for i in range(2)]; b_A = [Buf("A%d" % i) for i in range(2)]
        Acc = [sb("Acc%d" % i, [128, 2, T], BF16) for i in range(3)]; b_Acc = [Buf("Acc%d" % i) for i in range(3)]
        b_kscr = [Buf("kscr%d" % s_) for s_ in range(NS)]
        b_vscr = [Buf("vscr%d" % s_) for s_ in range(NS)]
        for h in range(2):
            P.op(POOL, lambda e, h=h: e.memset(qz[h][:], 0.0), writes=[b_qz])
        ctr = {"xin": 0, "stg": 0, "tA": 0, "tB": 0, "cu": 0, "att": 0}

        def rr(key, n):
            i = ctr[key] % n
            ctr[key] += 1
            return i

        def rms_to_hT(Tn, s_, kind, xT, b_xT):
            rms_stats(Tn, xT, b_xT)
            if kind is not None:
                rms_apply(Tn, s_, kind, xT, b_xT)

        def rms_stats(Tn, xT, b_xT):
            sb_i = nextbank()
            for m in range(8):
                qi = m % 2
                if m % 2 == 0:
                    P.op(ACT, lambda e, m=m, qi=qi: e.activation(sqs[qi][:, 0:Tn], xT[:, m, 0:Tn], AF.Square), reads=[b_xT], writes=[b_sqs[qi]])
                else:
                    P.op(DVE, lambda e, m=m, qi=qi: e.tensor_tensor(sqs[qi][:, 0:Tn], xT[:, m, 0:Tn], xT[:, m, 0:Tn], ALU.mult),
                         reads=[b_xT], writes=[b_sqs[qi]])
                P.op(PE, lambda e, m=m, qi=qi: e.matmul(bk(sb_i, Tn), lhsT=ones[:], rhs=sqs[qi][:, 0:Tn], start=(m == 0), stop=(m == 7)),
                     reads=[b_sqs[qi]] + b_const_all, writes=[bank[sb_i]])
            P.op(ACT, lambda e: e.activation(lnt[:, 0:Tn], bk(sb_i, Tn), AF.Ln, scale=1.0 / D, bias=eps_t[:, 0:1]),
                 reads=[bank[sb_i], b_eps], writes=[b_lnt])
            P.op(ACT, lambda e: e.activation(rstd[:, 0:Tn], lnt[:, 0:Tn], AF.Exp, scale=-0.5), reads=[b_lnt], writes=[b_rstd])

        def rms_apply(Tn, s_, kind, xT, b_xT):
            shift0 = 0 if kind == 0 else 24
            for m in range(8):
                i = rr("tA", 2)
                P.op(DVE, lambda e, m=m, i=i: e.scalar_tensor_tensor(tmpA[i][:, 0:Tn], xT[:, m, 0:Tn], modv[:, s_, kind, m:m + 1], rstd[:, 0:Tn],
                                                                     ALU.mult, ALU.mult),
                     reads=[b_xT, b_rstd, b_modv], writes=[b_tmpA[i]])
                if m % 2 == 0:
                    P.op(POOL, lambda e, m=m, i=i: e.tensor_scalar(hT[:, m, 0:Tn], tmpA[i][:, 0:Tn], 1.0, mcol(shift0 + m, s_), ALU.mult, ALU.add),
                         reads=[b_tmpA[i], b_modT], writes=[b_hT])
                else:
                    P.op(ACT, lambda e, m=m, i=i: e.activation(hT[:, m, 0:Tn], tmpA[i][:, 0:Tn], AF.Identity, bias=mcol(shift0 + m, s_), scale=1.0),
                         reads=[b_tmpA[i], b_modT], writes=[b_hT])

        def fm_group(wv, j, rhs3, Tn, kcn, bi, rbufs, wbuf):
            for kc in range(kcn):
                P.op(PE, lambda e, kc=kc: e.matmul(bk(bi, Tn), lhsT=wv[:, kc, j * 128:(j + 1) * 128], rhs=rhs3[:, kc, 0:Tn],
                                                   start=(kc == 0), stop=(kc == kcn - 1)),
                     reads=[wbuf] + rbufs, writes=[bank[bi]])

        def conv3(dst_fn, src, Tn, wkey, bkey, j, i_out):
            P.op(DVE, lambda e: e.tensor_scalar(tmpB[i_out][:, 0:Tn], src[:, 2:Tn + 2], vcol((wkey, 2), j), vcol(bkey, j), ALU.mult, ALU.add),
                 reads=dst_fn[0], writes=[b_tmpB[i_out]])
            for tap in (1, 0):
                P.op(DVE, lambda e, tap=tap: e.scalar_tensor_tensor(tmpB[i_out][:, 0:Tn], src[:, tap:Tn + tap], vcol((wkey, tap), j),
                                                                     tmpB[i_out][:, 0:Tn], ALU.mult, ALU.add),
                     reads=dst_fn[0] + [b_tmpB[i_out]], writes=[b_tmpB[i_out]])

        eps_t = sb("eps_t", [128, 1]); b_eps = Buf("eps")
        P.op(POOL, lambda e: e.memset(eps_t[:], EPS), writes=[b_eps])

        tilectr = [0]

        def tile_front(Tn, xsrc, s_):
            par = tilectr[0] % 2
            tilectr[0] += 1
            xT, b_xT = xTs[par], b_xTs[par]
            tb = min(128, Tn)
            nb = Tn // tb
            for b in range(nb):
                xi = rr("xin", 2)
                P.dma(SP, lambda e, b=b, xi=xi: e.dma_start(out=xin[xi][0:tb, :], in_=xsrc[b * tb:(b + 1) * tb, :]), "xin%d" % xi,
                      writes=[b_xin[xi]])
                for hh in range(2):
                    bi = nextbank()
                    for j in range(4):
                        m = hh * 4 + j
                        P.op(PE, lambda e, j=j, m=m, xi=xi, bi=bi: e.transpose(ps[:, bi * 512 + j * 128:bi * 512 + j * 128 + tb],
                                                                               xin[xi][0:tb, m * 128:(m + 1) * 128], ident[0:tb, 0:tb]),
                             reads=[b_xin[xi]] + b_const_all, writes=[bank[bi]])
                    P.op(DVE, lambda e, hh=hh, b=b, bi=bi: e.tensor_copy(xT[:, hh * 4:hh * 4 + 4, b * tb:(b + 1) * tb],
                                                                        bk(bi).rearrange("p (j t) -> p j t", t=128)[:, :, 0:tb]),
                         reads=[bank[bi]], writes=[b_xT])
            rms_stats(Tn, xT, b_xT)
            rms_apply(Tn, s_, 0, xT, b_xT)
            return xT, b_xT

        def process_tile(s_, Tn, xsrc, ysrc, koutsrc, voutsrc, kpos0, units_blocks, last, conv_out, ffn_out, prev_finish=None,
                         front=None, next_front_fn=None):
            pre = front is not None
            if front is None:
                qw = [wload([(lambda t: wview_k(t, 512), wsrc(wb_in, g * 512, 512))], b_w["in"]) for g in range(2)]
                front = tile_front(Tn, xsrc, s_)
            xT, b_xT = front
            tb = min(128, Tn)
            nb = Tn // tb
            b_vt = [b_vT[0], b_vT[1], b_vT[2]]
            if pre:
                qw = [wload([(lambda t: wview_k(t, 512), wsrc(wb_in, g * 512, 512))], b_w["in"]) for g in range(2)]
            if prev_finish is not None:
                prev_finish[0]()
            for g in range(2):
                wt, wbuf = qw[g]
                wv = wview_k(wt, 512)
                for j in range(4):
                    m = g * 4 + j
                    bi = nextbank()
                    fm_group(wv, j, hT, Tn, 8, bi, [b_hT], wbuf)
                    P.op(ACT, lambda e, m=m, bi=bi: e.activation(qz[0][0:64, m, 0:Tn], ps[0:64, bi * 512:bi * 512 + Tn], AF.Copy, scale=0.125),
                         reads=[bank[bi]], writes=[b_qz])
                    P.op(DVE, lambda e, m=m, bi=bi: e.tensor_scalar(qz[1][64:128, m, 0:Tn], ps[64:128, bi * 512:bi * 512 + Tn], 0.125, None, ALU.mult),
                         reads=[bank[bi]], writes=[b_qz])
            for g in range(2, 6):
                wt, wbuf = wload([(lambda t: wview_k(t, 512), wsrc(wb_in, g * 512, 512))], b_w["in"])
                wv = wview_k(wt, 512)
                isk = g < 4
                half = g % 2
                if isk:
                    for j in range(4):
                        m = half * 4 + j
                        bi = nextbank()
                        fm_group(wv, j, hT, Tn, 8, bi, [b_hT], wbuf)
                        P.op(ACT if j % 2 == 0 else DVE,
                             (lambda e, m=m, bi=bi: e.activation(kTs[:, m, 0:Tn], bk(bi, Tn), AF.Copy)) if j % 2 == 0 else
                             (lambda e, m=m, bi=bi: e.tensor_copy(kTs[:, m, 0:Tn], bk(bi, Tn))),
                             reads=[bank[bi]], writes=[b_kTs])
                for b in range(nb):
                    bi = nextbank()
                    for kc in range(8):
                        P.op(PE, lambda e, kc=kc, b=b, bi=bi, wv=wv: e.matmul(bk(bi, 512, tb), lhsT=hT[:, kc, b * tb:(b + 1) * tb], rhs=wv[:, kc, :],
                                                                             start=(kc == 0), stop=(kc == 7)),
                             reads=[wbuf, b_hT], writes=[bank[bi]])
                    si = rr("stg", 4)
                    P.op(DVE if b % 2 == 0 else ACT,
                         (lambda e, si=si, bi=bi: e.tensor_copy(stg[si][0:tb, :], bk(bi, 512, tb))) if b % 2 == 0 else
                         (lambda e, si=si, bi=bi: e.activation(stg[si][0:tb, :], bk(bi, 512, tb), AF.Copy)),
                         reads=[bank[bi]], writes=[b_stg[si]])
                    dst = koutsrc if isk else voutsrc
                    P.dma(POOL, lambda e, si=si, b=b, dst=dst, half=half: e.dma_start(out=dst[b * tb:(b + 1) * tb, half * 512:(half + 1) * 512],
                                                                                     in_=stg[si][0:tb, :]), "stg%d" % si, reads=[b_stg[si]])
                    if not isk:
                        P.dma(POOL, lambda e, si=si, b=b, half=half: e.dma_start(
                            out=vscr[s_, half * 4:half * 4 + 4, kpos0 + b * tb:kpos0 + (b + 1) * tb, :].rearrange("c t d -> t c d"),
                            in_=stg[si][0:tb, :].rearrange("t (c d) -> t c d", d=128)), "stgv%d" % si,
                            reads=[b_stg[si]], writes=[b_vscr[s_]])
                if isk and half == 1:
                    P.dma(POOL, lambda e: e.dma_start(out=kscr[s_, :, :, kpos0:kpos0 + Tn].rearrange("c p t -> p c t"), in_=kTs[:, :, 0:Tn]),
                          "kTs", reads=[b_kTs], writes=[b_kscr[s_]])
            if prev_finish is not None:
                prev_finish[1]()
            FB = 7

            def conv_gen():
                for m in range(8):
                    wt, wbuf = wload([(lambda t: wview_k(t, 384), wsrc(wb_cv, m * 384, 384))], b_w["cv"])
                    wv = wt[:, 0:3072].rearrange("p (k n) -> p k n", n=384)
                    ia = rr("tA", 2)
                    ci = rr("cu", 2)
                    io = rr("tB", 2)

                    def grp(j):
                        for kc in range(8):
                            P.op(PE, lambda e, kc=kc, j=j, wv=wv: e.matmul(bk(FB, Tn), lhsT=wv[:, kc, j * 128:(j + 1) * 128], rhs=hT[:, kc, 0:Tn],
                                                                    start=(kc == 0), stop=(kc == 7)),
                                 reads=[wbuf, b_hT], writes=[bank[FB]])
                            if kc in (1, 3, 5):
                                yield
                    yield from grp(1)
                    P.op(DVE, lambda e, ia=ia: e.tensor_copy(tmpA[ia][:, 0:Tn], bk(FB, Tn)), reads=[bank[FB]], writes=[b_tmpA[ia]])
                    yield
                    yield from grp(2)
                    P.op(POOL, lambda e, m=m, ci=ci: e.tensor_copy(cu[ci][:, 0:2], cuh[:, m, :]), reads=[b_cuh], writes=[b_cu[ci]])
                    P.op(DVE, lambda e, ia=ia, ci=ci: e.tensor_tensor(cu[ci][:, 2:Tn + 2], tmpA[ia][:, 0:Tn], bk(FB, Tn), ALU.mult),
                         reads=[b_tmpA[ia], bank[FB]], writes=[b_cu[ci]])
                    P.op(POOL, lambda e, m=m, ci=ci: e.tensor_copy(cuh[:, m, :], cu[ci][:, Tn:Tn + 2]), reads=[b_cu[ci]], writes=[b_cuh])
                    conv3(([b_cu[ci]] + b_vt,), cu[ci], Tn, "wc", "bc", m, io)
                    yield
                    yield from grp(0)
                    P.op(DVE, lambda e, m=m, io=io: e.tensor_tensor(obT[:, m, 0:Tn], tmpB[io][:, 0:Tn], bk(FB, Tn), ALU.mult),
                         reads=[b_tmpB[io], bank[FB]], writes=[b_obT])
                    yield

            filler = conv_gen()
            attention(s_, Tn, units_blocks, filler)
            for _ in filler:
                pass
            if conv_out is not None:
                for r in range(2):
                    P.dma(POOL, lambda e, r=r: e.dma_start(out=conv_out[r, :].rearrange("(m p) -> p m", p=128), in_=cuh[:, :, r],
                                                           allow_slow_non_contiguous=True), "cvout", reads=[b_cuh], group=True)

            for m in range(8):
                wt, wbuf = wload([(lambda t: wview_k(t, 512), wsrc(wb_mg, m * 512, 512))], b_w["mg"])
                wfull = wview_k(wt, 512)
                wg = wfull[:, :, 0:256]
                wa = wfull[:, :, 256:384]
                wbb_ = wfull[:, :, 384:512]
                bga, bgb, bya, byb = nextbank(), nextbank(), nextbank(), nextbank()
                fm_group(wg, 0, hT, Tn, 8, bga, [b_hT], wbuf)
                fm_group(wg, 1, hT, Tn, 8, bgb, [b_hT], wbuf)
                fm_group(wa, 0, oT, Tn, 8, bya, [b_oT], wbuf)
                fm_group(wbb_, 0, obT, Tn, 8, byb, [b_obT], wbuf)
                i0 = rr("tA", 2); i1 = rr("tA", 2); j0 = rr("tB", 2); j1 = rr("tB", 2)
                P.op(ACT, lambda e, i0=i0, bga=bga: e.activation(tmpA[i0][:, 0:Tn], bk(bga, Tn), AF.Sigmoid), reads=[bank[bga]], writes=[b_tmpA[i0]])
                P.op(ACT, lambda e, i1=i1, bgb=bgb: e.activation(tmpA[i1][:, 0:Tn], bk(bgb, Tn), AF.Sigmoid), reads=[bank[bgb]], writes=[b_tmpA[i1]])
                P.op(DVE, lambda e, i0=i0, j0=j0, bya=bya: e.tensor_tensor(tmpB[j0][:, 0:Tn], tmpA[i0][:, 0:Tn], bk(bya, Tn), ALU.mult),
                     reads=[b_tmpA[i0], bank[bya]], writes=[b_tmpB[j0]])
                P.op(DVE, lambda e, i1=i1, j1=j1, byb=byb: e.tensor_tensor(tmpB[j1][:, 0:Tn], tmpA[i1][:, 0:Tn], bk(byb, Tn), ALU.mult),
                     reads=[b_tmpA[i1], bank[byb]], writes=[b_tmpB[j1]])
                P.op(POOL, lambda e, m=m, j0=j0, j1=j1: e.tensor_tensor(mT[:, m, 0:Tn], tmpB[j0][:, 0:Tn], tmpB[j1][:, 0:Tn], ALU.add),
                     reads=[b_tmpB[j0], b_tmpB[j1]], writes=[b_mT])
            for g in range(2):
                wt, wbuf = wload([(lambda t: wview_k(t, 512), wsrc(wb_out, g * 512, 512))], b_w["out"])
                wv = wview_k(wt, 512)
                for j in range(4):
                    m = g * 4 + j
                    bi = nextbank()
                    fm_group(wv, j, mT, Tn, 8, bi, [b_mT], wbuf)
                    P.op(DVE, lambda e, m=m, bi=bi: e.scalar_tensor_tensor(xT[:, m, 0:Tn], bk(bi, Tn), mcol(16 + m, s_), xT[:, m, 0:Tn], ALU.mult, ALU.add),
                         reads=[bank[bi], b_modT, b_xT], writes=[b_xT])
            rms_to_hT(Tn, s_, 1, xT, b_xT)
            for f in range(NFF):
                if f % 2 == 0:
                    upw = wload([(lambda t: wview_k(t, 512), wsrc(wb_up, f * 256, 512))], b_w["up"])
                wt, wbuf = upw
                wv = wview_k(wt, 512)[:, :, (f % 2) * 256:(f % 2) * 256 + 256]
                bu, bg = nextbank(), nextbank()
                fm_group(wv, 0, hT, Tn, 8, bu, [b_hT], wbuf)
                fm_group(wv, 1, hT, Tn, 8, bg, [b_hT], wbuf)
                ci = rr("cu", 2); io = rr("tB", 2); ia = rr("tA", 2)
                P.op(POOL, lambda e, f=f, ci=ci: e.tensor_copy(cu[ci][:, 0:2], uph[:, f, :]), reads=[b_uph], writes=[b_cu[ci]])
                P.op(ACT, lambda e, ci=ci, bu=bu: e.activation(cu[ci][:, 2:Tn + 2], bk(bu, Tn), AF.Copy), reads=[bank[bu]], writes=[b_cu[ci]])
                P.op(POOL, lambda e, f=f, ci=ci: e.tensor_copy(uph[:, f, :], cu[ci][:, Tn:Tn + 2]), reads=[b_cu[ci]], writes=[b_uph])
                conv3(([b_cu[ci]] + b_vt,), cu[ci], Tn, "wf", "bf", f, io)
                P.op(ACT, lambda e, io=io, ia=ia: e.activation(tmpA[ia][:, 0:Tn], tmpB[io][:, 0:Tn], AF.Silu), reads=[b_tmpB[io]], writes=[b_tmpA[ia]])
                P.op(DVE, lambda e, f=f, ia=ia, bg=bg: e.tensor_tensor(actT[:, f, 0:Tn], tmpA[ia][:, 0:Tn], bk(bg, Tn), ALU.mult),
                     reads=[b_tmpA[ia], bank[bg]], writes=b_big)
            if ffn_out is not None:
                for r in range(2):
                    P.dma(POOL, lambda e, r=r: e.dma_start(out=ffn_out[r, :].rearrange("(f p) -> p f", p=128), in_=uph[:, :, r],
                                                         allow_slow_non_contiguous=True), "ffout", reads=[b_uph], group=True)
            dnw = [wload([(lambda t: t[:, 0:NFF * 128].rearrange("p (k n) -> p k n", n=128),
                           wb_dn[m, :, :].rearrange("p (k n) -> p k n", n=128))], b_w["down"]) for m in range(2)]
            nfront = next_front_fn() if next_front_fn is not None else None
            for m in range(8):
                if m < 2:
                    wt, wbuf = dnw[m]
                else:
                    wt, wbuf = wload([(lambda t: t[:, 0:NFF * 128].rearrange("p (k n) -> p k n", n=128),
                                       wb_dn[m, :, :].rearrange("p (k n) -> p k n", n=128))], b_w["down"])
                wv = wt[:, 0:NFF * 128].rearrange("p (k n) -> p k n", n=128)
                bi = nextbank()
                fm_group(wv, 0, actT, Tn, NFF, bi, b_big, wbuf)
                P.op(DVE, lambda e, m=m, bi=bi: e.scalar_tensor_tensor(xT[:, m, 0:Tn], bk(bi, Tn), mcol(40 + m, s_), xT[:, m, 0:Tn], ALU.mult, ALU.add),
                     reads=[bank[bi], b_modT, b_xT], writes=[b_xT])
            def finish_a():
                rms_stats(Tn, xT, b_xT)
                for m in range(8):
                    P.op(DVE, lambda e, m=m: e.scalar_tensor_tensor(xT[:, m, 0:Tn], xT[:, m, 0:Tn], vcol("gf", m), rstd[:, 0:Tn], ALU.mult, ALU.mult),
                         reads=[b_xT, b_rstd, b_vT[0]], writes=[b_xT])

            def emit_J():
                for b in range(nb):
                    for hh in range(2):
                        bi = nextbank()
                        for j in range(4):
                            m = hh * 4 + j
                            P.op(PE, lambda e, j=j, m=m, b=b, bi=bi: e.transpose(ps[0:tb, bi * 512 + j * 128:bi * 512 + (j + 1) * 128],
                                                                                xT[:, m, b * tb:(b + 1) * tb], ident[:, :]),
                                 reads=[b_xT] + b_const_all, writes=[bank[bi]])
                        si = rr("stg", 4)
                        P.op(ACT if hh == 0 else DVE,
                             (lambda e, si=si, bi=bi: e.activation(stg[si][0:tb, :], bk(bi, 512, tb), AF.Copy)) if hh == 0 else
                             (lambda e, si=si, bi=bi: e.tensor_copy(stg[si][0:tb, :], bk(bi, 512, tb))),
                             reads=[bank[bi]], writes=[b_stg[si]])
                        P.dma(POOL, lambda e, si=si, b=b, hh=hh: e.dma_start(out=ysrc[b * tb:(b + 1) * tb, hh * 512:(hh + 1) * 512], in_=stg[si][0:tb, :]),
                              "stg%d" % si, reads=[b_stg[si]])

            return (finish_a, emit_J), nfront

        def attention(s_, Tn, blocks, filler=None):
            Tk = max(k0 + nk for (_, k0, nk, _, _) in blocks)
            units = []
            for c in range(8):
                for bi_, blk in enumerate(blocks):
                    units.append((c, bi_, blk))
            nblk = len(blocks)
            nu = len(units)
            ZB = [(0, 1), (2, 3)]
            AB = (4, 5)
            OB = 6
            slot_of = {}
            accslot = {}

            def load_c(c):
                sl = rr("att", 2)
                slot_of[c] = sl
                P.dma(SP, lambda e: e.dma_start(out=kTh[sl][:, 0:Tk], in_=kscr[s_, c, :, 0:Tk]), "kTh%d" % sl,
                      reads=[b_kscr[s_]], writes=[b_big[sl]])
                nfull = Tk // 128
                if nfull:
                    P.dma(SP, lambda e: e.dma_start(out=vh[sl][:, 0:nfull, :], in_=vscr[s_, c, 0:nfull * 128, :].rearrange("(k p) d -> p k d", p=128)),
                          "vh%d" % sl, reads=[b_vscr[s_]], writes=[b_big[2 + sl]])
                rem = Tk - nfull * 128
                if rem:
                    P.dma(SP, lambda e: e.dma_start(out=vh[sl][0:rem, nfull, :], in_=vscr[s_, c, nfull * 128:Tk, :]),
                          "vh%d" % sl, reads=[b_vscr[s_]], writes=[b_big[2 + sl]], group=bool(nfull))

            def zviews(bpair, nk, q0):
                return [ps[0:nk, bpair[h] * 512 + q0:bpair[h] * 512 + Tn] for h in range(2)]

            def st_z(u):
                c, bi_, (kbidx, k0, nk, q0, diag) = units[u]
                if bi_ == 0 and c == 0:
                    load_c(0)
                    load_c(1)
                sl = slot_of[c]
                zb = ZB[u % 2]
                for h in range(2):
                    P.op(PE, lambda e, h=h: e.matmul(ps[0:nk, zb[h] * 512 + q0:zb[h] * 512 + Tn], lhsT=kTh[sl][:, k0:k0 + nk],
                                                     rhs=qz[h][:, c, q0:Tn], start=True, stop=True),
                         reads=[b_big[sl], b_qz], writes=[bank[zb[h]]])

            def st_EL(u):
                c, bi_, (kbidx, k0, nk, q0, diag) = units[u]
                zb = ZB[u % 2]
                i = u % 2
                zin = ps[0:nk, zb[0] * 512:zb[0] * 512 + 1024].rearrange("p (h t) -> p h t", t=512)[:, :, q0:Tn]
                P.op(ACT, lambda e: e.activation(Et[i][0:nk, :, q0:Tn], zin, AF.Exp), reads=[bank[zb[0]], bank[zb[1]]], writes=[b_E[i]])
                P.op(ACT, lambda e: e.activation(Lt[i][0:nk, :, q0:Tn], Et[i][0:nk, :, q0:Tn], AF.Ln, bias=one_t[0:nk, 0:1]),
                     reads=[b_E[i], b_eps], writes=[b_L[i]])
                if diag:
                    for h in range(2):
                        P.op(POOL, lambda e, h=h: e.tensor_tensor(Lt[i][0:nk, h, q0:q0 + nk], Lt[i][0:nk, h, q0:q0 + nk], m01[0:nk, 0:nk], ALU.mult),
                             reads=[b_L[i]] + b_const_all, writes=[b_L[i]])
                if bi_ == 0:
                    accslot[(c, 0)] = None
                if bi_ < nblk - 1:
                    a_new = rr_acc()
                    if bi_ == 0:
                        P.op(POOL, lambda e: e.memset(Acc[a_new][:], 0.0), writes=[b_Acc[a_new]])
                        P.op(POOL, lambda e: e.tensor_copy(Acc[a_new][0:nk, :, q0:Tn], Lt[i][0:nk, :, q0:Tn]), reads=[b_L[i]], writes=[b_Acc[a_new]])
                    else:
                        a_old = accslot[(c, bi_)]
                        if q0 > 0:
                            P.op(POOL, lambda e: e.memset(Acc[a_new][:, :, 0:q0], 0.0), writes=[b_Acc[a_new]])
                        P.op(POOL, lambda e: e.tensor_tensor(Acc[a_new][:, :, q0:Tn], Acc[a_old][:, :, q0:Tn], Lt[i][:, :, q0:Tn], ALU.add),
                             reads=[b_Acc[a_old], b_L[i]], writes=[b_Acc[a_new]])
                    accslot[(c, bi_ + 1)] = a_new

            accc = [0]

            def rr_acc():
                accc[0] += 1
                return accc[0] % 3

            def st_arg(u):
                c, bi_, (kbidx, k0, nk, q0, diag) = units[u]
                sl = slot_of[c]
                i = u % 2
                a_in = accslot[(c, bi_)]
                for h in range(2):
                    out = ps[0:nk, AB[h] * 512 + q0:AB[h] * 512 + Tn]
                    P.op(PE, lambda e, h=h, out=out: e.matmul(out, lhsT=kTh[sl][:, k0:k0 + nk], rhs=qz[h][:, c, q0:Tn], start=True, stop=False),
                         reads=[b_big[sl], b_qz], writes=[bank[AB[h]]])
                    P.op(PE, lambda e, h=h, out=out: e.matmul(out, lhsT=ntri[0:nk, 0:nk], rhs=Lt[i][0:nk, h, q0:Tn], start=False, stop=(a_in is None)),
                         reads=[b_L[i]] + b_const_all, writes=[bank[AB[h]]])
                    if a_in is not None:
                        P.op(PE, lambda e, h=h, out=out: e.matmul(out, lhsT=nones[:, 0:nk], rhs=Acc[a_in][:, h, q0:Tn], start=False, stop=True),
                             reads=[b_Acc[a_in]] + b_const_all, writes=[bank[AB[h]]])

            def st_a(u):
                c, bi_, (kbidx, k0, nk, q0, diag) = units[u]
                i = u % 2
                ain = ps[0:nk, AB[0] * 512:AB[0] * 512 + 1024].rearrange("p (h t) -> p h t", t=512)[:, :, q0:Tn]
                P.op(ACT, lambda e: e.activation(At[i][0:nk, :, q0:Tn], ain, AF.Exp), reads=[bank[AB[0]], bank[AB[1]]], writes=[b_A[i]])
                if diag:
                    for h in range(2):
                        P.op(POOL, lambda e, h=h: e.tensor_tensor(At[i][0:nk, h, q0:q0 + nk], At[i][0:nk, h, q0:q0 + nk], m01[0:nk, 0:nk], ALU.mult),
                             reads=[b_A[i]] + b_const_all, writes=[b_A[i]])

            def st_av(u):
                c, bi_, (kbidx, k0, nk, q0, diag) = units[u]
                sl = slot_of[c]
                i = u % 2
                if bi_ == 0:
                    P.op(PE, lambda e: e.matmul(ps[:, OB * 512:OB * 512 + Tn], lhsT=ones[:, :], rhs=zer[:, 0:Tn],
                                                start=True, stop=False, skip_group_check=True),
                         reads=b_const_all, writes=[bank[OB]])
                for h in range(2):
                    P.op(PE, lambda e, h=h: e.matmul(ps[64 * h:64 * h + 64, OB * 512 + q0:OB * 512 + Tn], lhsT=vh[sl][0:nk, kbidx, 64 * h:64 * h + 64],
                                                     rhs=At[i][0:nk, h, q0:Tn], start=False, stop=(bi_ == nblk - 1 and h == 1), skip_group_check=True),
                         reads=[b_big[2 + sl], b_A[i]], writes=[bank[OB]])
                if bi_ == nblk - 1 and c + 2 < 8:
                    load_c(c + 2)
                if bi_ == nblk - 1:
                    P.op(DVE, lambda e: e.tensor_copy(oT[:, c, 0:Tn], ps[:, OB * 512:OB * 512 + Tn]), reads=[bank[OB]], writes=[b_oT])

            for s in range(-2, nu + 1):
                if 0 <= s < nu:
                    st_arg(s)
                if filler is not None and s >= 0:
                    next(filler, None)
                if 0 <= s + 2 < nu:
                    st_z(s + 2)
                if 0 <= s - 1 < nu:
                    st_av(s - 1)
                if 0 <= s + 1 < nu:
                    st_EL(s + 1)
                if 0 <= s < nu:
                    st_a(s)

        one_t = sb("one_t", [128, 1])
        P.op(POOL, lambda e: e.memset(one_t[:], 1.0), writes=[b_eps])

        for kb in range(PAST // 128):
            xi = rr("xin", 2)
            P.dma(SP, lambda e, kb=kb, xi=xi: e.dma_start(out=xin[xi][:, :], in_=ck[kb * 128:(kb + 1) * 128, :]), "xin%d" % xi, writes=[b_xin[xi]])
            for hh in range(2):
                bi = nextbank()
                for j in range(4):
                    m = hh * 4 + j
                    P.op(PE, lambda e, j=j, m=m, xi=xi, bi=bi: e.transpose(ps[:, bi * 512 + j * 128:bi * 512 + (j + 1) * 128],
                                                                           xin[xi][:, m * 128:(m + 1) * 128], ident[:, :]),
                         reads=[b_xin[xi]] + b_const_all, writes=[bank[bi]])
                P.op(DVE, lambda e, hh=hh, kb=kb, bi=bi: e.tensor_copy(kTs[:, hh * 4:hh * 4 + 4, (kb % 4) * 128:(kb % 4 + 1) * 128],
                                                                      bk(bi).rearrange("p (j t) -> p j t", t=128)),
                     reads=[bank[bi]], writes=[b_kTs])
            if kb % 4 == 3:
                k0 = (kb - 3) * 128
                P.dma(SP, lambda e, k0=k0: e.dma_start(out=kscr[NP, :, :, k0:k0 + 512].rearrange("c p t -> p c t"), in_=kTs[:, :, :]),
                      "kTs", reads=[b_kTs], writes=[b_kscr[NP]])
        for kb in range(PAST // 128):
            P.dma(POOL, lambda e, kb=kb: e.dma_start(out=vscr[NP, :, kb * 128:(kb + 1) * 128, :].rearrange("c t d -> t c d"),
                                                     in_=cv[kb * 128:(kb + 1) * 128, :].rearrange("t (c d) -> t c d", d=128)),
                  "cvcast", writes=[b_vscr[NP]], group=True)

        pend = [None]
        tiles = [(s_, i) for s_ in range(NP) for i in range(NT)]
        nfr = None
        for ti, (s_, i) in enumerate(tiles):
            if i == 0:
                P.op(POOL, lambda e: e.memset(cuh[:], 0.0), writes=[b_cuh])
                P.op(POOL, lambda e: e.memset(uph[:], 0.0), writes=[b_uph])
            r0 = s_ * SEQ + i * T
            blocks = []
            for kb in range(4 * i + 3, -1, -1):
                j = kb - 4 * i
                blocks.append((kb, kb * 128, 128, 128 * j if j >= 0 else 0, j >= 0))
            last = (i == NT - 1)
            nff = None
            if ti + 1 < len(tiles):
                s2, i2 = tiles[ti + 1]
                r2 = s2 * SEQ + i2 * T
                nff = (lambda r2=r2, s2=s2: tile_front(T, xp[r2:r2 + T, :], s2))
            pend[0], nfr = process_tile(s_, T, xp[r0:r0 + T, :], yp[r0:r0 + T, :], nkp[r0:r0 + T, :], nvp[r0:r0 + T, :], i * T, blocks, last,
                                        ncp[2 * s_:2 * s_ + 2, :] if last else None, nfp[2 * s_:2 * s_ + 2, :] if last else None, pend[0],
                                        front=nfr, next_front_fn=nff)

        s_ = NP
        for r in range(2):
            P.op(POOL, lambda e, r=r: e.tensor_copy(cuh[:, :, r], vcols(("sc", r), 8)), reads=[b_vT[1]], writes=[b_cuh])
            P.op(POOL, lambda e, r=r: e.tensor_copy(uph[:, :, r], vcols(("sf", r), NFF)), reads=[b_vT[2]], writes=[b_uph])
        blocks = [(PAST // 128, PAST, SAMP, 0, True)] + [(kb, kb * 128, 128, 0, False) for kb in range(PAST // 128 - 1, -1, -1)]
        fin, _ = process_tile(s_, SAMP, xs, ys, nks, nvs, PAST, blocks, True, ncs, nfs, pend[0])
        fin[0]()
        fin[1]()

        P.emit(nc)
    return nc


_NC_CACHE = {}


def _prep_maps(inp, NP, SEQ, ncores):
    f = lambda a: np.ascontiguousarray(a, dtype=np.float32)
    shared = {
        "w_ada": f(inp["w_ada"][0]), "b_ada": f(inp["b_ada"][0]), "g1": f(inp["g_norm1"][0]), "w_in": f(inp["w_in"][0]),
        "w_conv": f(inp["w_conv"][0]), "b_conv": f(inp["b_conv"][0]), "w_ba": f(inp["w_branch_a"][0]),
        "w_bb": f(inp["w_branch_b"][0]), "w_out": f(inp["w_out"][0]), "g2": f(inp["g_norm2"][0]), "w_up": f(inp["w_up"][0]),
        "w_fconv": f(inp["w_fconv"][0]), "b_fconv": f(inp["b_fconv"][0]), "w_down": f(inp["w_down"][0]), "gfin": f(inp["g_final"]),
    }
    maps = []
    for c in range(ncores):
        m = dict(shared)
        m["xp"] = f(inp["x_prompt"][c * NP:(c + 1) * NP].reshape(NP * SEQ, D))
        m["xs"] = f(inp["x_sample"][c])
        m["ck"] = f(inp["cache_k"][0, c].reshape(-1, D))
        m["cv"] = f(inp["cache_v"][0, c].reshape(-1, D))
        m["sc"] = f(inp["state_conv"][0, c])
        m["sf"] = f(inp["state_ffn_conv"][0, c])
        m["cvec"] = f(np.concatenate([inp["c_prompt"][c * NP:(c + 1) * NP], inp["c_sample"][c:c + 1]], axis=0))
        maps.append(m)
    return maps


def run(inp, NP, SEQ, ncores, PAST=1024, SAMP=32):
    key = (SEQ, NP, PAST, SAMP)
    if key not in _NC_CACHE:
        _NC_CACHE[key] = build(SEQ, NP, PAST, SAMP)
    nc = _NC_CACHE[key]
    maps = _prep_maps(inp, NP, SEQ, ncores)
    res = run_bass_kernel_spmd(nc, maps, core_ids=list(range(ncores)))
    rs = res.results
    cat = lambda k: np.concatenate([r[k] for r in rs], axis=0)
    B = NP * ncores
    y_p = cat("yp").reshape(B, SEQ, D)
    y_s = cat("ys").reshape(ncores, SAMP, D)
    nk_p = cat("nkp").reshape(1, B, SEQ, NH, 64)
    nv_p = cat("nvp").reshape(1, B, SEQ, NH, 64)
    nc_p = cat("ncp").reshape(1, B, 2, D)
    nf_p = cat("nfp").reshape(1, B, 2, DFF)
    nk_s = cat("nks").reshape(1, ncores, SAMP, NH, 64)
    nv_s = cat("nvs").reshape(1, ncores, SAMP, NH, 64)
    nc_s = cat("ncs").reshape(1, ncores, 2, D)
    nf_s = cat("nfs").reshape(1, ncores, 2, DFF)
    return tuple(np.ascontiguousarray(a, dtype=np.float32) for a in (y_p, y_s, nk_p, nv_p, nc_p, nf_p, nk_s, nv_s, nc_s, nf_s))


def kernel(**inputs):
    return run(inputs, NP=2, SEQ=4096, ncores=8)
```

```python
import numpy as np
from contextlib import ExitStack
import concourse.bass as bass
import concourse.mybir as mybir
from concourse.bass_utils import run_bass_kernel_spmd

F32 = mybir.dt.float32
BF16 = mybir.dt.bfloat16
AF = mybir.ActivationFunctionType
ALU = mybir.AluOpType

PE, ACT, DVE, POOL, SP = "pe", "act", "dve", "pool", "sp"
ENGS = [PE, ACT, DVE, POOL, SP]

D = 1024
DFF = 2816
NFF = 22
NH = 16
EPS = 1e-6


class Buf:
    def __init__(self, name):
        self.name = name
        self.last_w = None
        self.readers = []


class Op:
    __slots__ = ("eng", "fn", "deps", "signal", "dma_key", "dma_cnt", "is_dma")

    def __init__(self, eng, fn):
        self.eng = eng
        self.fn = fn
        self.deps = []
        self.signal = 0
        self.dma_key = None
        self.dma_cnt = 0
        self.is_dma = False


class Prog:
    def __init__(self):
        self.ops = {e: [] for e in ENGS}
        self.dma_keys = {}

    def _add_deps(self, op, reads, writes):
        deps = []
        for b in reads:
            if b.last_w is not None:
                deps.append(("raw", b.last_w))
        for b in writes:
            if b.last_w is not None:
                deps.append(("waw", b.last_w))
            for r in b.readers:
                deps.append(("war", r))
        for kind, d in deps:
            if d is op:
                continue
            if d.eng == op.eng and not d.is_dma and not op.is_dma:
                if op.eng == PE:
                    continue
            op.deps.append(d)
        for b in reads:
            b.readers.append(op)
        for b in writes:
            b.last_w = op
            b.readers = []

    def op(self, eng, fn, reads=(), writes=()):
        o = Op(eng, fn)
        self._add_deps(o, reads, writes)
        self.ops[eng].append(o)
        return o

    def dma(self, eng, fn, key, reads=(), writes=(), group=False):
        o = Op(eng, fn)
        o.is_dma = True
        st = self.dma_keys.setdefault(key, [0, None])
        if st[1] is not None and not group:
            o.deps.append(st[1])
        self._add_deps(o, reads, writes)
        st[0] += 16
        st[1] = o
        o.dma_key = key
        o.dma_cnt = st[0]
        self.ops[eng].append(o)
        return o

    def emit(self, nc):
        needed = set()
        for e in ENGS:
            for o in self.ops[e]:
                for d in o.deps:
                    if not d.is_dma:
                        needed.add(id(d))
        for e in ENGS:
            c = 0
            for o in self.ops[e]:
                if not o.is_dma and id(o) in needed:
                    c += 1
                    o.signal = c
        with ExitStack() as st:
            esem = {e: st.enter_context(nc.semaphore("s_" + e)) for e in ENGS}
            dsem = {k: st.enter_context(nc.semaphore("d_%d" % i)) for i, k in enumerate(self.dma_keys)}
            block = st.enter_context(nc.Block())
            prog = self

            def run(eng_name, eng):
                known = {}
                for o in prog.ops[eng_name]:
                    w = {}
                    for d in o.deps:
                        if d.is_dma:
                            k, v = ("d", d.dma_key), d.dma_cnt
                        else:
                            k, v = ("e", d.eng), d.signal
                        if v > w.get(k, 0):
                            w[k] = v
                    for k, v in w.items():
                        if known.get(k, 0) >= v:
                            continue
                        known[k] = v
                        eng.wait_ge(dsem[k[1]] if k[0] == "d" else esem[k[1]], v)
                    ins = o.fn(eng)
                    if o.is_dma:
                        ins.then_inc(dsem[o.dma_key], 16)
                    elif o.signal:
                        ins.then_inc(esem[eng_name], 1)
                if eng_name == SP:
                    for k, stt in prog.dma_keys.items():
                        if known.get(("d", k), 0) < stt[0]:
                            eng.wait_ge(dsem[k], stt[0])

            @block.tensor
            def _(e):
                run(PE, e)

            @block.scalar
            def _(e):
                run(ACT, e)

            @block.vector
            def _(e):
                run(DVE, e)

            @block.gpsimd
            def _(e):
                run(POOL, e)

            @block.sync
            def _(e):
                run(SP, e)


def build(SEQ=4096, NP=2, PAST=1024, SAMP=32):
    nc = bass.Bass("TRN2", target_bir_lowering=False)
    T = 512
    NT = SEQ // T
    TKMAX = max(SEQ, PAST + SAMP)
    NS = NP + 1
    P = Prog()

    def din(name, shape):
        return nc.dram_tensor(name, list(shape), F32, kind="ExternalInput").ap()

    def dout(name, shape):
        return nc.dram_tensor(name, list(shape), F32, kind="ExternalOutput").ap()

    xp = din("xp", [NP * SEQ, D]); xs = din("xs", [SAMP, D])
    ck = din("ck", [PAST, D]); cv = din("cv", [PAST, D])
    sc = din("sc", [2, D]); sf = din("sf", [2, DFF]); cvec = din("cvec", [NS, D])
    w_ada = din("w_ada", [D, 6 * D]); b_ada = din("b_ada", [6 * D]); g1 = din("g1", [D])
    w_in = din("w_in", [D, 8 * D]); w_conv = din("w_conv", [3, D]); b_conv = din("b_conv", [D])
    w_ba = din("w_ba", [D, D]); w_bb = din("w_bb", [D, D]); w_out = din("w_out", [D, D])
    g2 = din("g2", [D]); w_up = din("w_up", [D, 2 * DFF]); w_fconv = din("w_fconv", [3, DFF])
    b_fconv = din("b_fconv", [DFF]); w_down = din("w_down", [DFF, D]); gfin = din("gfin", [D])
    yp = dout("yp", [NP * SEQ, D]); ys = dout("ys", [SAMP, D])
    nkp = dout("nkp", [NP * SEQ, D]); nvp = dout("nvp", [NP * SEQ, D])
    ncp = dout("ncp", [NP * 2, D]); nfp = dout("nfp", [NP * 2, DFF])
    nks = dout("nks", [SAMP, D]); nvs = dout("nvs", [SAMP, D])
    ncs = dout("ncs", [2, D]); nfs = dout("nfs", [2, DFF])

    def dscr(name, shape):
        return nc.dram_tensor(name, list(shape), BF16).ap()

    wb_ada = dscr("wb_ada", [D, 6 * D]); wb_in = dscr("wb_in", [D, 3 * D])
    wb_cv = dscr("wb_cv", [D, 8 * 3 * 128])
    wb_mg = dscr("wb_mg", [D, 8 * 4 * 128])
    wb_out = dscr("wb_out", [D, D])
    wb_up = dscr("wb_up", [D, 2 * DFF])
    wb_dn = dscr("wb_dn", [8, 128, NFF * 128])
    kscr = dscr("kscr", [NS, 8, 128, TKMAX]); vscr = dscr("vscr", [NS, 8, TKMAX, 128])

    with ExitStack() as st:
        def sb(name, shape, dt=F32):
            return st.enter_context(nc.sbuf_tensor(name, list(shape), dt))

        ps = st.enter_context(nc.psum_tensor("ps", [128, 4096], F32))
        bank = [Buf("bank%d" % i) for i in range(8)]

        def bk(i, n=512, p=128):
            return ps[0:p, i * 512:i * 512 + n]

        ident = sb("ident", [128, 128]); tmpc = sb("tmpc", [128, 128])
        ntri = sb("ntri", [128, 128], BF16); nones = sb("nones", [128, 128], BF16)
        ones = sb("ones", [128, 128], BF16); m01 = sb("m01", [128, 128], BF16)
        b_const = Buf("const")
        P.op(POOL, lambda e: e.memset(ident[:], 1.0), writes=[b_const])
        P.op(POOL, lambda e: e.affine_select(out=ident[:], in_=ident[:], pattern=[[-1, 128]], compare_op=ALU.is_equal,
                                            fill=0.0, base=0, channel_multiplier=1), reads=[b_const], writes=[b_const])
        P.op(POOL, lambda e: e.memset(tmpc[:], -1.0), writes=[b_const])
        P.op(POOL, lambda e: e.affine_select(out=tmpc[:], in_=tmpc[:], pattern=[[-1, 128]], compare_op=ALU.is_ge,
                                            fill=0.0, base=0, channel_multiplier=1), reads=[b_const], writes=[b_const])
        P.op(DVE, lambda e: e.tensor_copy(ntri[:], tmpc[:]), reads=[b_const], writes=[b_const])
        b_c2 = Buf("const2")
        tmpd = sb("tmpd", [128, 128])
        P.op(POOL, lambda e: e.memset(tmpd[:], 1.0), writes=[b_c2])
        P.op(POOL, lambda e: e.affine_select(out=tmpd[:], in_=tmpd[:], pattern=[[1, 128]], compare_op=ALU.is_gt,
                                            fill=0.0, base=0, channel_multiplier=-1), reads=[b_c2], writes=[b_c2])
        P.op(DVE, lambda e: e.tensor_copy(m01[:], tmpd[:]), reads=[b_c2], writes=[b_c2])
        zer = sb("zer", [128, 512], BF16)
        P.op(POOL, lambda e: e.memset(zer[:], 0.0), writes=[b_c2])
        P.op(POOL, lambda e: e.memset(nones[:], -1.0), writes=[b_c2])
        P.op(POOL, lambda e: e.memset(ones[:], 1.0), writes=[b_c2])
        b_const_all = [b_const, b_c2]

        b_w = {n: [] for n in ("ada", "in", "cv", "mg", "out", "up", "down")}

        def cast(name, dst, src):
            bw = Buf("w_%s_%d" % (name, len(b_w[name])))
            b_w[name].append(bw)
            P.dma(POOL, lambda e: e.dma_start(out=dst, in_=src), "wcast_" + name, writes=[bw], group=True)

        for r0 in range(0, D, 128):
            cast("ada", wb_ada[r0:r0 + 128, :], w_ada[r0:r0 + 128, :])
        for r0 in range(0, D, 128):
            cast("in", wb_in[r0:r0 + 128, :], w_in[r0:r0 + 128, 0:3 * D])
        for r0 in range(0, D, 128):
            for gi in range(3):
                cast("cv", wb_cv[r0:r0 + 128, :].rearrange("p (m g n) -> p m g n", g=3, n=128)[:, :, gi, :],
                     w_in[r0:r0 + 128, (3 + gi) * D:(4 + gi) * D].rearrange("p (m n) -> p m n", n=128))
        for r0 in range(0, D, 128):
            for gi, srcw in enumerate((w_in[:, 6 * D:7 * D], w_in[:, 7 * D:8 * D], w_ba, w_bb)):
                cast("mg", wb_mg[r0:r0 + 128, :].rearrange("p (m g n) -> p m g n", g=4, n=128)[:, :, gi, :],
                     srcw[r0:r0 + 128, :].rearrange("p (m n) -> p m n", n=128))
        for r0 in range(0, D, 128):
            cast("out", wb_out[r0:r0 + 128, :], w_out[r0:r0 + 128, :])
        for r0 in range(0, D, 128):
            for gi in range(2):
                cast("up", wb_up[r0:r0 + 128, :].rearrange("p (f g n) -> p f g n", g=2, n=128)[:, :, gi, :],
                     w_up[r0:r0 + 128, gi * DFF:(gi + 1) * DFF].rearrange("p (f n) -> p f n", n=128))
        for m in range(8):
            for f0 in range(0, NFF, 11):
                cast("down", wb_dn[m, :, f0 * 128:(f0 + 11) * 128].rearrange("p (f n) -> p f n", n=128),
                     w_down[f0 * 128:(f0 + 11) * 128, m * 128:(m + 1) * 128].rearrange("(f p) n -> p f n", p=128))

        vr = [sb("vr%d" % i, [128, 128]) for i in range(3)]
        vT = [sb("vT%d" % i, [128, 128]) for i in range(3)]
        b_vr = [Buf("vr%d" % i) for i in range(3)]
        b_vT = [Buf("vT%d" % i) for i in range(3)]
        R = {}
        rowp = [0, 0, 0]

        def vload(key, ti, src2d, nrows):
            r0 = rowp[ti]
            R[key] = (ti, r0)
            P.dma(SP, lambda e: e.dma_start(out=vr[ti][r0:r0 + nrows, :], in_=src2d), "vload%d" % ti, writes=[b_vr[ti]], group=True)
            rowp[ti] += nrows

        def v2(ap1d, n):
            return ap1d.rearrange("(a b) -> a b", b=128)

        for s_ in range(NS):
            vload(("c", s_), 0, v2(cvec[s_, :], 8), 8)
        vload("b_ada", 0, v2(b_ada, 48), 48)
        vload("g1", 0, v2(g1, 8), 8); vload("g2", 0, v2(g2, 8), 8); vload("gf", 0, v2(gfin, 8), 8)
        for i in range(3):
            vload(("wc", i), 0, v2(w_conv[i, :], 8), 8)
        vload("bc", 0, v2(b_conv, 8), 8)
        for i in range(3):
            vload(("wf", i), 1, v2(w_fconv[i, :], NFF), NFF)
        vload("bf", 1, v2(b_fconv, NFF), NFF)
        for i in range(2):
            vload(("sc", i), 1, v2(sc[i, :], 8), 8)
        for i in range(2):
            vload(("sf", i), 2, v2(sf[i, :], NFF), NFF)
        for ti in range(3):
            n = rowp[ti]
            P.op(PE, lambda e, ti=ti, n=n: e.transpose(bk(ti, n), vr[ti][0:n, :], ident[0:n, 0:n]),
                 reads=[b_vr[ti]] + b_const_all, writes=[bank[ti]])
            P.op(DVE, lambda e, ti=ti, n=n: e.tensor_copy(vT[ti][:, 0:n], bk(ti, n)), reads=[bank[ti]], writes=[b_vT[ti]])

        def vcol(key, j=0):
            ti, r0 = R[key]
            return vT[ti][:, r0 + j:r0 + j + 1]

        def vcols(key, n):
            ti, r0 = R[key]
            return vT[ti][:, r0:r0 + n]

        NW = 3
        wsl = [sb("wsl%d" % i, [128, 4096], BF16) for i in range(NW)]
        b_wsl = [Buf("wsl%d" % i) for i in range(NW)]
        wctr = [0]

        def wload(pieces, deps):
            i = wctr[0] % NW
            wctr[0] += 1
            if len(pieces) == 1:
                dv0, src0 = pieces[0]
                kh = src0.shape[1] // 2
                pieces = [((lambda t, dv0=dv0: dv0(t)[:, 0:kh, :]), src0[:, 0:kh, :]),
                          ((lambda t, dv0=dv0: dv0(t)[:, kh:, :]), src0[:, kh:, :])]
            for n_, (dv, src) in enumerate(pieces):
                P.dma(SP, lambda e, dv=dv, src=src, i=i: e.dma_start(out=dv(wsl[i]), in_=src), "w%d" % i,
                      reads=deps, writes=[b_wsl[i]], group=(n_ > 0))
            return wsl[i], b_wsl[i]

        def wview_k(t, ncol, kc=8):
            return t[:, 0:kc * ncol].rearrange("p (k n) -> p k n", n=ncol)

        def wsrc(wb, c0, ncol):
            return wb.rearrange("(k p) n -> p k n", p=128)[:, :, c0:c0 + ncol]

        pctr = [0]

        def nextbank():
            i = pctr[0] % 8
            pctr[0] += 1
            return i

        csil = sb("csil", [128, 8, NS], BF16); b_csil = Buf("csil")
        modT = sb("modT", [128, 48, NS]); b_modT = Buf("modT")
        modv = sb("modv", [128, NS, 2, 8]); b_modv = Buf("modv")
        for s_ in range(NS):
            P.op(ACT, lambda e, s_=s_: e.activation(csil[:, :, s_], vcols(("c", s_), 8), AF.Silu), reads=[b_vT[0]], writes=[b_csil])
        mb = nextbank()
        for g in range(12):
            wt, wbuf = wload([(lambda t: wview_k(t, 512), wsrc(wb_ada, g * 512, 512))], b_w["ada"])
            wv = wview_k(wt, 512)
            for j in range(4):
                jj = g * 4 + j
                for kc in range(8):
                    P.op(PE, lambda e, wv=wv, j=j, kc=kc, jj=jj: e.matmul(ps[:, mb * 512 + 4 * jj:mb * 512 + 4 * jj + NS],
                                                                         lhsT=wv[:, kc, j * 128:(j + 1) * 128], rhs=csil[:, kc, :],
                                                                         start=(kc == 0), stop=(kc == 7), skip_group_check=True),
                         reads=[wbuf, b_csil], writes=[bank[mb]])
        for s_ in range(NS):
            P.op(DVE, lambda e, s_=s_: e.tensor_tensor(modT[:, :, s_], ps[:, mb * 512:mb * 512 + 192].rearrange("p (j f) -> p j f", f=4)[:, :, s_],
                                                       vcols("b_ada", 48), ALU.add),
                 reads=[bank[mb], b_vT[0]], writes=[b_modT])
        for s_ in range(NS):
            for kind, sc0, gk in ((0, 8, "g1"), (1, 32, "g2")):
                P.op(DVE, lambda e, s_=s_, kind=kind, sc0=sc0, gk=gk: e.scalar_tensor_tensor(
                    modv[:, s_, kind, :], modT[:, sc0:sc0 + 8, s_], 1.0, vcols(gk, 8), ALU.add, ALU.mult),
                    reads=[b_modT, b_vT[0]], writes=[b_modv])

        def mcol(j, s_):
            return modT[:, j, s_:s_ + 1]

        xin = [sb("xin%d" % i, [128, 1024]) for i in range(2)]; b_xin = [Buf("xin%d" % i) for i in range(2)]
        stg = [sb("stg%d" % i, [128, 512]) for i in range(4)]; b_stg = [Buf("stg%d" % i) for i in range(4)]
        xTs = [sb("xT%d" % i, [128, 8, T]) for i in range(2)]; b_xTs = [Buf("xT%d" % i) for i in range(2)]
        hT = sb("hT", [128, 8, T], BF16); b_hT = Buf("hT")
        qz = [sb("qz%d" % h, [128, 8, T], BF16) for h in range(2)]; b_qz = Buf("qz")
        kTs = sb("kTs", [128, 8, T], BF16); b_kTs = Buf("kTs")
        obT = sb("obT", [128, 8, T], BF16); b_obT = Buf("obT")
        oT = sb("oT", [128, 8, T], BF16); b_oT = Buf("oT")
        mT = sb("mT", [128, 8, T], BF16); b_mT = Buf("mT")
        big = sb("big", [128, 16384], BF16)
        b_big = [Buf("big%d" % i) for i in range(4)]
        kTh = [big[:, 0:4096], big[:, 4096:8192]]
        vh = [big[:, 8192:12288].rearrange("p (k d) -> p k d", d=128), big[:, 12288:16384].rearrange("p (k d) -> p k d", d=128)]
        actT = big[:, 0:NFF * T].rearrange("p (f t) -> p f t", t=T)
        sqs = [sb("sq%d" % i, [128, T], BF16) for i in range(2)]; b_sqs = [Buf("sq%d" % i) for i in range(2)]
        rstd = sb("rstd", [128, T]); b_rstd = Buf("rstd")
        lnt = sb("lnt", [128, T]); b_lnt = Buf("lnt")
        tmpA = [sb("tmpA%d" % i, [128, T]) for i in range(2)]; b_tmpA = [Buf("tmpA%d" % i) for i in range(2)]
        tmpB = [sb("tmpB%d" % i, [128, T]) for i in range(2)]; b_tmpB = [Buf("tmpB%d" % i) for i in range(2)]
        cu = [sb("cu%d" % i, [128, T + 2]) for i in range(2)]; b_cu = [Buf("cu%d" % i) for i in range(2)]
        cuh = sb("cuh", [128, 8, 2]); b_cuh = Buf("cuh")
        uph = sb("uph", [128, NFF, 2]); b_uph = Buf("uph")
        Et = [sb("Et%d" % i, [128, 2, T]) for i in range(2)]; b_E = [Buf("E%d" % i) for i in range(2)]
        Lt = [sb("Lt%d" % i, [128, 2, T], BF16) for i in range(2)]; b_L = [Buf("L%d" % i) for i in range(2)]
        At = [sb("At%d" % i, [128, 2, T], BF16) for i in range(2)]; b_A = [Buf("A%d" % i) for i in range(2)]
        Acc = [sb("Acc%d" % i, [128, 2, T], BF16) for i in range(3)]; b_Acc = [Buf("Acc%d" % i) for i in range(3)]
        b_kscr = [Buf("kscr%d" % s_) for s_ in range(NS)]
        b_vscr = [Buf("vscr%d" % s_) for s_ in range(NS)]
        for h in range(2):
            P.op(POOL, lambda e, h=h: e.memset(qz[h][:], 0.0), writes=[b_qz])
        ctr = {"xin": 0, "stg": 0, "tA": 0, "tB": 0, "cu": 0, "att": 0}

        def rr(key, n):
            i = ctr[key] % n
            ctr[key] += 1
            return i

        def rms_to_hT(Tn, s_, kind, xT, b_xT):
            rms_stats(Tn, xT, b_xT)
            if kind is not None:
                rms_apply(Tn, s_, kind, xT, b_xT)

        def rms_stats(Tn, xT, b_xT):
            sb_i = nextbank()
            for m in range(8):
                qi = m % 2
                if m % 2 == 0:
                    P.op(ACT, lambda e, m=m, qi=qi: e.activation(sqs[qi][:, 0:Tn], xT[:, m, 0:Tn], AF.Square), reads=[b_xT], writes=[b_sqs[qi]])
                else:
                    P.op(DVE, lambda e, m=m, qi=qi: e.tensor_tensor(sqs[qi][:, 0:Tn], xT[:, m, 0:Tn], xT[:, m, 0:Tn], ALU.mult),
                         reads=[b_xT], writes=[b_sqs[qi]])
                P.op(PE, lambda e, m=m, qi=qi: e.matmul(bk(sb_i, Tn), lhsT=ones[:], rhs=sqs[qi][:, 0:Tn], start=(m == 0), stop=(m == 7)),
                     reads=[b_sqs[qi]] + b_const_all, writes=[bank[sb_i]])
            P.op(ACT, lambda e: e.activation(lnt[:, 0:Tn], bk(sb_i, Tn), AF.Ln, scale=1.0 / D, bias=eps_t[:, 0:1]),
                 reads=[bank[sb_i], b_eps], writes=[b_lnt])
            P.op(ACT, lambda e: e.activation(rstd[:, 0:Tn], lnt[:, 0:Tn], AF.Exp, scale=-0.5), reads=[b_lnt], writes=[b_rstd])

        def rms_apply(Tn, s_, kind, xT, b_xT):
            shift0 = 0 if kind == 0 else 24
            for m in range(8):
                i = rr("tA", 2)
                P.op(DVE, lambda e, m=m, i=i: e.scalar_tensor_tensor(tmpA[i][:, 0:Tn], xT[:, m, 0:Tn], modv[:, s_, kind, m:m + 1], rstd[:, 0:Tn],
                                                                     ALU.mult, ALU.mult),
                     reads=[b_xT, b_rstd, b_modv], writes=[b_tmpA[i]])
                if m % 2 == 0:
                    P.op(POOL, lambda e, m=m, i=i: e.tensor_scalar(hT[:, m, 0:Tn], tmpA[i][:, 0:Tn], 1.0, mcol(shift0 + m, s_), ALU.mult, ALU.add),
                         reads=[b_tmpA[i], b_modT], writes=[b_hT])
                else:
                    P.op(ACT, lambda e, m=m, i=i: e.activation(hT[:, m, 0:Tn], tmpA[i][:, 0:Tn], AF.Identity, bias=mcol(shift0 + m, s_), scale=1.0),
                         reads=[b_tmpA[i], b_modT], writes=[b_hT])

        def fm_group(wv, j, rhs3, Tn, kcn, bi, rbufs, wbuf):
            for kc in range(kcn):
                P.op(PE, lambda e, kc=kc: e.matmul(bk(bi, Tn), lhsT=wv[:, kc, j * 128:(j + 1) * 128], rhs=rhs3[:, kc, 0:Tn],
                                                   start=(kc == 0), stop=(kc == kcn - 1)),
                     reads=[wbuf] + rbufs, writes=[bank[bi]])

        def conv3(dst_fn, src, Tn, wkey, bkey, j, i_out):
            P.op(DVE, lambda e: e.tensor_scalar(tmpB[i_out][:, 0:Tn], src[:, 2:Tn + 2], vcol((wkey, 2), j), vcol(bkey, j), ALU.mult, ALU.add),
                 reads=dst_fn[0], writes=[b_tmpB[i_out]])
            for tap in (1, 0):
                P.op(DVE, lambda e, tap=tap: e.scalar_tensor_tensor(tmpB[i_out][:, 0:Tn], src[:, tap:Tn + tap], vcol((wkey, tap), j),
                                                                     tmpB[i_out][:, 0:Tn], ALU.mult, ALU.add),
                     reads=dst_fn[0] + [b_tmpB[i_out]], writes=[b_tmpB[i_out]])

        eps_t = sb("eps_t", [128, 1]); b_eps = Buf("eps")
        P.op(POOL, lambda e: e.memset(eps_t[:], EPS), writes=[b_eps])

        tilectr = [0]

        def tile_front(Tn, xsrc, s_):
            par = tilectr[0] % 2
            tilectr[0] += 1
            xT, b_xT = xTs[par], b_xTs[par]
            tb = min(128, Tn)
            nb = Tn // tb
            for b in range(nb):
                xi = rr("xin", 2)
                P.dma(SP, lambda e, b=b, xi=xi: e.dma_start(out=xin[xi][0:tb, :], in_=xsrc[b * tb:(b + 1) * tb, :]), "xin%d" % xi,
                      writes=[b_xin[xi]])
                for hh in range(2):
                    bi = nextbank()
                    for j in range(4):
                        m = hh * 4 + j
                        P.op(PE, lambda e, j=j, m=m, xi=xi, bi=bi: e.transpose(ps[:, bi * 512 + j * 128:bi * 512 + j * 128 + tb],
                                                                               xin[xi][0:tb, m * 128:(m + 1) * 128], ident[0:tb, 0:tb]),
                             reads=[b_xin[xi]] + b_const_all, writes=[bank[bi]])
                    P.op(DVE, lambda e, hh=hh, b=b, bi=bi: e.tensor_copy(xT[:, hh * 4:hh * 4 + 4, b * tb:(b + 1) * tb],
                                                                        bk(bi).rearrange("p (j t) -> p j t", t=128)[:, :, 0:tb]),
                         reads=[bank[bi]], writes=[b_xT])
            rms_stats(Tn, xT, b_xT)
            rms_apply(Tn, s_, 0, xT, b_xT)
            return xT, b_xT

        def process_tile(s_, Tn, xsrc, ysrc, koutsrc, voutsrc, kpos0, units_blocks, last, conv_out, ffn_out, prev_finish=None,
                         front=None, next_front_fn=None):
            pre = front is not None
            if front is None:
                qw = [wload([(lambda t: wview_k(t, 512), wsrc(wb_in, g * 512, 512))], b_w["in"]) for g in range(2)]
                front = tile_front(Tn, xsrc, s_)
            xT, b_xT = front
            tb = min(128, Tn)
            nb = Tn // tb
            b_vt = [b_vT[0], b_vT[1], b_vT[2]]
            if pre:
                qw = [wload([(lambda t: wview_k(t, 512), wsrc(wb_in, g * 512, 512))], b_w["in"]) for g in range(2)]
            if prev_finish is not None:
                prev_finish[0]()
            for g in range(2):
                wt, wbuf = qw[g]
                wv = wview_k(wt, 512)
                for j in range(4):
                    m = g * 4 + j
                    bi = nextbank()
                    fm_group(wv, j, hT, Tn, 8, bi, [b_hT], wbuf)
                    P.op(ACT, lambda e, m=m, bi=bi: e.activation(qz[0][0:64, m, 0:Tn], ps[0:64, bi * 512:bi * 512 + Tn], AF.Copy, scale=0.125),
                         reads=[bank[bi]], writes=[b_qz])
                    P.op(DVE, lambda e, m=m, bi=bi: e.tensor_scalar(qz[1][64:128, m, 0:Tn], ps[64:128, bi * 512:bi * 512 + Tn], 0.125, None, ALU.mult),
                         reads=[bank[bi]], writes=[b_qz])
            for g in range(2, 6):
                wt, wbuf = wload([(lambda t: wview_k(t, 512), wsrc(wb_in, g * 512, 512))], b_w["in"])
                wv = wview_k(wt, 512)
                isk = g < 4
                half = g % 2
                if isk:
                    for j in range(4):
                        m = half * 4 + j
                        bi = nextbank()
                        fm_group(wv, j, hT, Tn, 8, bi, [b_hT], wbuf)
                        P.op(ACT if j % 2 == 0 else DVE,
                             (lambda e, m=m, bi=bi: e.activation(kTs[:, m, 0:Tn], bk(bi, Tn), AF.Copy)) if j % 2 == 0 else
                             (lambda e, m=m, bi=bi: e.tensor_copy(kTs[:, m, 0:Tn], bk(bi, Tn))),
                             reads=[bank[bi]], writes=[b_kTs])
                for b in range(nb):
                    bi = nextbank()
                    for kc in range(8):
                        P.op(PE, lambda e, kc=kc, b=b, bi=bi, wv=wv: e.matmul(bk(bi, 512, tb), lhsT=hT[:, kc, b * tb:(b + 1) * tb], rhs=wv[:, kc, :],
                                                                             start=(kc == 0), stop=(kc == 7)),
                             reads=[wbuf, b_hT], writes=[bank[bi]])
                    si = rr("stg", 4)
                    P.op(DVE if b % 2 == 0 else ACT,
                         (lambda e, si=si, bi=bi: e.tensor_copy(stg[si][0:tb, :], bk(bi, 512, tb))) if b % 2 == 0 else
                         (lambda e, si=si, bi=bi: e.activation(stg[si][0:tb, :], bk(bi, 512, tb), AF.Copy)),
                         reads=[bank[bi]], writes=[b_stg[si]])
                    dst = koutsrc if isk else voutsrc
                    P.dma(POOL, lambda e, si=si, b=b, dst=dst, half=half: e.dma_start(out=dst[b * tb:(b + 1) * tb, half * 512:(half + 1) * 512],
                                                                                     in_=stg[si][0:tb, :]), "stg%d" % si, reads=[b_stg[si]])
                    if not isk:
                        P.dma(POOL, lambda e, si=si, b=b, half=half: e.dma_start(
                            out=vscr[s_, half * 4:half * 4 + 4, kpos0 + b * tb:kpos0 + (b + 1) * tb, :].rearrange("c t d -> t c d"),
                            in_=stg[si][0:tb, :].rearrange("t (c d) -> t c d", d=128)), "stgv%d" % si,
                            reads=[b_stg[si]], writes=[b_vscr[s_]])
                if isk and half == 1:
                    P.dma(POOL, lambda e: e.dma_start(out=kscr[s_, :, :, kpos0:kpos0 + Tn].rearrange("c p t -> p c t"), in_=kTs[:, :, 0:Tn]),
                          "kTs", reads=[b_kTs], writes=[b_kscr[s_]])
            if prev_finish is not None:
                prev_finish[1]()
            FB = 7

            def conv_gen():
                for m in range(8):
                    wt, wbuf = wload([(lambda t: wview_k(t, 384), wsrc(wb_cv, m * 384, 384))], b_w["cv"])
                    wv = wt[:, 0:3072].rearrange("p (k n) -> p k n", n=384)
                    ia = rr("tA", 2)
                    ci = rr("cu", 2)
                    io = rr("tB", 2)

                    def grp(j):
                        for kc in range(8):
                            P.op(PE, lambda e, kc=kc, j=j, wv=wv: e.matmul(bk(FB, Tn), lhsT=wv[:, kc, j * 128:(j + 1) * 128], rhs=hT[:, kc, 0:Tn],
                                                                    start=(kc == 0), stop=(kc == 7)),
                                 reads=[wbuf, b_hT], writes=[bank[FB]])
                            if kc in (1, 3, 5):
                                yield
                    yield from grp(1)
                    P.op(DVE, lambda e, ia=ia: e.tensor_copy(tmpA[ia][:, 0:Tn], bk(FB, Tn)), reads=[bank[FB]], writes=[b_tmpA[ia]])
                    yield
                    yield from grp(2)
                    P.op(POOL, lambda e, m=m, ci=ci: e.tensor_copy(cu[ci][:, 0:2], cuh[:, m, :]), reads=[b_cuh], writes=[b_cu[ci]])
                    P.op(DVE, lambda e, ia=ia, ci=ci: e.tensor_tensor(cu[ci][:, 2:Tn + 2], tmpA[ia][:, 0:Tn], bk(FB, Tn), ALU.mult),
                         reads=[b_tmpA[ia], bank[FB]], writes=[b_cu[ci]])
                    P.op(POOL, lambda e, m=m, ci=ci: e.tensor_copy(cuh[:, m, :], cu[ci][:, Tn:Tn + 2]), reads=[b_cu[ci]], writes=[b_cuh])
                    conv3(([b_cu[ci]] + b_vt,), cu[ci], Tn, "wc", "bc", m, io)
                    yield
                    yield from grp(0)
                    P.op(DVE, lambda e, m=m, io=io: e.tensor_tensor(obT[:, m, 0:Tn], tmpB[io][:, 0:Tn], bk(FB, Tn), ALU.mult),
                         reads=[b_tmpB[io], bank[FB]], writes=[b_obT])
                    yield

            filler = conv_gen()
            attention(s_, Tn, units_blocks, filler)
            for _ in filler:
                pass
            if conv_out is not None:
                for r in range(2):
                    P.dma(POOL, lambda e, r=r: e.dma_start(out=conv_out[r, :].rearrange("(m p) -> p m", p=128), in_=cuh[:, :, r],
                                                           allow_slow_non_contiguous=True), "cvout", reads=[b_cuh], group=True)

            for m in range(8):
                wt, wbuf = wload([(lambda t: wview_k(t, 512), wsrc(wb_mg, m * 512, 512))], b_w["mg"])
                wfull = wview_k(wt, 512)
                wg = wfull[:, :, 0:256]
                wa = wfull[:, :, 256:384]
                wbb_ = wfull[:, :, 384:512]
                bga, bgb, bya, byb = nextbank(), nextbank(), nextbank(), nextbank()
                fm_group(wg, 0, hT, Tn, 8, bga, [b_hT], wbuf)
                fm_group(wg, 1, hT, Tn, 8, bgb, [b_hT], wbuf)
                fm_group(wa, 0, oT, Tn, 8, bya, [b_oT], wbuf)
                fm_group(wbb_, 0, obT, Tn, 8, byb, [b_obT], wbuf)
                i0 = rr("tA", 2); i1 = rr("tA", 2); j0 = rr("tB", 2); j1 = rr("tB", 2)
                P.op(ACT, lambda e, i0=i0, bga=bga: e.activation(tmpA[i0][:, 0:Tn], bk(bga, Tn), AF.Sigmoid), reads=[bank[bga]], writes=[b_tmpA[i0]])
                P.op(ACT, lambda e, i1=i1, bgb=bgb: e.activation(tmpA[i1][:, 0:Tn], bk(bgb, Tn), AF.Sigmoid), reads=[bank[bgb]], writes=[b_tmpA[i1]])
                P.op(DVE, lambda e, i0=i0, j0=j0, bya=bya: e.tensor_tensor(tmpB[j0][:, 0:Tn], tmpA[i0][:, 0:Tn], bk(bya, Tn), ALU.mult),
                     reads=[b_tmpA[i0], bank[bya]], writes=[b_tmpB[j0]])
                P.op(DVE, lambda e, i1=i1, j1=j1, byb=byb: e.tensor_tensor(tmpB[j1][:, 0:Tn], tmpA[i1][:, 0:Tn], bk(byb, Tn), ALU.mult),
                     reads=[b_tmpA[i1], bank[byb]], writes=[b_tmpB[j1]])
                P.op(POOL, lambda e, m=m, j0=j0, j1=j1: e.tensor_tensor(mT[:, m, 0:Tn], tmpB[j0][:, 0:Tn], tmpB[j1][:, 0:Tn], ALU.add),
                     reads=[b_tmpB[j0], b_tmpB[j1]], writes=[b_mT])
            for g in range(2):
                wt, wbuf = wload([(lambda t: wview_k(t, 512), wsrc(wb_out, g * 512, 512))], b_w["out"])
                wv = wview_k(wt, 512)
                for j in range(4):
                    m = g * 4 + j
                    bi = nextbank()
                    fm_group(wv, j, mT, Tn, 8, bi, [b_mT], wbuf)
                    P.op(DVE, lambda e, m=m, bi=bi: e.scalar_tensor_tensor(xT[:, m, 0:Tn], bk(bi, Tn), mcol(16 + m, s_), xT[:, m, 0:Tn], ALU.mult, ALU.add),
                         reads=[bank[bi], b_modT, b_xT], writes=[b_xT])
            rms_to_hT(Tn, s_, 1, xT, b_xT)
            for f in range(NFF):
                if f % 2 == 0:
                    upw = wload([(lambda t: wview_k(t, 512), wsrc(wb_up, f * 256, 512))], b_w["up"])
                wt, wbuf = upw
                wv = wview_k(wt, 512)[:, :, (f % 2) * 256:(f % 2) * 256 + 256]
                bu, bg = nextbank(), nextbank()
                fm_group(wv, 0, hT, Tn, 8, bu, [b_hT], wbuf)
                fm_group(wv, 1, hT, Tn, 8, bg, [b_hT], wbuf)
                ci = rr("cu", 2); io = rr("tB", 2); ia = rr("tA", 2)
                P.op(POOL, lambda e, f=f, ci=ci: e.tensor_copy(cu[ci][:, 0:2], uph[:, f, :]), reads=[b_uph], writes=[b_cu[ci]])
                P.op(ACT, lambda e, ci=ci, bu=bu: e.activation(cu[ci][:, 2:Tn + 2], bk(bu, Tn), AF.Copy), reads=[bank[bu]], writes=[b_cu[ci]])
                P.op(POOL, lambda e, f=f, ci=ci: e.tensor_copy(uph[:, f, :], cu[ci][:, Tn:Tn + 2]), reads=[b_cu[ci]], writes=[b_uph])
                conv3(([b_cu[ci]] + b_vt,), cu[ci], Tn, "wf", "bf", f, io)
                P.op(ACT, lambda e, io=io, ia=ia: e.activation(tmpA[ia][:, 0:Tn], tmpB[io][:, 0:Tn], AF.Silu), reads=[b_tmpB[io]], writes=[b_tmpA[ia]])
                P.op(DVE, lambda e, f=f, ia=ia, bg=bg: e.tensor_tensor(actT[:, f, 0:Tn], tmpA[ia][:, 0:Tn], bk(bg, Tn), ALU.mult),
                     reads=[b_tmpA[ia], bank[bg]], writes=b_big)
            if ffn_out is not None:
                for r in range(2):
                    P.dma(POOL, lambda e, r=r: e.dma_start(out=ffn_out[r, :].rearrange("(f p) -> p f", p=128), in_=uph[:, :, r],
                                                         allow_slow_non_contiguous=True), "ffout", reads=[b_uph], group=True)
            dnw = [wload([(lambda t: t[:, 0:NFF * 128].rearrange("p (k n) -> p k n", n=128),
                           wb_dn[m, :, :].rearrange("p (k n) -> p k n", n=128))], b_w["down"]) for m in range(2)]
            nfront = next_front_fn() if next_front_fn is not None else None
            for m in range(8):
                if m < 2:
                    wt, wbuf = dnw[m]
                else:
                    wt, wbuf = wload([(lambda t: t[:, 0:NFF * 128].rearrange("p (k n) -> p k n", n=128),
                                       wb_dn[m, :, :].rearrange("p (k n) -> p k n", n=128))], b_w["down"])
                wv = wt[:, 0:NFF * 128].rearrange("p (k n) -> p k n", n=128)
                bi = nextbank()
                fm_group(wv, 0, actT, Tn, NFF, bi, b_big, wbuf)
                P.op(DVE, lambda e, m=m, bi=bi: e.scalar_tensor_tensor(xT[:, m, 0:Tn], bk(bi, Tn), mcol(40 + m, s_), xT[:, m, 0:Tn], ALU.mult, ALU.add),
                     reads=[bank[bi], b_modT, b_xT], writes=[b_xT])
            def finish_a():
                rms_stats(Tn, xT, b_xT)
                for m in range(8):
                    P.op(DVE, lambda e, m=m: e.scalar_tensor_tensor(xT[:, m, 0:Tn], xT[:, m, 0:Tn], vcol("gf", m), rstd[:, 0:Tn], ALU.mult, ALU.mult),
                         reads=[b_xT, b_rstd, b_vT[0]], writes=[b_xT])

            def emit_J():
                for b in range(nb):
                    for hh in range(2):
                        bi = nextbank()
                        for j in range(4):
                            m = hh * 4 + j
                            P.op(PE, lambda e, j=j, m=m, b=b, bi=bi: e.transpose(ps[0:tb, bi * 512 + j * 128:bi * 512 + (j + 1) * 128],
                                                                                xT[:, m, b * tb:(b + 1) * tb], ident[:, :]),
                                 reads=[b_xT] + b_const_all, writes=[bank[bi]])
                        si = rr("stg", 4)
                        P.op(ACT if hh == 0 else DVE,
                             (lambda e, si=si, bi=bi: e.activation(stg[si][0:tb, :], bk(bi, 512, tb), AF.Copy)) if hh == 0 else
                             (lambda e, si=si, bi=bi: e.tensor_copy(stg[si][0:tb, :], bk(bi, 512, tb))),
                             reads=[bank[bi]], writes=[b_stg[si]])
                        P.dma(POOL, lambda e, si=si, b=b, hh=hh: e.dma_start(out=ysrc[b * tb:(b + 1) * tb, hh * 512:(hh + 1) * 512], in_=stg[si][0:tb, :]),
                              "stg%d" % si, reads=[b_stg[si]])

            return (finish_a, emit_J), nfront

        def attention(s_, Tn, blocks, filler=None):
            Tk = max(k0 + nk for (_, k0, nk, _, _) in blocks)
            units = []
            for c in range(8):
                for bi_, blk in enumerate(blocks):
                    units.append((c, bi_, blk))
            nblk = len(blocks)
            nu = len(units)
            ZB = [(0, 1), (2, 3)]
            AB = (4, 5)
            OB = 6
            slot_of = {}
            accslot = {}

            def load_c(c):
                sl = rr("att", 2)
                slot_of[c] = sl
                P.dma(SP, lambda e: e.dma_start(out=kTh[sl][:, 0:Tk], in_=kscr[s_, c, :, 0:Tk]), "kTh%d" % sl,
                      reads=[b_kscr[s_]], writes=[b_big[sl]])
                nfull = Tk // 128
                if nfull:
                    P.dma(SP, lambda e: e.dma_start(out=vh[sl][:, 0:nfull, :], in_=vscr[s_, c, 0:nfull * 128, :].rearrange("(k p) d -> p k d", p=128)),
                          "vh%d" % sl, reads=[b_vscr[s_]], writes=[b_big[2 + sl]])
                rem = Tk - nfull * 128
                if rem:
                    P.dma(SP, lambda e: e.dma_start(out=vh[sl][0:rem, nfull, :], in_=vscr[s_, c, nfull * 128:Tk, :]),
                          "vh%d" % sl, reads=[b_vscr[s_]], writes=[b_big[2 + sl]], group=bool(nfull))

            def zviews(bpair, nk, q0):
                return [ps[0:nk, bpair[h] * 512 + q0:bpair[h] * 512 + Tn] for h in range(2)]

            def st_z(u):
                c, bi_, (kbidx, k0, nk, q0, diag) = units[u]
                if bi_ == 0 and c == 0:
                    load_c(0)
                    load_c(1)
                sl = slot_of[c]
                zb = ZB[u % 2]
                for h in range(2):
                    P.op(PE, lambda e, h=h: e.matmul(ps[0:nk, zb[h] * 512 + q0:zb[h] * 512 + Tn], lhsT=kTh[sl][:, k0:k0 + nk],
                                                     rhs=qz[h][:, c, q0:Tn], start=True, stop=True),
                         reads=[b_big[sl], b_qz], writes=[bank[zb[h]]])

            def st_EL(u):
                c, bi_, (kbidx, k0, nk, q0, diag) = units[u]
                zb = ZB[u % 2]
                i = u % 2
                zin = ps[0:nk, zb[0] * 512:zb[0] * 512 + 1024].rearrange("p (h t) -> p h t", t=512)[:, :, q0:Tn]
                P.op(ACT, lambda e: e.activation(Et[i][0:nk, :, q0:Tn], zin, AF.Exp), reads=[bank[zb[0]], bank[zb[1]]], writes=[b_E[i]])
                P.op(ACT, lambda e: e.activation(Lt[i][0:nk, :, q0:Tn], Et[i][0:nk, :, q0:Tn], AF.Ln, bias=one_t[0:nk, 0:1]),
                     reads=[b_E[i], b_eps], writes=[b_L[i]])
                if diag:
                    for h in range(2):
                        P.op(DVE, lambda e, h=h: e.tensor_tensor(Lt[i][0:nk, h, q0:q0 + nk], Lt[i][0:nk, h, q0:q0 + nk], m01[0:nk, 0:nk], ALU.mult),
                             reads=[b_L[i]] + b_const_all, writes=[b_L[i]])
                if bi_ == 0:
                    accslot[(c, 0)] = None
                if bi_ < nblk - 1:
                    a_new = rr_acc()
                    if bi_ == 0:
                        P.op(DVE, lambda e: e.memset(Acc[a_new][:], 0.0), writes=[b_Acc[a_new]])
                        P.op(DVE, lambda e: e.tensor_copy(Acc[a_new][0:nk, :, q0:Tn], Lt[i][0:nk, :, q0:Tn]), reads=[b_L[i]], writes=[b_Acc[a_new]])
                    else:
                        a_old = accslot[(c, bi_)]
                        if q0 > 0:
                            P.op(DVE, lambda e: e.memset(Acc[a_new][:, :, 0:q0], 0.0), writes=[b_Acc[a_new]])
                        P.op(DVE, lambda e: e.tensor_tensor(Acc[a_new][:, :, q0:Tn], Acc[a_old][:, :, q0:Tn], Lt[i][:, :, q0:Tn], ALU.add),
                             reads=[b_Acc[a_old], b_L[i]], writes=[b_Acc[a_new]])
                    accslot[(c, bi_ + 1)] = a_new

            accc = [0]

            def rr_acc():
                accc[0] += 1
                return accc[0] % 3

            def st_arg(u):
                c, bi_, (kbidx, k0, nk, q0, diag) = units[u]
                sl = slot_of[c]
                i = u % 2
                a_in = accslot[(c, bi_)]
                for h in range(2):
                    out = ps[0:nk, AB[h] * 512 + q0:AB[h] * 512 + Tn]
                    P.op(PE, lambda e, h=h, out=out: e.matmul(out, lhsT=kTh[sl][:, k0:k0 + nk], rhs=qz[h][:, c, q0:Tn], start=True, stop=False),
                         reads=[b_big[sl], b_qz], writes=[bank[AB[h]]])
                    P.op(PE, lambda e, h=h, out=out: e.matmul(out, lhsT=ntri[0:nk, 0:nk], rhs=Lt[i][0:nk, h, q0:Tn], start=False, stop=(a_in is None)),
                         reads=[b_L[i]] + b_const_all, writes=[bank[AB[h]]])
                    if a_in is not None:
                        P.op(PE, lambda e, h=h, out=out: e.matmul(out, lhsT=nones[:, 0:nk], rhs=Acc[a_in][:, h, q0:Tn], start=False, stop=True),
                             reads=[b_Acc[a_in]] + b_const_all, writes=[bank[AB[h]]])

            def st_a(u):
                c, bi_, (kbidx, k0, nk, q0, diag) = units[u]
                i = u % 2
                ain = ps[0:nk, AB[0] * 512:AB[0] * 512 + 1024].rearrange("p (h t) -> p h t", t=512)[:, :, q0:Tn]
                P.op(ACT, lambda e: e.activation(At[i][0:nk, :, q0:Tn], ain, AF.Exp), reads=[bank[AB[0]], bank[AB[1]]], writes=[b_A[i]])
                if diag:
                    for h in range(2):
                        P.op(DVE, lambda e, h=h: e.tensor_tensor(At[i][0:nk, h, q0:q0 + nk], At[i][0:nk, h, q0:q0 + nk], m01[0:nk, 0:nk], ALU.mult),
                             reads=[b_A[i]] + b_const_all, writes=[b_A[i]])

            def st_av(u):
                c, bi_, (kbidx, k0, nk, q0, diag) = units[u]
                sl = slot_of[c]
                i = u % 2
                if bi_ == 0:
                    P.op(PE, lambda e: e.matmul(ps[:, OB * 512:OB * 512 + Tn], lhsT=ones[:, :], rhs=zer[:, 0:Tn],
                                                start=True, stop=False, skip_group_check=True),
                         reads=b_const_all, writes=[bank[OB]])
                for h in range(2):
                    P.op(PE, lambda e, h=h: e.matmul(ps[64 * h:64 * h + 64, OB * 512 + q0:OB * 512 + Tn], lhsT=vh[sl][0:nk, kbidx, 64 * h:64 * h + 64],
                                                     rhs=At[i][0:nk, h, q0:Tn], start=False, stop=(bi_ == nblk - 1 and h == 1), skip_group_check=True),
                         reads=[b_big[2 + sl], b_A[i]], writes=[bank[OB]])
                if bi_ == nblk - 1 and c + 2 < 8:
                    load_c(c + 2)
                if bi_ == nblk - 1:
                    P.op(DVE, lambda e: e.tensor_copy(oT[:, c, 0:Tn], ps[:, OB * 512:OB * 512 + Tn]), reads=[bank[OB]], writes=[b_oT])

            for s in range(-2, nu + 1):
                if 0 <= s < nu:
                    st_arg(s)
                if filler is not None and s >= 0:
                    next(filler, None)
                if 0 <= s + 2 < nu:
                    st_z(s + 2)
                if 0 <= s - 1 < nu:
                    st_av(s - 1)
                if 0 <= s + 1 < nu:
                    st_EL(s + 1)
                if 0 <= s < nu:
                    st_a(s)

        one_t = sb("one_t", [128, 1])
        P.op(POOL, lambda e: e.memset(one_t[:], 1.0), writes=[b_eps])

        for kb in range(PAST // 128):
            xi = rr("xin", 2)
            P.dma(SP, lambda e, kb=kb, xi=xi: e.dma_start(out=xin[xi][:, :], in_=ck[kb * 128:(kb + 1) * 128, :]), "xin%d" % xi, writes=[b_xin[xi]])
            for hh in range(2):
                bi = nextbank()
                for j in range(4):
                    m = hh * 4 + j
                    P.op(PE, lambda e, j=j, m=m, xi=xi, bi=bi: e.transpose(ps[:, bi * 512 + j * 128:bi * 512 + (j + 1) * 128],
                                                                           xin[xi][:, m * 128:(m + 1) * 128], ident[:, :]),
                         reads=[b_xin[xi]] + b_const_all, writes=[bank[bi]])
                P.op(DVE, lambda e, hh=hh, kb=kb, bi=bi: e.tensor_copy(kTs[:, hh * 4:hh * 4 + 4, (kb % 4) * 128:(kb % 4 + 1) * 128],
                                                                      bk(bi).rearrange("p (j t) -> p j t", t=128)),
                     reads=[bank[bi]], writes=[b_kTs])
            if kb % 4 == 3:
                k0 = (kb - 3) * 128
                P.dma(SP, lambda e, k0=k0: e.dma_start(out=kscr[NP, :, :, k0:k0 + 512].rearrange("c p t -> p c t"), in_=kTs[:, :, :]),
                      "kTs", reads=[b_kTs], writes=[b_kscr[NP]])
        for kb in range(PAST // 128):
            P.dma(POOL, lambda e, kb=kb: e.dma_start(out=vscr[NP, :, kb * 128:(kb + 1) * 128, :].rearrange("c t d -> t c d"),
                                                     in_=cv[kb * 128:(kb + 1) * 128, :].rearrange("t (c d) -> t c d", d=128)),
                  "cvcast", writes=[b_vscr[NP]], group=True)

        pend = [None]
        tiles = [(s_, i) for s_ in range(NP) for i in range(NT)]
        nfr = None
        for ti, (s_, i) in enumerate(tiles):
            if i == 0:
                P.op(POOL, lambda e: e.memset(cuh[:], 0.0), writes=[b_cuh])
                P.op(POOL, lambda e: e.memset(uph[:], 0.0), writes=[b_uph])
            r0 = s_ * SEQ + i * T
            blocks = []
            for kb in range(4 * i + 3, -1, -1):
                j = kb - 4 * i
                blocks.append((kb, kb * 128, 128, 128 * j if j >= 0 else 0, j >= 0))
            last = (i == NT - 1)
            nff = None
            if ti + 1 < len(tiles):
                s2, i2 = tiles[ti + 1]
                r2 = s2 * SEQ + i2 * T
                nff = (lambda r2=r2, s2=s2: tile_front(T, xp[r2:r2 + T, :], s2))
            pend[0], nfr = process_tile(s_, T, xp[r0:r0 + T, :], yp[r0:r0 + T, :], nkp[r0:r0 + T, :], nvp[r0:r0 + T, :], i * T, blocks, last,
                                        ncp[2 * s_:2 * s_ + 2, :] if last else None, nfp[2 * s_:2 * s_ + 2, :] if last else None, pend[0],
                                        front=nfr, next_front_fn=nff)

        s_ = NP
        for r in range(2):
            P.op(POOL, lambda e, r=r: e.tensor_copy(cuh[:, :, r], vcols(("sc", r), 8)), reads=[b_vT[1]], writes=[b_cuh])
            P.op(POOL, lambda e, r=r: e.tensor_copy(uph[:, :, r], vcols(("sf", r), NFF)), reads=[b_vT[2]], writes=[b_uph])
        blocks = [(PAST // 128, PAST, SAMP, 0, True)] + [(kb, kb * 128, 128, 0, False) for kb in range(PAST // 128 - 1, -1, -1)]
        fin, _ = process_tile(s_, SAMP, xs, ys, nks, nvs, PAST, blocks, True, ncs, nfs, pend[0])
        fin[0]()
        fin[1]()

        P.emit(nc)
    return nc


_NC_CACHE = {}


def _prep_maps(inp, NP, SEQ, ncores):
    f = lambda a: np.ascontiguousarray(a, dtype=np.float32)
    shared = {
        "w_ada": f(inp["w_ada"][0]), "b_ada": f(inp["b_ada"][0]), "g1": f(inp["g_norm1"][0]), "w_in": f(inp["w_in"][0]),
        "w_conv": f(inp["w_conv"][0]), "b_conv": f(inp["b_conv"][0]), "w_ba": f(inp["w_branch_a"][0]),
        "w_bb": f(inp["w_branch_b"][0]), "w_out": f(inp["w_out"][0]), "g2": f(inp["g_norm2"][0]), "w_up": f(inp["w_up"][0]),
        "w_fconv": f(inp["w_fconv"][0]), "b_fconv": f(inp["b_fconv"][0]), "w_down": f(inp["w_down"][0]), "gfin": f(inp["g_final"]),
    }
    maps = []
    for c in range(ncores):
        m = dict(shared)
        m["xp"] = f(inp["x_prompt"][c * NP:(c + 1) * NP].reshape(NP * SEQ, D))
        m["xs"] = f(inp["x_sample"][c])
        m["ck"] = f(inp["cache_k"][0, c].reshape(-1, D))
        m["cv"] = f(inp["cache_v"][0, c].reshape(-1, D))
        m["sc"] = f(inp["state_conv"][0, c])
        m["sf"] = f(inp["state_ffn_conv"][0, c])
        m["cvec"] = f(np.concatenate([inp["c_prompt"][c * NP:(c + 1) * NP], inp["c_sample"][c:c + 1]], axis=0))
        maps.append(m)
    return maps


def run(inp, NP, SEQ, ncores, PAST=1024, SAMP=32):
    key = (SEQ, NP, PAST, SAMP)
    if key not in _NC_CACHE:
        _NC_CACHE[key] = build(SEQ, NP, PAST, SAMP)
    nc = _NC_CACHE[key]
    maps = _prep_maps(inp, NP, SEQ, ncores)
    res = run_bass_kernel_spmd(nc, maps, core_ids=list(range(ncores)))
    rs = res.results
    cat = lambda k: np.concatenate([r[k] for r in rs], axis=0)
    B = NP * ncores
    y_p = cat("yp").reshape(B, SEQ, D)
    y_s = cat("ys").reshape(ncores, SAMP, D)
    nk_p = cat("nkp").reshape(1, B, SEQ, NH, 64)
    nv_p = cat("nvp").reshape(1, B, SEQ, NH, 64)
    nc_p = cat("ncp").reshape(1, B, 2, D)
    nf_p = cat("nfp").reshape(1, B, 2, DFF)
    nk_s = cat("nks").reshape(1, ncores, SAMP, NH, 64)
    nv_s = cat("nvs").reshape(1, ncores, SAMP, NH, 64)
    nc_s = cat("ncs").reshape(1, ncores, 2, D)
    nf_s = cat("nfs").reshape(1, ncores, 2, DFF)
    return tuple(np.ascontiguousarray(a, dtype=np.float32) for a in (y_p, y_s, nk_p, nv_p, nc_p, nf_p, nk_s, nv_s, nc_s, nf_s))


def kernel(**inputs):
    return run(inputs, NP=2, SEQ=4096, ncores=8)
```

```python
import numpy as np
from contextlib import ExitStack
import concourse.bass as bass
import concourse.mybir as mybir
from concourse.bass_utils import run_bass_kernel_spmd

F32 = mybir.dt.float32
BF16 = mybir.dt.bfloat16
AF = mybir.ActivationFunctionType
ALU = mybir.AluOpType

PE, ACT, DVE, POOL, SP = "pe", "act", "dve", "pool", "sp"
ENGS = [PE, ACT, DVE, POOL, SP]

D = 1024
DFF = 2816
NFF = 22
NH = 16
EPS = 1e-6


class Buf:
    def __init__(self, name):
        self.name = name
        self.last_w = None
        self.readers = []


class Op:
    __slots__ = ("eng", "fn", "deps", "signal", "dma_key", "dma_cnt", "is_dma")

    def __init__(self, eng, fn):
        self.eng = eng
        self.fn = fn
        self.deps = []
        self.signal = 0
        self.dma_key = None
        self.dma_cnt = 0
        self.is_dma = False


class Prog:
    def __init__(self):
        self.ops = {e: [] for e in ENGS}
        self.dma_keys = {}

    def _add_deps(self, op, reads, writes):
        deps = []
        for b in reads:
            if b.last_w is not None:
                deps.append(("raw", b.last_w))
        for b in writes:
            if b.last_w is not None:
                deps.append(("waw", b.last_w))
            for r in b.readers:
                deps.append(("war", r))
        for kind, d in deps:
            if d is op:
                continue
            if d.eng == op.eng and not d.is_dma and not op.is_dma:
                if op.eng == PE:
                    continue
            op.deps.append(d)
        for b in reads:
            b.readers.append(op)
        for b in writes:
            b.last_w = op
            b.readers = []

    def op(self, eng, fn, reads=(), writes=()):
        o = Op(eng, fn)
        self._add_deps(o, reads, writes)
        self.ops[eng].append(o)
        return o

    def dma(self, eng, fn, key, reads=(), writes=(), group=False):
        o = Op(eng, fn)
        o.is_dma = True
        st = self.dma_keys.setdefault(key, [0, None])
        if st[1] is not None and not group:
            o.deps.append(st[1])
        self._add_deps(o, reads, writes)
        st[0] += 16
        st[1] = o
        o.dma_key = key
        o.dma_cnt = st[0]
        self.ops[eng].append(o)
        return o

    def emit(self, nc):
        needed = set()
        for e in ENGS:
            for o in self.ops[e]:
                for d in o.deps:
                    if not d.is_dma:
                        needed.add(id(d))
        for e in ENGS:
            c = 0
            for o in self.ops[e]:
                if not o.is_dma and id(o) in needed:
                    c += 1
                    o.signal = c
        with ExitStack() as st:
            esem = {e: st.enter_context(nc.semaphore("s_" + e)) for e in ENGS}
            dsem = {k: st.enter_context(nc.semaphore("d_%d" % i)) for i, k in enumerate(self.dma_keys)}
            block = st.enter_context(nc.Block())
            prog = self

            def run(eng_name, eng):
                known = {}
                for o in prog.ops[eng_name]:
                    w = {}
                    for d in o.deps:
                        if d.is_dma:
                            k, v = ("d", d.dma_key), d.dma_cnt
                        else:
                            k, v = ("e", d.eng), d.signal
                        if v > w.get(k, 0):
                            w[k] = v
                    for k, v in w.items():
                        if known.get(k, 0) >= v:
                            continue
                        known[k] = v
                        eng.wait_ge(dsem[k[1]] if k[0] == "d" else esem[k[1]], v)
                    ins = o.fn(eng)
                    if o.is_dma:
                        ins.then_inc(dsem[o.dma_key], 16)
                    elif o.signal:
                        ins.then_inc(esem[eng_name], 1)
                if eng_name == SP:
                    for k, stt in prog.dma_keys.items():
                        if known.get(("d", k), 0) < stt[0]:
                            eng.wait_ge(dsem[k], stt[0])

            @block.tensor
            def _(e):
                run(PE, e)

            @block.scalar
            def _(e):
                run(ACT, e)

            @block.vector
            def _(e):
                run(DVE, e)

            @block.gpsimd
            def _(e):
                run(POOL, e)

            @block.sync
            def _(e):
                run(SP, e)


def build(SEQ=4096, NP=2, PAST=1024, SAMP=32):
    nc = bass.Bass("TRN2", target_bir_lowering=False)
    T = 512
    NT = SEQ // T
    TKMAX = max(SEQ, PAST + SAMP)
    NS = NP + 1
    P = Prog()

    def din(name, shape):
        return nc.dram_tensor(name, list(shape), F32, kind="ExternalInput").ap()

    def dout(name, shape):
        return nc.dram_tensor(name, list(shape), F32, kind="ExternalOutput").ap()

    xp = din("xp", [NP * SEQ, D]); xs = din("xs", [SAMP, D])
    ck = din("ck", [PAST, D]); cv = din("cv", [PAST, D])
    sc = din("sc", [2, D]); sf = din("sf", [2, DFF]); cvec = din("cvec", [NS, D])
    w_ada = din("w_ada", [D, 6 * D]); b_ada = din("b_ada", [6 * D]); g1 = din("g1", [D])
    w_in = din("w_in", [D, 8 * D]); w_conv = din("w_conv", [3, D]); b_conv = din("b_conv", [D])
    w_ba = din("w_ba", [D, D]); w_bb = din("w_bb", [D, D]); w_out = din("w_out", [D, D])
    g2 = din("g2", [D]); w_up = din("w_up", [D, 2 * DFF]); w_fconv = din("w_fconv", [3, DFF])
    b_fconv = din("b_fconv", [DFF]); w_down = din("w_down", [DFF, D]); gfin = din("gfin", [D])
    yp = dout("yp", [NP * SEQ, D]); ys = dout("ys", [SAMP, D])
    nkp = dout("nkp", [NP * SEQ, D]); nvp = dout("nvp", [NP * SEQ, D])
    ncp = dout("ncp", [NP * 2, D]); nfp = dout("nfp", [NP * 2, DFF])
    nks = dout("nks", [SAMP, D]); nvs = dout("nvs", [SAMP, D])
    ncs = dout("ncs", [2, D]); nfs = dout("nfs", [2, DFF])

    def dscr(name, shape):
        return nc.dram_tensor(name, list(shape), BF16).ap()

    wb_ada = dscr("wb_ada", [D, 6 * D]); wb_in = dscr("wb_in", [D, 3 * D])
    wb_cv = dscr("wb_cv", [D, 8 * 3 * 128])
    wb_mg = dscr("wb_mg", [D, 8 * 4 * 128])
    wb_out = dscr("wb_out", [D, D])
    wb_up = dscr("wb_up", [D, 2 * DFF])
    wb_dn = dscr("wb_dn", [8, 128, NFF * 128])
    kscr = dscr("kscr", [NS, 8, 128, TKMAX]); vscr = dscr("vscr", [NS, 8, TKMAX, 128])

    with ExitStack() as st:
        def sb(name, shape, dt=F32):
            return st.enter_context(nc.sbuf_tensor(name, list(shape), dt))

        ps = st.enter_context(nc.psum_tensor("ps", [128, 4096], F32))
        bank = [Buf("bank%d" % i) for i in range(8)]

        def bk(i, n=512, p=128):
            return ps[0:p, i * 512:i * 512 + n]

        ident = sb("ident", [128, 128]); tmpc = sb("tmpc", [128, 128])
        ntri = sb("ntri", [128, 128], BF16); nones = sb("nones", [128, 128], BF16)
        ones = sb("ones", [128, 128], BF16); m01 = sb("m01", [128, 128], BF16)
        b_const = Buf("const")
        P.op(POOL, lambda e: e.memset(ident[:], 1.0), writes=[b_const])
        P.op(POOL, lambda e: e.affine_select(out=ident[:], in_=ident[:], pattern=[[-1, 128]], compare_op=ALU.is_equal,
                                            fill=0.0, base=0, channel_multiplier=1), reads=[b_const], writes=[b_const])
        P.op(POOL, lambda e: e.memset(tmpc[:], -1.0), writes=[b_const])
        P.op(POOL, lambda e: e.affine_select(out=tmpc[:], in_=tmpc[:], pattern=[[-1, 128]], compare_op=ALU.is_ge,
                                            fill=0.0, base=0, channel_multiplier=1), reads=[b_const], writes=[b_const])
        P.op(DVE, lambda e: e.tensor_copy(ntri[:], tmpc[:]), reads=[b_const], writes=[b_const])
        b_c2 = Buf("const2")
        tmpd = sb("tmpd", [128, 128])
        P.op(POOL, lambda e: e.memset(tmpd[:], 1.0), writes=[b_c2])
        P.op(POOL, lambda e: e.affine_select(out=tmpd[:], in_=tmpd[:], pattern=[[1, 128]], compare_op=ALU.is_gt,
                                            fill=0.0, base=0, channel_multiplier=-1), reads=[b_c2], writes=[b_c2])
        P.op(DVE, lambda e: e.tensor_copy(m01[:], tmpd[:]), reads=[b_c2], writes=[b_c2])
        zer = sb("zer", [128, 512], BF16)
        P.op(POOL, lambda e: e.memset(zer[:], 0.0), writes=[b_c2])
        P.op(POOL, lambda e: e.memset(nones[:], -1.0), writes=[b_c2])
        P.op(POOL, lambda e: e.memset(ones[:], 1.0), writes=[b_c2])
        b_const_all = [b_const, b_c2]

        b_w = {n: [] for n in ("ada", "in", "cv", "mg", "out", "up", "down")}

        def cast(name, dst, src):
            bw = Buf("w_%s_%d" % (name, len(b_w[name])))
            b_w[name].append(bw)
            P.dma(POOL, lambda e: e.dma_start(out=dst, in_=src), "wcast_" + name, writes=[bw], group=True)

        for r0 in range(0, D, 128):
            cast("ada", wb_ada[r0:r0 + 128, :], w_ada[r0:r0 + 128, :])
        for r0 in range(0, D, 128):
            cast("in", wb_in[r0:r0 + 128, :], w_in[r0:r0 + 128, 0:3 * D])
        late_hooks = []

        def cast_late():
          for r0 in range(0, D, 128):
            for gi in range(3):
                  cast("cv", wb_cv[r0:r0 + 128, :].rearrange("p (m g n) -> p m g n", g=3, n=128)[:, :, gi, :],
                       w_in[r0:r0 + 128, (3 + gi) * D:(4 + gi) * D].rearrange("p (m n) -> p m n", n=128))
          for r0 in range(0, D, 128):
              for gi, srcw in enumerate((w_in[:, 6 * D:7 * D], w_in[:, 7 * D:8 * D], w_ba, w_bb)):
                  cast("mg", wb_mg[r0:r0 + 128, :].rearrange("p (m g n) -> p m g n", g=4, n=128)[:, :, gi, :],
                       srcw[r0:r0 + 128, :].rearrange("p (m n) -> p m n", n=128))
          for r0 in range(0, D, 128):
              cast("out", wb_out[r0:r0 + 128, :], w_out[r0:r0 + 128, :])
          for r0 in range(0, D, 128):
              for gi in range(2):
                  cast("up", wb_up[r0:r0 + 128, :].rearrange("p (f g n) -> p f g n", g=2, n=128)[:, :, gi, :],
                       w_up[r0:r0 + 128, gi * DFF:(gi + 1) * DFF].rearrange("p (f n) -> p f n", n=128))
          for m in range(8):
              for f0 in range(0, NFF, 11):
                  cast("down", wb_dn[m, :, f0 * 128:(f0 + 11) * 128].rearrange("p (f n) -> p f n", n=128),
                       w_down[f0 * 128:(f0 + 11) * 128, m * 128:(m + 1) * 128].rearrange("(f p) n -> p f n", p=128))

        late_hooks.append(cast_late)

        vr = [sb("vr%d" % i, [128, 128]) for i in range(3)]
        vT = [sb("vT%d" % i, [128, 128]) for i in range(3)]
        b_vr = [Buf("vr%d" % i) for i in range(3)]
        b_vT = [Buf("vT%d" % i) for i in range(3)]
        R = {}
        rowp = [0, 0, 0]

        def vload(key, ti, src2d, nrows):
            r0 = rowp[ti]
            R[key] = (ti, r0)
            P.dma(SP, lambda e: e.dma_start(out=vr[ti][r0:r0 + nrows, :], in_=src2d), "vload%d" % ti, writes=[b_vr[ti]], group=True)
            rowp[ti] += nrows

        def v2(ap1d, n):
            return ap1d.rearrange("(a b) -> a b", b=128)

        for s_ in range(NS):
            vload(("c", s_), 0, v2(cvec[s_, :], 8), 8)
        vload("b_ada", 0, v2(b_ada, 48), 48)
        vload("g1", 0, v2(g1, 8), 8); vload("g2", 0, v2(g2, 8), 8); vload("gf", 0, v2(gfin, 8), 8)
        for i in range(3):
            vload(("wc", i), 0, v2(w_conv[i, :], 8), 8)
        vload("bc", 0, v2(b_conv, 8), 8)
        for i in range(3):
            vload(("wf", i), 1, v2(w_fconv[i, :], NFF), NFF)
        vload("bf", 1, v2(b_fconv, NFF), NFF)
        for i in range(2):
            vload(("sc", i), 1, v2(sc[i, :], 8), 8)
        for i in range(2):
            vload(("sf", i), 2, v2(sf[i, :], NFF), NFF)
        for ti in range(3):
            n = rowp[ti]
            P.op(PE, lambda e, ti=ti, n=n: e.transpose(bk(ti, n), vr[ti][0:n, :], ident[0:n, 0:n]),
                 reads=[b_vr[ti]] + b_const_all, writes=[bank[ti]])
            P.op(DVE, lambda e, ti=ti, n=n: e.tensor_copy(vT[ti][:, 0:n], bk(ti, n)), reads=[bank[ti]], writes=[b_vT[ti]])

        def vcol(key, j=0):
            ti, r0 = R[key]
            return vT[ti][:, r0 + j:r0 + j + 1]

        def vcols(key, n):
            ti, r0 = R[key]
            return vT[ti][:, r0:r0 + n]

        NW = 3
        wsl = [sb("wsl%d" % i, [128, 4096], BF16) for i in range(NW)]
        b_wsl = [Buf("wsl%d" % i) for i in range(NW)]
        wctr = [0]

        def wload(pieces, deps):
            i = wctr[0] % NW
            wctr[0] += 1
            if len(pieces) == 1:
                dv0, src0 = pieces[0]
                kh = src0.shape[1] // 2
                pieces = [((lambda t, dv0=dv0: dv0(t)[:, 0:kh, :]), src0[:, 0:kh, :]),
                          ((lambda t, dv0=dv0: dv0(t)[:, kh:, :]), src0[:, kh:, :])]
            for n_, (dv, src) in enumerate(pieces):
                P.dma(SP, lambda e, dv=dv, src=src, i=i: e.dma_start(out=dv(wsl[i]), in_=src), "w%d" % i,
                      reads=deps, writes=[b_wsl[i]], group=(n_ > 0))
            return wsl[i], b_wsl[i]

        def wview_k(t, ncol, kc=8):
            return t[:, 0:kc * ncol].rearrange("p (k n) -> p k n", n=ncol)

        def wsrc(wb, c0, ncol):
            return wb.rearrange("(k p) n -> p k n", p=128)[:, :, c0:c0 + ncol]

        pctr = [0]

        def nextbank():
            i = pctr[0] % 8
            pctr[0] += 1
            return i

        csil = sb("csil", [128, 8, NS], BF16); b_csil = Buf("csil")
        modT = sb("modT", [128, 48, NS]); b_modT = Buf("modT")
        modv = sb("modv", [128, NS, 2, 8]); b_modv = Buf("modv")
        for s_ in range(NS):
            P.op(ACT, lambda e, s_=s_: e.activation(csil[:, :, s_], vcols(("c", s_), 8), AF.Silu), reads=[b_vT[0]], writes=[b_csil])
        mb = nextbank()
        for g in range(12):
            wt, wbuf = wload([(lambda t: wview_k(t, 512), wsrc(wb_ada, g * 512, 512))], b_w["ada"])
            wv = wview_k(wt, 512)
            for j in range(4):
                jj = g * 4 + j
                for kc in range(8):
                    P.op(PE, lambda e, wv=wv, j=j, kc=kc, jj=jj: e.matmul(ps[:, mb * 512 + 4 * jj:mb * 512 + 4 * jj + NS],
                                                                         lhsT=wv[:, kc, j * 128:(j + 1) * 128], rhs=csil[:, kc, :],
                                                                         start=(kc == 0), stop=(kc == 7), skip_group_check=True),
                         reads=[wbuf, b_csil], writes=[bank[mb]])
        for s_ in range(NS):
            P.op(DVE, lambda e, s_=s_: e.tensor_tensor(modT[:, :, s_], ps[:, mb * 512:mb * 512 + 192].rearrange("p (j f) -> p j f", f=4)[:, :, s_],
                                                       vcols("b_ada", 48), ALU.add),
                 reads=[bank[mb], b_vT[0]], writes=[b_modT])
        for s_ in range(NS):
            for kind, sc0, gk in ((0, 8, "g1"), (1, 32, "g2")):
                P.op(DVE, lambda e, s_=s_, kind=kind, sc0=sc0, gk=gk: e.scalar_tensor_tensor(
                    modv[:, s_, kind, :], modT[:, sc0:sc0 + 8, s_], 1.0, vcols(gk, 8), ALU.add, ALU.mult),
                    reads=[b_modT, b_vT[0]], writes=[b_modv])

        def mcol(j, s_):
            return modT[:, j, s_:s_ + 1]

        xin = [sb("xin%d" % i, [128, 1024]) for i in range(2)]; b_xin = [Buf("xin%d" % i) for i in range(2)]
        stg = [sb("stg%d" % i, [128, 512]) for i in range(4)]; b_stg = [Buf("stg%d" % i) for i in range(4)]
        xTs = [sb("xT%d" % i, [128, 8, T]) for i in range(2)]; b_xTs = [Buf("xT%d" % i) for i in range(2)]
        hT = sb("hT", [128, 8, T], BF16); b_hT = Buf("hT")
        qz = [sb("qz%d" % h, [128, 8, T], BF16) for h in range(2)]; b_qz = Buf("qz")
        kTs = sb("kTs", [128, 8, T], BF16); b_kTs = Buf("kTs")
        obT = sb("obT", [128, 8, T], BF16); b_obT = Buf("obT")
        oT = sb("oT", [128, 8, T], BF16); b_oT = Buf("oT")
        mT = sb("mT", [128, 8, T], BF16); b_mT = Buf("mT")
        big = sb("big", [128, 16384], BF16)
        b_big = [Buf("big%d" % i) for i in range(4)]
        kTh = [big[:, 0:4096], big[:, 4096:8192]]
        vh = [big[:, 8192:12288].rearrange("p (k d) -> p k d", d=128), big[:, 12288:16384].rearrange("p (k d) -> p k d", d=128)]
        actT = big[:, 0:NFF * T].rearrange("p (f t) -> p f t", t=T)
        sqs = [sb("sq%d" % i, [128, T], BF16) for i in range(2)]; b_sqs = [Buf("sq%d" % i) for i in range(2)]
        rstd = sb("rstd", [128, T]); b_rstd = Buf("rstd")
        lnt = sb("lnt", [128, T]); b_lnt = Buf("lnt")
        tmpA = [sb("tmpA%d" % i, [128, T]) for i in range(2)]; b_tmpA = [Buf("tmpA%d" % i) for i in range(2)]
        tmpB = [sb("tmpB%d" % i, [128, T]) for i in range(2)]; b_tmpB = [Buf("tmpB%d" % i) for i in range(2)]
        cu = [sb("cu%d" % i, [128, T + 2]) for i in range(2)]; b_cu = [Buf("cu%d" % i) for i in range(2)]
        cuh = sb("cuh", [128, 8, 2]); b_cuh = Buf("cuh")
        uph = sb("uph", [128, NFF, 2]); b_uph = Buf("uph")
        Et = [sb("Et%d" % i, [128, 2, T]) for i in range(2)]; b_E = [Buf("E%d" % i) for i in range(2)]
        Lt = [sb("Lt%d" % i, [128, 2, T], BF16) for i in range(2)]; b_L = [Buf("L%d" % i) for i in range(2)]
        At = [sb("At%d" % i, [128, 2, T], BF16) for i in range(2)]; b_A = [Buf("A%d" % i) for i in range(2)]
        Acc = [sb("Acc%d" % i, [128, 2, T], BF16) for i in range(3)]; b_Acc = [Buf("Acc%d" % i) for i in range(3)]
        b_kscr = [Buf("kscr%d" % s_) for s_ in range(NS)]
        b_vscr = [Buf("vscr%d" % s_) for s_ in range(NS)]
        for h in range(2):
            P.op(POOL, lambda e, h=h: e.memset(qz[h][:], 0.0), writes=[b_qz])
        ctr = {"xin": 0, "stg": 0, "tA": 0, "tB": 0, "cu": 0, "att": 0}

        def rr(key, n):
            i = ctr[key] % n
            ctr[key] += 1
            return i

        def rms_to_hT(Tn, s_, kind, xT, b_xT):
            rms_stats(Tn, xT, b_xT)
            if kind is not None:
                rms_apply(Tn, s_, kind, xT, b_xT)

        def rms_stats(Tn, xT, b_xT):
            sb_i = nextbank()
            for m in range(8):
                qi = m % 2
                if m % 2 == 0:
                    P.op(ACT, lambda e, m=m, qi=qi: e.activation(sqs[qi][:, 0:Tn], xT[:, m, 0:Tn], AF.Square), reads=[b_xT], writes=[b_sqs[qi]])
                else:
                    P.op(DVE, lambda e, m=m, qi=qi: e.tensor_tensor(sqs[qi][:, 0:Tn], xT[:, m, 0:Tn], xT[:, m, 0:Tn], ALU.mult),
                         reads=[b_xT], writes=[b_sqs[qi]])
                P.op(PE, lambda e, m=m, qi=qi: e.matmul(bk(sb_i, Tn), lhsT=ones[:], rhs=sqs[qi][:, 0:Tn], start=(m == 0), stop=(m == 7)),
                     reads=[b_sqs[qi]] + b_const_all, writes=[bank[sb_i]])
            P.op(ACT, lambda e: e.activation(lnt[:, 0:Tn], bk(sb_i, Tn), AF.Ln, scale=1.0 / D, bias=eps_t[:, 0:1]),
                 reads=[bank[sb_i], b_eps], writes=[b_lnt])
            P.op(ACT, lambda e: e.activation(rstd[:, 0:Tn], lnt[:, 0:Tn], AF.Exp, scale=-0.5), reads=[b_lnt], writes=[b_rstd])

        def rms_apply(Tn, s_, kind, xT, b_xT):
            shift0 = 0 if kind == 0 else 24
            for m in range(8):
                i = rr("tA", 2)
                P.op(DVE, lambda e, m=m, i=i: e.scalar_tensor_tensor(tmpA[i][:, 0:Tn], xT[:, m, 0:Tn], modv[:, s_, kind, m:m + 1], rstd[:, 0:Tn],
                                                                     ALU.mult, ALU.mult),
                     reads=[b_xT, b_rstd, b_modv], writes=[b_tmpA[i]])
                if m % 2 == 0:
                    P.op(POOL, lambda e, m=m, i=i: e.tensor_scalar(hT[:, m, 0:Tn], tmpA[i][:, 0:Tn], 1.0, mcol(shift0 + m, s_), ALU.mult, ALU.add),
                         reads=[b_tmpA[i], b_modT], writes=[b_hT])
                else:
                    P.op(ACT, lambda e, m=m, i=i: e.activation(hT[:, m, 0:Tn], tmpA[i][:, 0:Tn], AF.Identity, bias=mcol(shift0 + m, s_), scale=1.0),
                         reads=[b_tmpA[i], b_modT], writes=[b_hT])

        def fm_group(wv, j, rhs3, Tn, kcn, bi, rbufs, wbuf):
            for kc in range(kcn):
                P.op(PE, lambda e, kc=kc: e.matmul(bk(bi, Tn), lhsT=wv[:, kc, j * 128:(j + 1) * 128], rhs=rhs3[:, kc, 0:Tn],
                                                   start=(kc == 0), stop=(kc == kcn - 1)),
                     reads=[wbuf] + rbufs, writes=[bank[bi]])

        def conv3(dst_fn, src, Tn, wkey, bkey, j, i_out):
            P.op(DVE, lambda e: e.tensor_scalar(tmpB[i_out][:, 0:Tn], src[:, 2:Tn + 2], vcol((wkey, 2), j), vcol(bkey, j), ALU.mult, ALU.add),
                 reads=dst_fn[0], writes=[b_tmpB[i_out]])
            for tap in (1, 0):
                P.op(DVE, lambda e, tap=tap: e.scalar_tensor_tensor(tmpB[i_out][:, 0:Tn], src[:, tap:Tn + tap], vcol((wkey, tap), j),
                                                                     tmpB[i_out][:, 0:Tn], ALU.mult, ALU.add),
                     reads=dst_fn[0] + [b_tmpB[i_out]], writes=[b_tmpB[i_out]])

        eps_t = sb("eps_t", [128, 1]); b_eps = Buf("eps")
        P.op(POOL, lambda e: e.memset(eps_t[:], EPS), writes=[b_eps])

        tilectr = [0]

        def tile_front(Tn, xsrc, s_):
            par = tilectr[0] % 2
            tilectr[0] += 1
            xT, b_xT = xTs[par], b_xTs[par]
            tb = min(128, Tn)
            nb = Tn // tb
            for b in range(nb):
                xi = rr("xin", 2)
                P.dma(SP, lambda e, b=b, xi=xi: e.dma_start(out=xin[xi][0:tb, :], in_=xsrc[b * tb:(b + 1) * tb, :]), "xin%d" % xi,
                      writes=[b_xin[xi]])
                for hh in range(2):
                    bi = nextbank()
                    for j in range(4):
                        m = hh * 4 + j
                        P.op(PE, lambda e, j=j, m=m, xi=xi, bi=bi: e.transpose(ps[:, bi * 512 + j * 128:bi * 512 + j * 128 + tb],
                                                                               xin[xi][0:tb, m * 128:(m + 1) * 128], ident[0:tb, 0:tb]),
                             reads=[b_xin[xi]] + b_const_all, writes=[bank[bi]])
                    P.op(DVE, lambda e, hh=hh, b=b, bi=bi: e.tensor_copy(xT[:, hh * 4:hh * 4 + 4, b * tb:(b + 1) * tb],
                                                                        bk(bi).rearrange("p (j t) -> p j t", t=128)[:, :, 0:tb]),
                         reads=[bank[bi]], writes=[b_xT])
            rms_stats(Tn, xT, b_xT)
            rms_apply(Tn, s_, 0, xT, b_xT)
            return xT, b_xT

        def process_tile(s_, Tn, xsrc, ysrc, koutsrc, voutsrc, kpos0, units_blocks, last, conv_out, ffn_out, prev_finish=None,
                         front=None, next_front_fn=None):
            pre = front is not None
            if front is None:
                qw = [wload([(lambda t: wview_k(t, 512), wsrc(wb_in, g * 512, 512))], b_w["in"]) for g in range(2)]
                front = tile_front(Tn, xsrc, s_)
            xT, b_xT = front
            tb = min(128, Tn)
            nb = Tn // tb
            b_vt = [b_vT[0], b_vT[1], b_vT[2]]
            if pre:
                qw = [wload([(lambda t: wview_k(t, 512), wsrc(wb_in, g * 512, 512))], b_w["in"]) for g in range(2)]
            if prev_finish is not None:
                prev_finish[0]()
            for g in range(2):
                wt, wbuf = qw[g]
                wv = wview_k(wt, 512)
                for j in range(4):
                    m = g * 4 + j
                    bi = nextbank()
                    fm_group(wv, j, hT, Tn, 8, bi, [b_hT], wbuf)
                    P.op(ACT, lambda e, m=m, bi=bi: e.activation(qz[0][0:64, m, 0:Tn], ps[0:64, bi * 512:bi * 512 + Tn], AF.Copy, scale=0.125),
                         reads=[bank[bi]], writes=[b_qz])
                    P.op(DVE, lambda e, m=m, bi=bi: e.tensor_scalar(qz[1][64:128, m, 0:Tn], ps[64:128, bi * 512:bi * 512 + Tn], 0.125, None, ALU.mult),
                         reads=[bank[bi]], writes=[b_qz])
            for g in range(2, 6):
                wt, wbuf = wload([(lambda t: wview_k(t, 512), wsrc(wb_in, g * 512, 512))], b_w["in"])
                wv = wview_k(wt, 512)
                isk = g < 4
                half = g % 2
                if isk:
                    for j in range(4):
                        m = half * 4 + j
                        bi = nextbank()
                        fm_group(wv, j, hT, Tn, 8, bi, [b_hT], wbuf)
                        P.op(ACT if j % 2 == 0 else DVE,
                             (lambda e, m=m, bi=bi: e.activation(kTs[:, m, 0:Tn], bk(bi, Tn), AF.Copy)) if j % 2 == 0 else
                             (lambda e, m=m, bi=bi: e.tensor_copy(kTs[:, m, 0:Tn], bk(bi, Tn))),
                             reads=[bank[bi]], writes=[b_kTs])
                for b in range(nb):
                    bi = nextbank()
                    for kc in range(8):
                        P.op(PE, lambda e, kc=kc, b=b, bi=bi, wv=wv: e.matmul(bk(bi, 512, tb), lhsT=hT[:, kc, b * tb:(b + 1) * tb], rhs=wv[:, kc, :],
                                                                             start=(kc == 0), stop=(kc == 7)),
                             reads=[wbuf, b_hT], writes=[bank[bi]])
                    si = rr("stg", 4)
                    P.op(DVE if b % 2 == 0 else ACT,
                         (lambda e, si=si, bi=bi: e.tensor_copy(stg[si][0:tb, :], bk(bi, 512, tb))) if b % 2 == 0 else
                         (lambda e, si=si, bi=bi: e.activation(stg[si][0:tb, :], bk(bi, 512, tb), AF.Copy)),
                         reads=[bank[bi]], writes=[b_stg[si]])
                    dst = koutsrc if isk else voutsrc
                    P.dma(POOL, lambda e, si=si, b=b, dst=dst, half=half: e.dma_start(out=dst[b * tb:(b + 1) * tb, half * 512:(half + 1) * 512],
                                                                                     in_=stg[si][0:tb, :]), "stg%d" % si, reads=[b_stg[si]])
                    if not isk:
                        P.dma(POOL, lambda e, si=si, b=b, half=half: e.dma_start(
                            out=vscr[s_, half * 4:half * 4 + 4, kpos0 + b * tb:kpos0 + (b + 1) * tb, :].rearrange("c t d -> t c d"),
                            in_=stg[si][0:tb, :].rearrange("t (c d) -> t c d", d=128)), "stgv%d" % si,
                            reads=[b_stg[si]], writes=[b_vscr[s_]])
                if isk and half == 1:
                    P.dma(POOL, lambda e: e.dma_start(out=kscr[s_, :, :, kpos0:kpos0 + Tn].rearrange("c p t -> p c t"), in_=kTs[:, :, 0:Tn]),
                          "kTs", reads=[b_kTs], writes=[b_kscr[s_]])
            if prev_finish is not None:
                prev_finish[1]()
            while late_hooks:
                late_hooks.pop(0)()
            FB = 7

            def conv_gen():
                for m in range(8):
                    wt, wbuf = wload([(lambda t: wview_k(t, 384), wsrc(wb_cv, m * 384, 384))], b_w["cv"])
                    wv = wt[:, 0:3072].rearrange("p (k n) -> p k n", n=384)
                    ia = rr("tA", 2)
                    ci = rr("cu", 2)
                    io = rr("tB", 2)

                    def grp(j):
                        for kc in range(8):
                            P.op(PE, lambda e, kc=kc, j=j, wv=wv: e.matmul(bk(FB, Tn), lhsT=wv[:, kc, j * 128:(j + 1) * 128], rhs=hT[:, kc, 0:Tn],
                                                                    start=(kc == 0), stop=(kc == 7)),
                                 reads=[wbuf, b_hT], writes=[bank[FB]])
                            if kc in (1, 3, 5):
                                yield
                    yield from grp(1)
                    P.op(DVE, lambda e, ia=ia: e.tensor_copy(tmpA[ia][:, 0:Tn], bk(FB, Tn)), reads=[bank[FB]], writes=[b_tmpA[ia]])
                    yield
                    yield from grp(2)
                    P.op(POOL, lambda e, m=m, ci=ci: e.tensor_copy(cu[ci][:, 0:2], cuh[:, m, :]), reads=[b_cuh], writes=[b_cu[ci]])
                    P.op(DVE, lambda e, ia=ia, ci=ci: e.tensor_tensor(cu[ci][:, 2:Tn + 2], tmpA[ia][:, 0:Tn], bk(FB, Tn), ALU.mult),
                         reads=[b_tmpA[ia], bank[FB]], writes=[b_cu[ci]])
                    P.op(POOL, lambda e, m=m, ci=ci: e.tensor_copy(cuh[:, m, :], cu[ci][:, Tn:Tn + 2]), reads=[b_cu[ci]], writes=[b_cuh])
                    conv3(([b_cu[ci]] + b_vt,), cu[ci], Tn, "wc", "bc", m, io)
                    yield
                    yield from grp(0)
                    P.op(DVE, lambda e, m=m, io=io: e.tensor_tensor(obT[:, m, 0:Tn], tmpB[io][:, 0:Tn], bk(FB, Tn), ALU.mult),
                         reads=[b_tmpB[io], bank[FB]], writes=[b_obT])
                    yield

            filler = conv_gen()
            attention(s_, Tn, units_blocks, filler)
            for _ in filler:
                pass
            if conv_out is not None:
                for r in range(2):
                    P.dma(POOL, lambda e, r=r: e.dma_start(out=conv_out[r, :].rearrange("(m p) -> p m", p=128), in_=cuh[:, :, r],
                                                           allow_slow_non_contiguous=True), "cvout", reads=[b_cuh], group=True)

            for m in range(8):
                wt, wbuf = wload([(lambda t: wview_k(t, 512), wsrc(wb_mg, m * 512, 512))], b_w["mg"])
                wfull = wview_k(wt, 512)
                wg = wfull[:, :, 0:256]
                wa = wfull[:, :, 256:384]
                wbb_ = wfull[:, :, 384:512]
                bga, bgb, bya, byb = nextbank(), nextbank(), nextbank(), nextbank()
                fm_group(wg, 0, hT, Tn, 8, bga, [b_hT], wbuf)
                fm_group(wg, 1, hT, Tn, 8, bgb, [b_hT], wbuf)
                fm_group(wa, 0, oT, Tn, 8, bya, [b_oT], wbuf)
                fm_group(wbb_, 0, obT, Tn, 8, byb, [b_obT], wbuf)
                i0 = rr("tA", 2); i1 = rr("tA", 2); j0 = rr("tB", 2); j1 = rr("tB", 2)
                P.op(ACT, lambda e, i0=i0, bga=bga: e.activation(tmpA[i0][:, 0:Tn], bk(bga, Tn), AF.Sigmoid), reads=[bank[bga]], writes=[b_tmpA[i0]])
                P.op(ACT, lambda e, i1=i1, bgb=bgb: e.activation(tmpA[i1][:, 0:Tn], bk(bgb, Tn), AF.Sigmoid), reads=[bank[bgb]], writes=[b_tmpA[i1]])
                P.op(DVE, lambda e, i0=i0, j0=j0, bya=bya: e.tensor_tensor(tmpB[j0][:, 0:Tn], tmpA[i0][:, 0:Tn], bk(bya, Tn), ALU.mult),
                     reads=[b_tmpA[i0], bank[bya]], writes=[b_tmpB[j0]])
                P.op(DVE, lambda e, i1=i1, j1=j1, byb=byb: e.tensor_tensor(tmpB[j1][:, 0:Tn], tmpA[i1][:, 0:Tn], bk(byb, Tn), ALU.mult),
                     reads=[b_tmpA[i1], bank[byb]], writes=[b_tmpB[j1]])
                P.op(POOL, lambda e, m=m, j0=j0, j1=j1: e.tensor_tensor(mT[:, m, 0:Tn], tmpB[j0][:, 0:Tn], tmpB[j1][:, 0:Tn], ALU.add),
                     reads=[b_tmpB[j0], b_tmpB[j1]], writes=[b_mT])
            for g in range(2):
                wt, wbuf = wload([(lambda t: wview_k(t, 512), wsrc(wb_out, g * 512, 512))], b_w["out"])
                wv = wview_k(wt, 512)
                for j in range(4):
                    m = g * 4 + j
                    bi = nextbank()
                    fm_group(wv, j, mT, Tn, 8, bi, [b_mT], wbuf)
                    P.op(DVE, lambda e, m=m, bi=bi: e.scalar_tensor_tensor(xT[:, m, 0:Tn], bk(bi, Tn), mcol(16 + m, s_), xT[:, m, 0:Tn], ALU.mult, ALU.add),
                         reads=[bank[bi], b_modT, b_xT], writes=[b_xT])
            rms_to_hT(Tn, s_, 1, xT, b_xT)
            for f in range(NFF):
                if f % 2 == 0:
                    upw = wload([(lambda t: wview_k(t, 512), wsrc(wb_up, f * 256, 512))], b_w["up"])
                wt, wbuf = upw
                wv = wview_k(wt, 512)[:, :, (f % 2) * 256:(f % 2) * 256 + 256]
                bu, bg = nextbank(), nextbank()
                fm_group(wv, 0, hT, Tn, 8, bu, [b_hT], wbuf)
                fm_group(wv, 1, hT, Tn, 8, bg, [b_hT], wbuf)
                ci = rr("cu", 2); io = rr("tB", 2); ia = rr("tA", 2)
                P.op(POOL, lambda e, f=f, ci=ci: e.tensor_copy(cu[ci][:, 0:2], uph[:, f, :]), reads=[b_uph], writes=[b_cu[ci]])
                P.op(ACT, lambda e, ci=ci, bu=bu: e.activation(cu[ci][:, 2:Tn + 2], bk(bu, Tn), AF.Copy), reads=[bank[bu]], writes=[b_cu[ci]])
                P.op(POOL, lambda e, f=f, ci=ci: e.tensor_copy(uph[:, f, :], cu[ci][:, Tn:Tn + 2]), reads=[b_cu[ci]], writes=[b_uph])
                conv3(([b_cu[ci]] + b_vt,), cu[ci], Tn, "wf", "bf", f, io)
                P.op(ACT, lambda e, io=io, ia=ia: e.activation(tmpA[ia][:, 0:Tn], tmpB[io][:, 0:Tn], AF.Silu), reads=[b_tmpB[io]], writes=[b_tmpA[ia]])
                P.op(DVE, lambda e, f=f, ia=ia, bg=bg: e.tensor_tensor(actT[:, f, 0:Tn], tmpA[ia][:, 0:Tn], bk(bg, Tn), ALU.mult),
                     reads=[b_tmpA[ia], bank[bg]], writes=b_big)
            if ffn_out is not None:
                for r in range(2):
                    P.dma(POOL, lambda e, r=r: e.dma_start(out=ffn_out[r, :].rearrange("(f p) -> p f", p=128), in_=uph[:, :, r],
                                                         allow_slow_non_contiguous=True), "ffout", reads=[b_uph], group=True)
            dnw = [wload([(lambda t: t[:, 0:NFF * 128].rearrange("p (k n) -> p k n", n=128),
                           wb_dn[m, :, :].rearrange("p (k n) -> p k n", n=128))], b_w["down"]) for m in range(2)]
            nfront = next_front_fn() if next_front_fn is not None else None
            for m in range(8):
                if m < 2:
                    wt, wbuf = dnw[m]
                else:
                    wt, wbuf = wload([(lambda t: t[:, 0:NFF * 128].rearrange("p (k n) -> p k n", n=128),
                                       wb_dn[m, :, :].rearrange("p (k n) -> p k n", n=128))], b_w["down"])
                wv = wt[:, 0:NFF * 128].rearrange("p (k n) -> p k n", n=128)
                bi = nextbank()
                fm_group(wv, 0, actT, Tn, NFF, bi, b_big, wbuf)
                P.op(DVE, lambda e, m=m, bi=bi: e.scalar_tensor_tensor(xT[:, m, 0:Tn], bk(bi, Tn), mcol(40 + m, s_), xT[:, m, 0:Tn], ALU.mult, ALU.add),
                     reads=[bank[bi], b_modT, b_xT], writes=[b_xT])
            def finish_a():
                rms_stats(Tn, xT, b_xT)
                for m in range(8):
                    P.op(DVE, lambda e, m=m: e.scalar_tensor_tensor(xT[:, m, 0:Tn], xT[:, m, 0:Tn], vcol("gf", m), rstd[:, 0:Tn], ALU.mult, ALU.mult),
                         reads=[b_xT, b_rstd, b_vT[0]], writes=[b_xT])

            def emit_J():
                for b in range(nb):
                    for hh in range(2):
                        bi = nextbank()
                        for j in range(4):
                            m = hh * 4 + j
                            P.op(PE, lambda e, j=j, m=m, b=b, bi=bi: e.transpose(ps[0:tb, bi * 512 + j * 128:bi * 512 + (j + 1) * 128],
                                                                                xT[:, m, b * tb:(b + 1) * tb], ident[:, :]),
                                 reads=[b_xT] + b_const_all, writes=[bank[bi]])
                        si = rr("stg", 4)
                        P.op(ACT if hh == 0 else DVE,
                             (lambda e, si=si, bi=bi: e.activation(stg[si][0:tb, :], bk(bi, 512, tb), AF.Copy)) if hh == 0 else
                             (lambda e, si=si, bi=bi: e.tensor_copy(stg[si][0:tb, :], bk(bi, 512, tb))),
                             reads=[bank[bi]], writes=[b_stg[si]])
                        P.dma(POOL, lambda e, si=si, b=b, hh=hh: e.dma_start(out=ysrc[b * tb:(b + 1) * tb, hh * 512:(hh + 1) * 512], in_=stg[si][0:tb, :]),
                              "stg%d" % si, reads=[b_stg[si]])

            return (finish_a, emit_J), nfront

        def attention(s_, Tn, blocks, filler=None):
            Tk = max(k0 + nk for (_, k0, nk, _, _) in blocks)
            units = []
            for c in range(8):
                for bi_, blk in enumerate(blocks):
                    units.append((c, bi_, blk))
            nblk = len(blocks)
            nu = len(units)
            ZB = [(0, 1), (2, 3)]
            AB = (4, 5)
            OB = 6
            slot_of = {}
            accslot = {}

            def load_c(c):
                sl = rr("att", 2)
                slot_of[c] = sl
                P.dma(SP, lambda e: e.dma_start(out=kTh[sl][:, 0:Tk], in_=kscr[s_, c, :, 0:Tk]), "kTh%d" % sl,
                      reads=[b_kscr[s_]], writes=[b_big[sl]])
                nfull = Tk // 128
                if nfull:
                    P.dma(SP, lambda e: e.dma_start(out=vh[sl][:, 0:nfull, :], in_=vscr[s_, c, 0:nfull * 128, :].rearrange("(k p) d -> p k d", p=128)),
                          "vh%d" % sl, reads=[b_vscr[s_]], writes=[b_big[2 + sl]])
                rem = Tk - nfull * 128
                if rem:
                    P.dma(SP, lambda e: e.dma_start(out=vh[sl][0:rem, nfull, :], in_=vscr[s_, c, nfull * 128:Tk, :]),
                          "vh%d" % sl, reads=[b_vscr[s_]], writes=[b_big[2 + sl]], group=bool(nfull))

            def zviews(bpair, nk, q0):
                return [ps[0:nk, bpair[h] * 512 + q0:bpair[h] * 512 + Tn] for h in range(2)]

            def st_z(u):
                c, bi_, (kbidx, k0, nk, q0, diag) = units[u]
                if bi_ == 0 and c == 0:
                    load_c(0)
                    load_c(1)
                sl = slot_of[c]
                zb = ZB[u % 2]
                for h in range(2):
                    P.op(PE, lambda e, h=h: e.matmul(ps[0:nk, zb[h] * 512 + q0:zb[h] * 512 + Tn], lhsT=kTh[sl][:, k0:k0 + nk],
                                                     rhs=qz[h][:, c, q0:Tn], start=True, stop=True),
                         reads=[b_big[sl], b_qz], writes=[bank[zb[h]]])

            def st_EL(u):
                c, bi_, (kbidx, k0, nk, q0, diag) = units[u]
                zb = ZB[u % 2]
                i = u % 2
                zin = ps[0:nk, zb[0] * 512:zb[0] * 512 + 1024].rearrange("p (h t) -> p h t", t=512)[:, :, q0:Tn]
                P.op(ACT, lambda e: e.activation(Et[i][0:nk, :, q0:Tn], zin, AF.Exp), reads=[bank[zb[0]], bank[zb[1]]], writes=[b_E[i]])
                P.op(ACT, lambda e: e.activation(Lt[i][0:nk, :, q0:Tn], Et[i][0:nk, :, q0:Tn], AF.Ln, bias=one_t[0:nk, 0:1]),
                     reads=[b_E[i], b_eps], writes=[b_L[i]])
                if diag:
                    for h in range(2):
                        P.op(POOL, lambda e, h=h: e.tensor_tensor(Lt[i][0:nk, h, q0:q0 + nk], Lt[i][0:nk, h, q0:q0 + nk], m01[0:nk, 0:nk], ALU.mult),
                             reads=[b_L[i]] + b_const_all, writes=[b_L[i]])
                if bi_ == 0:
                    accslot[(c, 0)] = None
                if bi_ < nblk - 1:
                    a_new = rr_acc()
                    if bi_ == 0:
                        P.op(POOL, lambda e: e.memset(Acc[a_new][:], 0.0), writes=[b_Acc[a_new]])
                        P.op(POOL, lambda e: e.tensor_copy(Acc[a_new][0:nk, :, q0:Tn], Lt[i][0:nk, :, q0:Tn]), reads=[b_L[i]], writes=[b_Acc[a_new]])
                    else:
                        a_old = accslot[(c, bi_)]
                        if q0 > 0:
                            P.op(POOL, lambda e: e.memset(Acc[a_new][:, :, 0:q0], 0.0), writes=[b_Acc[a_new]])
                        P.op(DVE, lambda e: e.tensor_tensor(Acc[a_new][:, :, q0:Tn], Acc[a_old][:, :, q0:Tn], Lt[i][:, :, q0:Tn], ALU.add),
                             reads=[b_Acc[a_old], b_L[i]], writes=[b_Acc[a_new]])
                    accslot[(c, bi_ + 1)] = a_new

            accc = [0]

            def rr_acc():
                accc[0] += 1
                return accc[0] % 3

            def st_arg(u):
                c, bi_, (kbidx, k0, nk, q0, diag) = units[u]
                sl = slot_of[c]
                i = u % 2
                a_in = accslot[(c, bi_)]
                for h in range(2):
                    out = ps[0:nk, AB[h] * 512 + q0:AB[h] * 512 + Tn]
                    P.op(PE, lambda e, h=h, out=out: e.matmul(out, lhsT=kTh[sl][:, k0:k0 + nk], rhs=qz[h][:, c, q0:Tn], start=True, stop=False),
                         reads=[b_big[sl], b_qz], writes=[bank[AB[h]]])
                    P.op(PE, lambda e, h=h, out=out: e.matmul(out, lhsT=ntri[0:nk, 0:nk], rhs=Lt[i][0:nk, h, q0:Tn], start=False, stop=(a_in is None)),
                         reads=[b_L[i]] + b_const_all, writes=[bank[AB[h]]])
                    if a_in is not None:
                        P.op(PE, lambda e, h=h, out=out: e.matmul(out, lhsT=nones[:, 0:nk], rhs=Acc[a_in][:, h, q0:Tn], start=False, stop=True),
                             reads=[b_Acc[a_in]] + b_const_all, writes=[bank[AB[h]]])

            def st_a(u):
                c, bi_, (kbidx, k0, nk, q0, diag) = units[u]
                i = u % 2
                ain = ps[0:nk, AB[0] * 512:AB[0] * 512 + 1024].rearrange("p (h t) -> p h t", t=512)[:, :, q0:Tn]
                P.op(ACT, lambda e: e.activation(At[i][0:nk, :, q0:Tn], ain, AF.Exp), reads=[bank[AB[0]], bank[AB[1]]], writes=[b_A[i]])
                if diag:
                    for h in range(2):
                        P.op(POOL, lambda e, h=h: e.tensor_tensor(At[i][0:nk, h, q0:q0 + nk], At[i][0:nk, h, q0:q0 + nk], m01[0:nk, 0:nk], ALU.mult),
                             reads=[b_A[i]] + b_const_all, writes=[b_A[i]])

            def st_av(u):
                c, bi_, (kbidx, k0, nk, q0, diag) = units[u]
                sl = slot_of[c]
                i = u % 2
                if bi_ == 0:
                    P.op(PE, lambda e: e.matmul(ps[:, OB * 512:OB * 512 + Tn], lhsT=ones[:, :], rhs=zer[:, 0:Tn],
                                                start=True, stop=False, skip_group_check=True),
                         reads=b_const_all, writes=[bank[OB]])
                for h in range(2):
                    P.op(PE, lambda e, h=h: e.matmul(ps[64 * h:64 * h + 64, OB * 512 + q0:OB * 512 + Tn], lhsT=vh[sl][0:nk, kbidx, 64 * h:64 * h + 64],
                                                     rhs=At[i][0:nk, h, q0:Tn], start=False, stop=(bi_ == nblk - 1 and h == 1), skip_group_check=True),
                         reads=[b_big[2 + sl], b_A[i]], writes=[bank[OB]])
                if bi_ == nblk - 1 and c + 2 < 8:
                    load_c(c + 2)
                if bi_ == nblk - 1:
                    P.op(DVE, lambda e: e.tensor_copy(oT[:, c, 0:Tn], ps[:, OB * 512:OB * 512 + Tn]), reads=[bank[OB]], writes=[b_oT])

            for s in range(-2, nu + 1):
                if 0 <= s < nu:
                    st_arg(s)
                if filler is not None and s >= 0:
                    next(filler, None)
                if 0 <= s + 2 < nu:
                    st_z(s + 2)
                if 0 <= s - 1 < nu:
                    st_av(s - 1)
                if 0 <= s + 1 < nu:
                    st_EL(s + 1)
                if 0 <= s < nu:
                    st_a(s)

        one_t = sb("one_t", [128, 1])
        P.op(POOL, lambda e: e.memset(one_t[:], 1.0), writes=[b_eps])

        for kb in range(PAST // 128):
            xi = rr("xin", 2)
            P.dma(SP, lambda e, kb=kb, xi=xi: e.dma_start(out=xin[xi][:, :], in_=ck[kb * 128:(kb + 1) * 128, :]), "xin%d" % xi, writes=[b_xin[xi]])
            for hh in range(2):
                bi = nextbank()
                for j in range(4):
                    m = hh * 4 + j
                    P.op(PE, lambda e, j=j, m=m, xi=xi, bi=bi: e.transpose(ps[:, bi * 512 + j * 128:bi * 512 + (j + 1) * 128],
                                                                           xin[xi][:, m * 128:(m + 1) * 128], ident[:, :]),
                         reads=[b_xin[xi]] + b_const_all, writes=[bank[bi]])
                P.op(DVE, lambda e, hh=hh, kb=kb, bi=bi: e.tensor_copy(kTs[:, hh * 4:hh * 4 + 4, (kb % 4) * 128:(kb % 4 + 1) * 128],
                                                                      bk(bi).rearrange("p (j t) -> p j t", t=128)),
                     reads=[bank[bi]], writes=[b_kTs])
            if kb % 4 == 3:
                k0 = (kb - 3) * 128
                P.dma(SP, lambda e, k0=k0: e.dma_start(out=kscr[NP, :, :, k0:k0 + 512].rearrange("c p t -> p c t"), in_=kTs[:, :, :]),
                      "kTs", reads=[b_kTs], writes=[b_kscr[NP]])
        def cvcast_late():
            for kb in range(PAST // 128):
                P.dma(POOL, lambda e, kb=kb: e.dma_start(out=vscr[NP, :, kb * 128:(kb + 1) * 128, :].rearrange("c t d -> t c d"),
                                                         in_=cv[kb * 128:(kb + 1) * 128, :].rearrange("t (c d) -> t c d", d=128)),
                      "cvcast", writes=[b_vscr[NP]], group=True)
        late_hooks.append(cvcast_late)

        pend = [None]
        tiles = [(s_, i) for s_ in range(NP) for i in range(NT)]
        nfr = None
        for ti, (s_, i) in enumerate(tiles):
            if i == 0:
                P.op(POOL, lambda e: e.memset(cuh[:], 0.0), writes=[b_cuh])
                P.op(POOL, lambda e: e.memset(uph[:], 0.0), writes=[b_uph])
            r0 = s_ * SEQ + i * T
            blocks = []
            for kb in range(4 * i + 3, -1, -1):
                j = kb - 4 * i
                blocks.append((kb, kb * 128, 128, 128 * j if j >= 0 else 0, j >= 0))
            last = (i == NT - 1)
            nff = None
            if ti + 1 < len(tiles):
                s2, i2 = tiles[ti + 1]
                r2 = s2 * SEQ + i2 * T
                nff = (lambda r2=r2, s2=s2: tile_front(T, xp[r2:r2 + T, :], s2))
            pend[0], nfr = process_tile(s_, T, xp[r0:r0 + T, :], yp[r0:r0 + T, :], nkp[r0:r0 + T, :], nvp[r0:r0 + T, :], i * T, blocks, last,
                                        ncp[2 * s_:2 * s_ + 2, :] if last else None, nfp[2 * s_:2 * s_ + 2, :] if last else None, pend[0],
                                        front=nfr, next_front_fn=nff)

        s_ = NP
        for r in range(2):
            P.op(POOL, lambda e, r=r: e.tensor_copy(cuh[:, :, r], vcols(("sc", r), 8)), reads=[b_vT[1]], writes=[b_cuh])
            P.op(POOL, lambda e, r=r: e.tensor_copy(uph[:, :, r], vcols(("sf", r), NFF)), reads=[b_vT[2]], writes=[b_uph])
        blocks = [(PAST // 128, PAST, SAMP, 0, True)] + [(kb, kb * 128, 128, 0, False) for kb in range(PAST // 128 - 1, -1, -1)]
        fin, _ = process_tile(s_, SAMP, xs, ys, nks, nvs, PAST, blocks, True, ncs, nfs, pend[0])
        fin[0]()
        fin[1]()

        P.emit(nc)
    return nc


_NC_CACHE = {}


def _prep_maps(inp, NP, SEQ, ncores):
    f = lambda a: np.ascontiguousarray(a, dtype=np.float32)
    shared = {
        "w_ada": f(inp["w_ada"][0]), "b_ada": f(inp["b_ada"][0]), "g1": f(inp["g_norm1"][0]), "w_in": f(inp["w_in"][0]),
        "w_conv": f(inp["w_conv"][0]), "b_conv": f(inp["b_conv"][0]), "w_ba": f(inp["w_branch_a"][0]),
        "w_bb": f(inp["w_branch_b"][0]), "w_out": f(inp["w_out"][0]), "g2": f(inp["g_norm2"][0]), "w_up": f(inp["w_up"][0]),
        "w_fconv": f(inp["w_fconv"][0]), "b_fconv": f(inp["b_fconv"][0]), "w_down": f(inp["w_down"][0]), "gfin": f(inp["g_final"]),
    }
    maps = []
    for c in range(ncores):
        m = dict(shared)
        m["xp"] = f(inp["x_prompt"][c * NP:(c + 1) * NP].reshape(NP * SEQ, D))
        m["xs"] = f(inp["x_sample"][c])
        m["ck"] = f(inp["cache_k"][0, c].reshape(-1, D))
        m["cv"] = f(inp["cache_v"][0, c].reshape(-1, D))
        m["sc"] = f(inp["state_conv"][0, c])
        m["sf"] = f(inp["state_ffn_conv"][0, c])
        m["cvec"] = f(np.concatenate([inp["c_prompt"][c * NP:(c + 1) * NP], inp["c_sample"][c:c + 1]], axis=0))
        maps.append(m)
    return maps


def run(inp, NP, SEQ, ncores, PAST=1024, SAMP=32):
    key = (SEQ, NP, PAST, SAMP)
    if key not in _NC_CACHE:
        _NC_CACHE[key] = build(SEQ, NP, PAST, SAMP)
    nc = _NC_CACHE[key]
    maps = _prep_maps(inp, NP, SEQ, ncores)
    res = run_bass_kernel_spmd(nc, maps, core_ids=list(range(ncores)))
    rs = res.results
    cat = lambda k: np.concatenate([r[k] for r in rs], axis=0)
    B = NP * ncores
    y_p = cat("yp").reshape(B, SEQ, D)
    y_s = cat("ys").reshape(ncores, SAMP, D)
    nk_p = cat("nkp").reshape(1, B, SEQ, NH, 64)
    nv_p = cat("nvp").reshape(1, B, SEQ, NH, 64)
    nc_p = cat("ncp").reshape(1, B, 2, D)
    nf_p = cat("nfp").reshape(1, B, 2, DFF)
    nk_s = cat("nks").reshape(1, ncores, SAMP, NH, 64)
    nv_s = cat("nvs").reshape(1, ncores, SAMP, NH, 64)
    nc_s = cat("ncs").reshape(1, ncores, 2, D)
    nf_s = cat("nfs").reshape(1, ncores, 2, DFF)
    return tuple(np.ascontiguousarray(a, dtype=np.float32) for a in (y_p, y_s, nk_p, nv_p, nc_p, nf_p, nk_s, nv_s, nc_s, nf_s))


def kernel(**inputs):
    return run(inputs, NP=2, SEQ=4096, ncores=8)
```

```python
import numpy as np
from contextlib import ExitStack
import concourse.bass as bass
import concourse.mybir as mybir
from concourse.bass_utils import run_bass_kernel_spmd

F32 = mybir.dt.float32
BF16 = mybir.dt.bfloat16
AF = mybir.ActivationFunctionType
ALU = mybir.AluOpType

PE, ACT, DVE, POOL, SP = "pe", "act", "dve", "pool", "sp"
ENGS = [PE, ACT, DVE, POOL, SP]

D = 1024
DFF = 2816
NFF = 22
NH = 16
EPS = 1e-6


class Buf:
    def __init__(self, name):
        self.name = name
        self.last_w = None
        self.readers = []


class Op:
    __slots__ = ("eng", "fn", "deps", "signal", "dma_key", "dma_cnt", "is_dma")

    def __init__(self, eng, fn):
        self.eng = eng
        self.fn = fn
        self.deps = []
        self.signal = 0
        self.dma_key = None
        self.dma_cnt = 0
        self.is_dma = False


class Prog:
    def __init__(self):
        self.ops = {e: [] for e in ENGS}
        self.dma_keys = {}

    def _add_deps(self, op, reads, writes):
        deps = []
        for b in reads:
            if b.last_w is not None:
                deps.append(("raw", b.last_w))
        for b in writes:
            if b.last_w is not None:
                deps.append(("waw", b.last_w))
            for r in b.readers:
                deps.append(("war", r))
        for kind, d in deps:
            if d is op:
                continue
            if d.eng == op.eng and not d.is_dma and not op.is_dma:
                if op.eng == PE:
                    continue
            op.deps.append(d)
        for b in reads:
            b.readers.append(op)
        for b in writes:
            b.last_w = op
            b.readers = []

    def op(self, eng, fn, reads=(), writes=()):
        o = Op(eng, fn)
        self._add_deps(o, reads, writes)
        self.ops[eng].append(o)
        return o

    def dma(self, eng, fn, key, reads=(), writes=(), group=False):
        o = Op(eng, fn)
        o.is_dma = True
        st = self.dma_keys.setdefault(key, [0, None])
        if st[1] is not None and not group:
            o.deps.append(st[1])
        self._add_deps(o, reads, writes)
        st[0] += 16
        st[1] = o
        o.dma_key = key
        o.dma_cnt = st[0]
        self.ops[eng].append(o)
        return o

    def emit(self, nc):
        needed = set()
        for e in ENGS:
            for o in self.ops[e]:
                for d in o.deps:
                    if not d.is_dma:
                        needed.add(id(d))
        for e in ENGS:
            c = 0
            for o in self.ops[e]:
                if not o.is_dma and id(o) in needed:
                    c += 1
                    o.signal = c
        with ExitStack() as st:
            esem = {e: st.enter_context(nc.semaphore("s_" + e)) for e in ENGS}
            dsem = {k: st.enter_context(nc.semaphore("d_%d" % i)) for i, k in enumerate(self.dma_keys)}
            block = st.enter_context(nc.Block())
            prog = self

            def run(eng_name, eng):
                known = {}
                for o in prog.ops[eng_name]:
                    w = {}
                    for d in o.deps:
                        if d.is_dma:
                            k, v = ("d", d.dma_key), d.dma_cnt
                        else:
                            k, v = ("e", d.eng), d.signal
                        if v > w.get(k, 0):
                            w[k] = v
                    for k, v in w.items():
                        if known.get(k, 0) >= v:
                            continue
                        known[k] = v
                        eng.wait_ge(dsem[k[1]] if k[0] == "d" else esem[k[1]], v)
                    ins = o.fn(eng)
                    if o.is_dma:
                        ins.then_inc(dsem[o.dma_key], 16)
                    elif o.signal:
                        ins.then_inc(esem[eng_name], 1)
                if eng_name == SP:
                    for k, stt in prog.dma_keys.items():
                        if known.get(("d", k), 0) < stt[0]:
                            eng.wait_ge(dsem[k], stt[0])

            @block.tensor
            def _(e):
                run(PE, e)

            @block.scalar
            def _(e):
                run(ACT, e)

            @block.vector
            def _(e):
                run(DVE, e)

            @block.gpsimd
            def _(e):
                run(POOL, e)

            @block.sync
            def _(e):
                run(SP, e)


def build(SEQ=4096, NP=2, PAST=1024, SAMP=32):
    nc = bass.Bass("TRN2", target_bir_lowering=False)
    T = 512
    NT = SEQ // T
    TKMAX = max(SEQ, PAST + SAMP)
    NS = NP + 1
    P = Prog()

    def din(name, shape):
        return nc.dram_tensor(name, list(shape), F32, kind="ExternalInput").ap()

    def dout(name, shape):
        return nc.dram_tensor(name, list(shape), F32, kind="ExternalOutput").ap()

    xp = din("xp", [NP * SEQ, D]); xs = din("xs", [SAMP, D])
    ck = din("ck", [PAST, D]); cv = din("cv", [PAST, D])
    sc = din("sc", [2, D]); sf = din("sf", [2, DFF]); cvec = din("cvec", [NS, D])
    w_ada = din("w_ada", [D, 6 * D]); b_ada = din("b_ada", [6 * D]); g1 = din("g1", [D])
    w_in = din("w_in", [D, 8 * D]); w_conv = din("w_conv", [3, D]); b_conv = din("b_conv", [D])
    w_ba = din("w_ba", [D, D]); w_bb = din("w_bb", [D, D]); w_out = din("w_out", [D, D])
    g2 = din("g2", [D]); w_up = din("w_up", [D, 2 * DFF]); w_fconv = din("w_fconv", [3, DFF])
    b_fconv = din("b_fconv", [DFF]); w_down = din("w_down", [DFF, D]); gfin = din("gfin", [D])
    yp = dout("yp", [NP * SEQ, D]); ys = dout("ys", [SAMP, D])
    nkp = dout("nkp", [NP * SEQ, D]); nvp = dout("nvp", [NP * SEQ, D])
    ncp = dout("ncp", [NP * 2, D]); nfp = dout("nfp", [NP * 2, DFF])
    nks = dout("nks", [SAMP, D]); nvs = dout("nvs", [SAMP, D])
    ncs = dout("ncs", [2, D]); nfs = dout("nfs", [2, DFF])

    def dscr(name, shape):
        return nc.dram_tensor(name, list(shape), BF16).ap()

    wb_ada = dscr("wb_ada", [D, 6 * D]); wb_in = dscr("wb_in", [D, 3 * D])
    wb_cv = dscr("wb_cv", [D, 8 * 3 * 128])
    wb_mg = dscr("wb_mg", [D, 8 * 4 * 128])
    wb_out = dscr("wb_out", [D, D])
    wb_up = dscr("wb_up", [D, 2 * DFF])
    wb_dn = dscr("wb_dn", [8, 128, NFF * 128])
    kscr = dscr("kscr", [NS, 8, 128, TKMAX]); vscr = dscr("vscr", [NS, 8, TKMAX, 128])

    with ExitStack() as st:
        def sb(name, shape, dt=F32):
            return st.enter_context(nc.sbuf_tensor(name, list(shape), dt))

        ps = st.enter_context(nc.psum_tensor("ps", [128, 4096], F32))
        bank = [Buf("bank%d" % i) for i in range(8)]

        def bk(i, n=512, p=128):
            return ps[0:p, i * 512:i * 512 + n]

        ident = sb("ident", [128, 128]); tmpc = sb("tmpc", [128, 128])
        ntri = sb("ntri", [128, 128], BF16); nones = sb("nones", [128, 128], BF16)
        ones = sb("ones", [128, 128], BF16); m01 = sb("m01", [128, 128], BF16)
        b_const = Buf("const")
        P.op(POOL, lambda e: e.memset(ident[:], 1.0), writes=[b_const])
        P.op(POOL, lambda e: e.affine_select(out=ident[:], in_=ident[:], pattern=[[-1, 128]], compare_op=ALU.is_equal,
                                            fill=0.0, base=0, channel_multiplier=1), reads=[b_const], writes=[b_const])
        P.op(POOL, lambda e: e.memset(tmpc[:], -1.0), writes=[b_const])
        P.op(POOL, lambda e: e.affine_select(out=tmpc[:], in_=tmpc[:], pattern=[[-1, 128]], compare_op=ALU.is_ge,
                                            fill=0.0, base=0, channel_multiplier=1), reads=[b_const], writes=[b_const])
        P.op(DVE, lambda e: e.tensor_copy(ntri[:], tmpc[:]), reads=[b_const], writes=[b_const])
        b_c2 = Buf("const2")
        tmpd = sb("tmpd", [128, 128])
        P.op(POOL, lambda e: e.memset(tmpd[:], 1.0), writes=[b_c2])
        P.op(POOL, lambda e: e.affine_select(out=tmpd[:], in_=tmpd[:], pattern=[[1, 128]], compare_op=ALU.is_gt,
                                            fill=0.0, base=0, channel_multiplier=-1), reads=[b_c2], writes=[b_c2])
        P.op(DVE, lambda e: e.tensor_copy(m01[:], tmpd[:]), reads=[b_c2], writes=[b_c2])
        zer = sb("zer", [128, 512], BF16)
        P.op(POOL, lambda e: e.memset(zer[:], 0.0), writes=[b_c2])
        P.op(POOL, lambda e: e.memset(nones[:], -1.0), writes=[b_c2])
        P.op(POOL, lambda e: e.memset(ones[:], 1.0), writes=[b_c2])
        b_const_all = [b_const, b_c2]

        b_w = {n: [] for n in ("ada", "in", "cv", "mg", "out", "up", "down")}

        def cast(name, dst, src):
            bw = Buf("w_%s_%d" % (name, len(b_w[name])))
            b_w[name].append(bw)
            P.dma(POOL, lambda e: e.dma_start(out=dst, in_=src), "wcast_" + name, writes=[bw], group=True)

        for r0 in range(0, D, 128):
            cast("ada", wb_ada[r0:r0 + 128, :], w_ada[r0:r0 + 128, :])
        for r0 in range(0, D, 128):
            cast("in", wb_in[r0:r0 + 128, :], w_in[r0:r0 + 128, 0:3 * D])
        for r0 in range(0, D, 128):
            for gi in range(3):
                cast("cv", wb_cv[r0:r0 + 128, :].rearrange("p (m g n) -> p m g n", g=3, n=128)[:, :, gi, :],
                     w_in[r0:r0 + 128, (3 + gi) * D:(4 + gi) * D].rearrange("p (m n) -> p m n", n=128))
        for r0 in range(0, D, 128):
            for gi, srcw in enumerate((w_in[:, 6 * D:7 * D], w_in[:, 7 * D:8 * D], w_ba, w_bb)):
                cast("mg", wb_mg[r0:r0 + 128, :].rearrange("p (m g n) -> p m g n", g=4, n=128)[:, :, gi, :],
                     srcw[r0:r0 + 128, :].rearrange("p (m n) -> p m n", n=128))
        for r0 in range(0, D, 128):
            cast("out", wb_out[r0:r0 + 128, :], w_out[r0:r0 + 128, :])
        for r0 in range(0, D, 128):
            for gi in range(2):
                cast("up", wb_up[r0:r0 + 128, :].rearrange("p (f g n) -> p f g n", g=2, n=128)[:, :, gi, :],
                     w_up[r0:r0 + 128, gi * DFF:(gi + 1) * DFF].rearrange("p (f n) -> p f n", n=128))
        for m in range(8):
            for f0 in range(0, NFF, 11):
                cast("down", wb_dn[m, :, f0 * 128:(f0 + 11) * 128].rearrange("p (f n) -> p f n", n=128),
                     w_down[f0 * 128:(f0 + 11) * 128, m * 128:(m + 1) * 128].rearrange("(f p) n -> p f n", p=128))

        vr = [sb("vr%d" % i, [128, 128]) for i in range(3)]
        vT = [sb("vT%d" % i, [128, 128]) for i in range(3)]
        b_vr = [Buf("vr%d" % i) for i in range(3)]
        b_vT = [Buf("vT%d" % i) for i in range(3)]
        R = {}
        rowp = [0, 0, 0]

        def vload(key, ti, src2d, nrows):
            r0 = rowp[ti]
            R[key] = (ti, r0)
            P.dma(SP, lambda e: e.dma_start(out=vr[ti][r0:r0 + nrows, :], in_=src2d), "vload%d" % ti, writes=[b_vr[ti]], group=True)
            rowp[ti] += nrows

        def v2(ap1d, n):
            return ap1d.rearrange("(a b) -> a b", b=128)

        for s_ in range(NS):
            vload(("c", s_), 0, v2(cvec[s_, :], 8), 8)
        vload("b_ada", 0, v2(b_ada, 48), 48)
        vload("g1", 0, v2(g1, 8), 8); vload("g2", 0, v2(g2, 8), 8); vload("gf", 0, v2(gfin, 8), 8)
        for i in range(3):
            vload(("wc", i), 0, v2(w_conv[i, :], 8), 8)
        vload("bc", 0, v2(b_conv, 8), 8)
        for i in range(3):
            vload(("wf", i), 1, v2(w_fconv[i, :], NFF), NFF)
        vload("bf", 1, v2(b_fconv, NFF), NFF)
        for i in range(2):
            vload(("sc", i), 1, v2(sc[i, :], 8), 8)
        for i in range(2):
            vload(("sf", i), 2, v2(sf[i, :], NFF), NFF)
        for ti in range(3):
            n = rowp[ti]
            P.op(PE, lambda e, ti=ti, n=n: e.transpose(bk(ti, n), vr[ti][0:n, :], ident[0:n, 0:n]),
                 reads=[b_vr[ti]] + b_const_all, writes=[bank[ti]])
            P.op(DVE, lambda e, ti=ti, n=n: e.tensor_copy(vT[ti][:, 0:n], bk(ti, n)), reads=[bank[ti]], writes=[b_vT[ti]])

        def vcol(key, j=0):
            ti, r0 = R[key]
            return vT[ti][:, r0 + j:r0 + j + 1]

        def vcols(key, n):
            ti, r0 = R[key]
            return vT[ti][:, r0:r0 + n]

        NW = 3
        wsl = [sb("wsl%d" % i, [128, 4096], BF16) for i in range(NW)]
        b_wsl = [Buf("wsl%d" % i) for i in range(NW)]
        wctr = [0]

        def wload(pieces, deps):
            i = wctr[0] % NW
            wctr[0] += 1
            if len(pieces) == 1:
                dv0, src0 = pieces[0]
                kh = src0.shape[1] // 2
                pieces = [((lambda t, dv0=dv0: dv0(t)[:, 0:kh, :]), src0[:, 0:kh, :]),
                          ((lambda t, dv0=dv0: dv0(t)[:, kh:, :]), src0[:, kh:, :])]
            for n_, (dv, src) in enumerate(pieces):
                P.dma(SP, lambda e, dv=dv, src=src, i=i: e.dma_start(out=dv(wsl[i]), in_=src), "w%d" % i,
                      reads=deps, writes=[b_wsl[i]], group=(n_ > 0))
            return wsl[i], b_wsl[i]

        def wview_k(t, ncol, kc=8):
            return t[:, 0:kc * ncol].rearrange("p (k n) -> p k n", n=ncol)

        def wsrc(wb, c0, ncol):
            return wb.rearrange("(k p) n -> p k n", p=128)[:, :, c0:c0 + ncol]

        pctr = [0]

        def nextbank():
            i = pctr[0] % 8
            pctr[0] += 1
            return i

        csil = sb("csil", [128, 8, NS], BF16); b_csil = Buf("csil")
        modT = sb("modT", [128, 48, NS]); b_modT = Buf("modT")
        modv = sb("modv", [128, NS, 2, 8]); b_modv = Buf("modv")
        for s_ in range(NS):
            P.op(ACT, lambda e, s_=s_: e.activation(csil[:, :, s_], vcols(("c", s_), 8), AF.Silu), reads=[b_vT[0]], writes=[b_csil])
        mb = nextbank()
        for g in range(12):
            wt, wbuf = wload([(lambda t: wview_k(t, 512), wsrc(wb_ada, g * 512, 512))], b_w["ada"])
            wv = wview_k(wt, 512)
            for j in range(4):
                jj = g * 4 + j
                for kc in range(8):
                    P.op(PE, lambda e, wv=wv, j=j, kc=kc, jj=jj: e.matmul(ps[:, mb * 512 + 4 * jj:mb * 512 + 4 * jj + NS],
                                                                         lhsT=wv[:, kc, j * 128:(j + 1) * 128], rhs=csil[:, kc, :],
                                                                         start=(kc == 0), stop=(kc == 7), skip_group_check=True),
                         reads=[wbuf, b_csil], writes=[bank[mb]])
        for s_ in range(NS):
            P.op(DVE, lambda e, s_=s_: e.tensor_tensor(modT[:, :, s_], ps[:, mb * 512:mb * 512 + 192].rearrange("p (j f) -> p j f", f=4)[:, :, s_],
                                                       vcols("b_ada", 48), ALU.add),
                 reads=[bank[mb], b_vT[0]], writes=[b_modT])
        for s_ in range(NS):
            for kind, sc0, gk in ((0, 8, "g1"), (1, 32, "g2")):
                P.op(DVE, lambda e, s_=s_, kind=kind, sc0=sc0, gk=gk: e.scalar_tensor_tensor(
                    modv[:, s_, kind, :], modT[:, sc0:sc0 + 8, s_], 1.0, vcols(gk, 8), ALU.add, ALU.mult),
                    reads=[b_modT, b_vT[0]], writes=[b_modv])

        def mcol(j, s_):
            return modT[:, j, s_:s_ + 1]

        xin = [sb("xin%d" % i, [128, 1024]) for i in range(2)]; b_xin = [Buf("xin%d" % i) for i in range(2)]
        stg = [sb("stg%d" % i, [128, 512]) for i in range(4)]; b_stg = [Buf("stg%d" % i) for i in range(4)]
        xTs = [sb("xT%d" % i, [128, 8, T]) for i in range(2)]; b_xTs = [Buf("xT%d" % i) for i in range(2)]
        hT = sb("hT", [128, 8, T], BF16); b_hT = Buf("hT")
        qz = [sb("qz%d" % h, [128, 8, T], BF16) for h in range(2)]; b_qz = Buf("qz")
        kTs = sb("kTs", [128, 8, T], BF16); b_kTs = Buf("kTs")
        obT = sb("obT", [128, 8, T], BF16); b_obT = Buf("obT")
        oT = sb("oT", [128, 8, T], BF16); b_oT = Buf("oT")
        mT = sb("mT", [128, 8, T], BF16); b_mT = Buf("mT")
        big = sb("big", [128, 16384], BF16)
        b_big = [Buf("big%d" % i) for i in range(4)]
        kTh = [big[:, 0:4096], big[:, 4096:8192]]
        vh = [big[:, 8192:12288].rearrange("p (k d) -> p k d", d=128), big[:, 12288:16384].rearrange("p (k d) -> p k d", d=128)]
        actT = big[:, 0:NFF * T].rearrange("p (f t) -> p f t", t=T)
        sqs = [sb("sq%d" % i, [128, T], BF16) for i in range(2)]; b_sqs = [Buf("sq%d" % i) for i in range(2)]
        rstd = sb("rstd", [128, T]); b_rstd = Buf("rstd")
        lnt = sb("lnt", [128, T]); b_lnt = Buf("lnt")
        tmpA = [sb("tmpA%d" % i, [128, T]) for i in range(2)]; b_tmpA = [Buf("tmpA%d" % i) for i in range(2)]
        tmpB = [sb("tmpB%d" % i, [128, T]) for i in range(2)]; b_tmpB = [Buf("tmpB%d" % i) for i in range(2)]
        cu = [sb("cu%d" % i, [128, T + 2]) for i in range(2)]; b_cu = [Buf("cu%d" % i) for i in range(2)]
        cuh = sb("cuh", [128, 8, 2]); b_cuh = Buf("cuh")
        uph = sb("uph", [128, NFF, 2]); b_uph = Buf("uph")
        Et = [sb("Et%d" % i, [128, 2, T]) for i in range(2)]; b_E = [Buf("E%d" % i) for i in range(2)]
        Lt = [sb("Lt%d" % i, [128, 2, T], BF16) for i in range(2)]; b_L = [Buf("L%d" % i) for i in range(2)]
        At = [sb("At%d" % i, [128, 2, T], BF16) for i in range(2)]; b_A = [Buf("A%d" % i) for i in range(2)]
        Acc = [sb("Acc%d" % i, [128, 2, T], BF16) for i in range(3)]; b_Acc = [Buf("Acc%d" % i) for i in range(3)]
        b_kscr = [Buf("kscr%d" % s_) for s_ in range(NS)]
        b_vscr = [Buf("vscr%d" % s_) for s_ in range(NS)]
        for h in range(2):
            P.op(POOL, lambda e, h=h: e.memset(qz[h][:], 0.0), writes=[b_qz])
        ctr = {"xin": 0, "stg": 0, "tA": 0, "tB": 0, "cu": 0, "att": 0}

        def rr(key, n):
            i = ctr[key] % n
            ctr[key] += 1
            return i

        def rms_to_hT(Tn, s_, kind, xT, b_xT):
            rms_stats(Tn, xT, b_xT)
            if kind is not None:
                rms_apply(Tn, s_, kind, xT, b_xT)

        def rms_stats(Tn, xT, b_xT):
            sb_i = nextbank()
            for m in range(8):
                qi = m % 2
                if m % 2 == 0:
                    P.op(ACT, lambda e, m=m, qi=qi: e.activation(sqs[qi][:, 0:Tn], xT[:, m, 0:Tn], AF.Square), reads=[b_xT], writes=[b_sqs[qi]])
                else:
                    P.op(DVE, lambda e, m=m, qi=qi: e.tensor_tensor(sqs[qi][:, 0:Tn], xT[:, m, 0:Tn], xT[:, m, 0:Tn], ALU.mult),
                         reads=[b_xT], writes=[b_sqs[qi]])
                P.op(PE, lambda e, m=m, qi=qi: e.matmul(bk(sb_i, Tn), lhsT=ones[:], rhs=sqs[qi][:, 0:Tn], start=(m == 0), stop=(m == 7)),
                     reads=[b_sqs[qi]] + b_const_all, writes=[bank[sb_i]])
            P.op(ACT, lambda e: e.activation(lnt[:, 0:Tn], bk(sb_i, Tn), AF.Ln, scale=1.0 / D, bias=eps_t[:, 0:1]),
                 reads=[bank[sb_i], b_eps], writes=[b_lnt])
            P.op(ACT, lambda e: e.activation(rstd[:, 0:Tn], lnt[:, 0:Tn], AF.Exp, scale=-0.5), reads=[b_lnt], writes=[b_rstd])

        def rms_apply(Tn, s_, kind, xT, b_xT):
            shift0 = 0 if kind == 0 else 24
            for m in range(8):
                i = rr("tA", 2)
                P.op(DVE, lambda e, m=m, i=i: e.scalar_tensor_tensor(tmpA[i][:, 0:Tn], xT[:, m, 0:Tn], modv[:, s_, kind, m:m + 1], rstd[:, 0:Tn],
                                                                     ALU.mult, ALU.mult),
                     reads=[b_xT, b_rstd, b_modv], writes=[b_tmpA[i]])
                if m % 2 == 0:
                    P.op(POOL, lambda e, m=m, i=i: e.tensor_scalar(hT[:, m, 0:Tn], tmpA[i][:, 0:Tn], 1.0, mcol(shift0 + m, s_), ALU.mult, ALU.add),
                         reads=[b_tmpA[i], b_modT], writes=[b_hT])
                else:
                    P.op(ACT, lambda e, m=m, i=i: e.activation(hT[:, m, 0:Tn], tmpA[i][:, 0:Tn], AF.Identity, bias=mcol(shift0 + m, s_), scale=1.0),
                         reads=[b_tmpA[i], b_modT], writes=[b_hT])

        def fm_group(wv, j, rhs3, Tn, kcn, bi, rbufs, wbuf):
            for kc in range(kcn):
                P.op(PE, lambda e, kc=kc: e.matmul(bk(bi, Tn), lhsT=wv[:, kc, j * 128:(j + 1) * 128], rhs=rhs3[:, kc, 0:Tn],
                                                   start=(kc == 0), stop=(kc == kcn - 1)),
                     reads=[wbuf] + rbufs, writes=[bank[bi]])

        def conv3(dst_fn, src, Tn, wkey, bkey, j, i_out):
            P.op(DVE, lambda e: e.tensor_scalar(tmpB[i_out][:, 0:Tn], src[:, 2:Tn + 2], vcol((wkey, 2), j), vcol(bkey, j), ALU.mult, ALU.add),
                 reads=dst_fn[0], writes=[b_tmpB[i_out]])
            for tap in (1, 0):
                P.op(DVE, lambda e, tap=tap: e.scalar_tensor_tensor(tmpB[i_out][:, 0:Tn], src[:, tap:Tn + tap], vcol((wkey, tap), j),
                                                                     tmpB[i_out][:, 0:Tn], ALU.mult, ALU.add),
                     reads=dst_fn[0] + [b_tmpB[i_out]], writes=[b_tmpB[i_out]])

        eps_t = sb("eps_t", [128, 1]); b_eps = Buf("eps")
        P.op(POOL, lambda e: e.memset(eps_t[:], EPS), writes=[b_eps])

        tilectr = [0]

        def tile_front(Tn, xsrc, s_):
            par = tilectr[0] % 2
            tilectr[0] += 1
            xT, b_xT = xTs[par], b_xTs[par]
            tb = min(128, Tn)
            nb = Tn // tb
            for b in range(nb):
                xi = rr("xin", 2)
                P.dma(SP, lambda e, b=b, xi=xi: e.dma_start(out=xin[xi][0:tb, :], in_=xsrc[b * tb:(b + 1) * tb, :]), "xin%d" % xi,
                      writes=[b_xin[xi]])
                for hh in range(2):
                    bi = nextbank()
                    for j in range(4):
                        m = hh * 4 + j
                        P.op(PE, lambda e, j=j, m=m, xi=xi, bi=bi: e.transpose(ps[:, bi * 512 + j * 128:bi * 512 + j * 128 + tb],
                                                                               xin[xi][0:tb, m * 128:(m + 1) * 128], ident[0:tb, 0:tb]),
                             reads=[b_xin[xi]] + b_const_all, writes=[bank[bi]])
                    P.op(DVE, lambda e, hh=hh, b=b, bi=bi: e.tensor_copy(xT[:, hh * 4:hh * 4 + 4, b * tb:(b + 1) * tb],
                                                                        bk(bi).rearrange("p (j t) -> p j t", t=128)[:, :, 0:tb]),
                         reads=[bank[bi]], writes=[b_xT])
            rms_stats(Tn, xT, b_xT)
            rms_apply(Tn, s_, 0, xT, b_xT)
            return xT, b_xT

        def process_tile(s_, Tn, xsrc, ysrc, koutsrc, voutsrc, kpos0, units_blocks, last, conv_out, ffn_out, prev_finish=None,
                         front=None, next_front_fn=None):
            pre = front is not None
            if front is None:
                qw = [wload([(lambda t: wview_k(t, 512), wsrc(wb_in, g * 512, 512))], b_w["in"]) for g in range(2)]
                front = tile_front(Tn, xsrc, s_)
            xT, b_xT = front
            tb = min(128, Tn)
            nb = Tn // tb
            b_vt = [b_vT[0], b_vT[1], b_vT[2]]
            if pre:
                qw = [wload([(lambda t: wview_k(t, 512), wsrc(wb_in, g * 512, 512))], b_w["in"]) for g in range(2)]
            if prev_finish is not None:
                prev_finish[0]()
            for g in range(2):
                wt, wbuf = qw[g]
                wv = wview_k(wt, 512)
                for j in range(4):
                    m = g * 4 + j
                    bi = nextbank()
                    fm_group(wv, j, hT, Tn, 8, bi, [b_hT], wbuf)
                    P.op(ACT, lambda e, m=m, bi=bi: e.activation(qz[0][0:64, m, 0:Tn], ps[0:64, bi * 512:bi * 512 + Tn], AF.Copy, scale=0.125),
                         reads=[bank[bi]], writes=[b_qz])
                    P.op(DVE, lambda e, m=m, bi=bi: e.tensor_scalar(qz[1][64:128, m, 0:Tn], ps[64:128, bi * 512:bi * 512 + Tn], 0.125, None, ALU.mult),
                         reads=[bank[bi]], writes=[b_qz])
            for g in range(2, 6):
                wt, wbuf = wload([(lambda t: wview_k(t, 512), wsrc(wb_in, g * 512, 512))], b_w["in"])
                wv = wview_k(wt, 512)
                isk = g < 4
                half = g % 2
                if isk:
                    for j in range(4):
                        m = half * 4 + j
                        bi = nextbank()
                        fm_group(wv, j, hT, Tn, 8, bi, [b_hT], wbuf)
                        P.op(ACT if j % 2 == 0 else DVE,
                             (lambda e, m=m, bi=bi: e.activation(kTs[:, m, 0:Tn], bk(bi, Tn), AF.Copy)) if j % 2 == 0 else
                             (lambda e, m=m, bi=bi: e.tensor_copy(kTs[:, m, 0:Tn], bk(bi, Tn))),
                             reads=[bank[bi]], writes=[b_kTs])
                for b in range(nb if not isk else 0):
                    bi = nextbank()
                    for kc in range(8):
                        P.op(PE, lambda e, kc=kc, b=b, bi=bi, wv=wv: e.matmul(bk(bi, 512, tb), lhsT=hT[:, kc, b * tb:(b + 1) * tb], rhs=wv[:, kc, :],
                                                                             start=(kc == 0), stop=(kc == 7)),
                             reads=[wbuf, b_hT], writes=[bank[bi]])
                    si = rr("stg", 4)
                    P.op(DVE if b % 2 == 0 else ACT,
                         (lambda e, si=si, bi=bi: e.tensor_copy(stg[si][0:tb, :], bk(bi, 512, tb))) if b % 2 == 0 else
                         (lambda e, si=si, bi=bi: e.activation(stg[si][0:tb, :], bk(bi, 512, tb), AF.Copy)),
                         reads=[bank[bi]], writes=[b_stg[si]])
                    dst = koutsrc if isk else voutsrc
                    P.dma(POOL, lambda e, si=si, b=b, dst=dst, half=half: e.dma_start(out=dst[b * tb:(b + 1) * tb, half * 512:(half + 1) * 512],
                                                                                     in_=stg[si][0:tb, :]), "stg%d" % si, reads=[b_stg[si]])
                    if not isk:
                        P.dma(POOL, lambda e, si=si, b=b, half=half: e.dma_start(
                            out=vscr[s_, half * 4:half * 4 + 4, kpos0 + b * tb:kpos0 + (b + 1) * tb, :].rearrange("c t d -> t c d"),
                            in_=stg[si][0:tb, :].rearrange("t (c d) -> t c d", d=128)), "stgv%d" % si,
                            reads=[b_stg[si]], writes=[b_vscr[s_]])
                if isk and half == 1:
                    P.dma(POOL, lambda e: e.dma_start(out=kscr[s_, :, :, kpos0:kpos0 + Tn].rearrange("c p t -> p c t"), in_=kTs[:, :, 0:Tn]),
                          "kTs", reads=[b_kTs], writes=[b_kscr[s_]])
            if prev_finish is not None:
                prev_finish[1]()
            FB = 7

            def conv_gen():
                for m in range(8):
                    wt, wbuf = wload([(lambda t: wview_k(t, 384), wsrc(wb_cv, m * 384, 384))], b_w["cv"])
                    wv = wt[:, 0:3072].rearrange("p (k n) -> p k n", n=384)
                    ia = rr("tA", 2)
                    ci = rr("cu", 2)
                    io = rr("tB", 2)

                    def grp(j):
                        for kc in range(8):
                            P.op(PE, lambda e, kc=kc, j=j, wv=wv: e.matmul(bk(FB, Tn), lhsT=wv[:, kc, j * 128:(j + 1) * 128], rhs=hT[:, kc, 0:Tn],
                                                                    start=(kc == 0), stop=(kc == 7)),
                                 reads=[wbuf, b_hT], writes=[bank[FB]])
                            if kc in (1, 3, 5):
                                yield
                    yield from grp(1)
                    P.op(DVE, lambda e, ia=ia: e.tensor_copy(tmpA[ia][:, 0:Tn], bk(FB, Tn)), reads=[bank[FB]], writes=[b_tmpA[ia]])
                    yield
                    yield from grp(2)
                    P.op(POOL, lambda e, m=m, ci=ci: e.tensor_copy(cu[ci][:, 0:2], cuh[:, m, :]), reads=[b_cuh], writes=[b_cu[ci]])
                    P.op(DVE, lambda e, ia=ia, ci=ci: e.tensor_tensor(cu[ci][:, 2:Tn + 2], tmpA[ia][:, 0:Tn], bk(FB, Tn), ALU.mult),
                         reads=[b_tmpA[ia], bank[FB]], writes=[b_cu[ci]])
                    P.op(POOL, lambda e, m=m, ci=ci: e.tensor_copy(cuh[:, m, :], cu[ci][:, Tn:Tn + 2]), reads=[b_cu[ci]], writes=[b_cuh])
                    conv3(([b_cu[ci]] + b_vt,), cu[ci], Tn, "wc", "bc", m, io)
                    yield
                    yield from grp(0)
                    P.op(DVE, lambda e, m=m, io=io: e.tensor_tensor(obT[:, m, 0:Tn], tmpB[io][:, 0:Tn], bk(FB, Tn), ALU.mult),
                         reads=[b_tmpB[io], bank[FB]], writes=[b_obT])
                    yield

            def ktok_gen():
                for half in range(2):
                    wt, wbuf = wload([(lambda t: wview_k(t, 512), wsrc(wb_in, (2 + half) * 512, 512))], b_w["in"])
                    wv = wview_k(wt, 512)
                    for b in range(nb):
                        for kc in range(8):
                            P.op(PE, lambda e, kc=kc, b=b, wv=wv: e.matmul(bk(FB, 512, tb), lhsT=hT[:, kc, b * tb:(b + 1) * tb], rhs=wv[:, kc, :],
                                                                          start=(kc == 0), stop=(kc == 7)),
                                 reads=[wbuf, b_hT], writes=[bank[FB]])
                            if kc in (1, 3, 5):
                                yield
                        si = rr("stg", 4)
                        P.op(DVE, lambda e, si=si: e.tensor_copy(stg[si][0:tb, :], bk(FB, 512, tb)), reads=[bank[FB]], writes=[b_stg[si]])
                        P.dma(POOL, lambda e, si=si, b=b, half=half: e.dma_start(out=koutsrc[b * tb:(b + 1) * tb, half * 512:(half + 1) * 512],
                                                                                in_=stg[si][0:tb, :]), "stg%d" % si, reads=[b_stg[si]])
                        yield

            def all_fill():
                yield from conv_gen()
                yield from ktok_gen()

            filler = all_fill()
            attention(s_, Tn, units_blocks, filler)
            for _ in filler:
                pass
            if conv_out is not None:
                for r in range(2):
                    P.dma(POOL, lambda e, r=r: e.dma_start(out=conv_out[r, :].rearrange("(m p) -> p m", p=128), in_=cuh[:, :, r],
                                                           allow_slow_non_contiguous=True), "cvout", reads=[b_cuh], group=True)

            for m in range(8):
                wt, wbuf = wload([(lambda t: wview_k(t, 512), wsrc(wb_mg, m * 512, 512))], b_w["mg"])
                wfull = wview_k(wt, 512)
                wg = wfull[:, :, 0:256]
                wa = wfull[:, :, 256:384]
                wbb_ = wfull[:, :, 384:512]
                bga, bgb, bya, byb = nextbank(), nextbank(), nextbank(), nextbank()
                fm_group(wg, 0, hT, Tn, 8, bga, [b_hT], wbuf)
                fm_group(wg, 1, hT, Tn, 8, bgb, [b_hT], wbuf)
                fm_group(wa, 0, oT, Tn, 8, bya, [b_oT], wbuf)
                fm_group(wbb_, 0, obT, Tn, 8, byb, [b_obT], wbuf)
                i0 = rr("tA", 2); i1 = rr("tA", 2); j0 = rr("tB", 2); j1 = rr("tB", 2)
                P.op(ACT, lambda e, i0=i0, bga=bga: e.activation(tmpA[i0][:, 0:Tn], bk(bga, Tn), AF.Sigmoid), reads=[bank[bga]], writes=[b_tmpA[i0]])
                P.op(ACT, lambda e, i1=i1, bgb=bgb: e.activation(tmpA[i1][:, 0:Tn], bk(bgb, Tn), AF.Sigmoid), reads=[bank[bgb]], writes=[b_tmpA[i1]])
                P.op(DVE, lambda e, i0=i0, j0=j0, bya=bya: e.tensor_tensor(tmpB[j0][:, 0:Tn], tmpA[i0][:, 0:Tn], bk(bya, Tn), ALU.mult),
                     reads=[b_tmpA[i0], bank[bya]], writes=[b_tmpB[j0]])
                P.op(DVE, lambda e, i1=i1, j1=j1, byb=byb: e.tensor_tensor(tmpB[j1][:, 0:Tn], tmpA[i1][:, 0:Tn], bk(byb, Tn), ALU.mult),
                     reads=[b_tmpA[i1], bank[byb]], writes=[b_tmpB[j1]])
                P.op(POOL, lambda e, m=m, j0=j0, j1=j1: e.tensor_tensor(mT[:, m, 0:Tn], tmpB[j0][:, 0:Tn], tmpB[j1][:, 0:Tn], ALU.add),
                     reads=[b_tmpB[j0], b_tmpB[j1]], writes=[b_mT])
            for g in range(2):
                wt, wbuf = wload([(lambda t: wview_k(t, 512), wsrc(wb_out, g * 512, 512))], b_w["out"])
                wv = wview_k(wt, 512)
                for j in range(4):
                    m = g * 4 + j
                    bi = nextbank()
                    fm_group(wv, j, mT, Tn, 8, bi, [b_mT], wbuf)
                    P.op(DVE, lambda e, m=m, bi=bi: e.scalar_tensor_tensor(xT[:, m, 0:Tn], bk(bi, Tn), mcol(16 + m, s_), xT[:, m, 0:Tn], ALU.mult, ALU.add),
                         reads=[bank[bi], b_modT, b_xT], writes=[b_xT])
            rms_to_hT(Tn, s_, 1, xT, b_xT)
            for f in range(NFF):
                if f % 2 == 0:
                    upw = wload([(lambda t: wview_k(t, 512), wsrc(wb_up, f * 256, 512))], b_w["up"])
                wt, wbuf = upw
                wv = wview_k(wt, 512)[:, :, (f % 2) * 256:(f % 2) * 256 + 256]
                bu, bg = nextbank(), nextbank()
                fm_group(wv, 0, hT, Tn, 8, bu, [b_hT], wbuf)
                fm_group(wv, 1, hT, Tn, 8, bg, [b_hT], wbuf)
                ci = rr("cu", 2); io = rr("tB", 2); ia = rr("tA", 2)
                P.op(POOL, lambda e, f=f, ci=ci: e.tensor_copy(cu[ci][:, 0:2], uph[:, f, :]), reads=[b_uph], writes=[b_cu[ci]])
                P.op(ACT, lambda e, ci=ci, bu=bu: e.activation(cu[ci][:, 2:Tn + 2], bk(bu, Tn), AF.Copy), reads=[bank[bu]], writes=[b_cu[ci]])
                P.op(POOL, lambda e, f=f, ci=ci: e.tensor_copy(uph[:, f, :], cu[ci][:, Tn:Tn + 2]), reads=[b_cu[ci]], writes=[b_uph])
                conv3(([b_cu[ci]] + b_vt,), cu[ci], Tn, "wf", "bf", f, io)
                P.op(ACT, lambda e, io=io, ia=ia: e.activation(tmpA[ia][:, 0:Tn], tmpB[io][:, 0:Tn], AF.Silu), reads=[b_tmpB[io]], writes=[b_tmpA[ia]])
                P.op(DVE, lambda e, f=f, ia=ia, bg=bg: e.tensor_tensor(actT[:, f, 0:Tn], tmpA[ia][:, 0:Tn], bk(bg, Tn), ALU.mult),
                     reads=[b_tmpA[ia], bank[bg]], writes=b_big)
            if ffn_out is not None:
                for r in range(2):
                    P.dma(POOL, lambda e, r=r: e.dma_start(out=ffn_out[r, :].rearrange("(f p) -> p f", p=128), in_=uph[:, :, r],
                                                         allow_slow_non_contiguous=True), "ffout", reads=[b_uph], group=True)
            dnw = [wload([(lambda t: t[:, 0:NFF * 128].rearrange("p (k n) -> p k n", n=128),
                           wb_dn[m, :, :].rearrange("p (k n) -> p k n", n=128))], b_w["down"]) for m in range(2)]
            nfront = next_front_fn() if next_front_fn is not None else None
            for m in range(8):
                if m < 2:
                    wt, wbuf = dnw[m]
                else:
                    wt, wbuf = wload([(lambda t: t[:, 0:NFF * 128].rearrange("p (k n) -> p k n", n=128),
                                       wb_dn[m, :, :].rearrange("p (k n) -> p k n", n=128))], b_w["down"])
                wv = wt[:, 0:NFF * 128].rearrange("p (k n) -> p k n", n=128)
                bi = nextbank()
                fm_group(wv, 0, actT, Tn, NFF, bi, b_big, wbuf)
                P.op(DVE, lambda e, m=m, bi=bi: e.scalar_tensor_tensor(xT[:, m, 0:Tn], bk(bi, Tn), mcol(40 + m, s_), xT[:, m, 0:Tn], ALU.mult, ALU.add),
                     reads=[bank[bi], b_modT, b_xT], writes=[b_xT])
            def finish_a():
                rms_stats(Tn, xT, b_xT)
                for m in range(8):
                    P.op(DVE, lambda e, m=m: e.scalar_tensor_tensor(xT[:, m, 0:Tn], xT[:, m, 0:Tn], vcol("gf", m), rstd[:, 0:Tn], ALU.mult, ALU.mult),
                         reads=[b_xT, b_rstd, b_vT[0]], writes=[b_xT])

            def emit_J():
                for b in range(nb):
                    for hh in range(2):
                        bi = nextbank()
                        for j in range(4):
                            m = hh * 4 + j
                            P.op(PE, lambda e, j=j, m=m, b=b, bi=bi: e.transpose(ps[0:tb, bi * 512 + j * 128:bi * 512 + (j + 1) * 128],
                                                                                xT[:, m, b * tb:(b + 1) * tb], ident[:, :]),
                                 reads=[b_xT] + b_const_all, writes=[bank[bi]])
                        si = rr("stg", 4)
                        P.op(ACT if hh == 0 else DVE,
                             (lambda e, si=si, bi=bi: e.activation(stg[si][0:tb, :], bk(bi, 512, tb), AF.Copy)) if hh == 0 else
                             (lambda e, si=si, bi=bi: e.tensor_copy(stg[si][0:tb, :], bk(bi, 512, tb))),
                             reads=[bank[bi]], writes=[b_stg[si]])
                        P.dma(POOL, lambda e, si=si, b=b, hh=hh: e.dma_start(out=ysrc[b * tb:(b + 1) * tb, hh * 512:(hh + 1) * 512], in_=stg[si][0:tb, :]),
                              "stg%d" % si, reads=[b_stg[si]])

            return (finish_a, emit_J), nfront

        def attention(s_, Tn, blocks, filler=None):
            Tk = max(k0 + nk for (_, k0, nk, _, _) in blocks)
            units = []
            for c in range(8):
                for bi_, blk in enumerate(blocks):
                    units.append((c, bi_, blk))
            nblk = len(blocks)
            nu = len(units)
            ZB = [(0, 1), (2, 3)]
            AB = (4, 5)
            OB = 6
            slot_of = {}
            accslot = {}

            def load_c(c):
                sl = rr("att", 2)
                slot_of[c] = sl
                P.dma(SP, lambda e: e.dma_start(out=kTh[sl][:, 0:Tk], in_=kscr[s_, c, :, 0:Tk]), "kTh%d" % sl,
                      reads=[b_kscr[s_]], writes=[b_big[sl]])
                nfull = Tk // 128
                if nfull:
                    P.dma(SP, lambda e: e.dma_start(out=vh[sl][:, 0:nfull, :], in_=vscr[s_, c, 0:nfull * 128, :].rearrange("(k p) d -> p k d", p=128)),
                          "vh%d" % sl, reads=[b_vscr[s_]], writes=[b_big[2 + sl]])
                rem = Tk - nfull * 128
                if rem:
                    P.dma(SP, lambda e: e.dma_start(out=vh[sl][0:rem, nfull, :], in_=vscr[s_, c, nfull * 128:Tk, :]),
                          "vh%d" % sl, reads=[b_vscr[s_]], writes=[b_big[2 + sl]], group=bool(nfull))

            def zviews(bpair, nk, q0):
                return [ps[0:nk, bpair[h] * 512 + q0:bpair[h] * 512 + Tn] for h in range(2)]

            def st_z(u):
                c, bi_, (kbidx, k0, nk, q0, diag) = units[u]
                if bi_ == 0 and c == 0:
                    load_c(0)
                    load_c(1)
                sl = slot_of[c]
                zb = ZB[u % 2]
                for h in range(2):
                    P.op(PE, lambda e, h=h: e.matmul(ps[0:nk, zb[h] * 512 + q0:zb[h] * 512 + Tn], lhsT=kTh[sl][:, k0:k0 + nk],
                                                     rhs=qz[h][:, c, q0:Tn], start=True, stop=True),
                         reads=[b_big[sl], b_qz], writes=[bank[zb[h]]])

            def st_EL(u):
                c, bi_, (kbidx, k0, nk, q0, diag) = units[u]
                zb = ZB[u % 2]
                i = u % 2
                zin = ps[0:nk, zb[0] * 512:zb[0] * 512 + 1024].rearrange("p (h t) -> p h t", t=512)[:, :, q0:Tn]
                P.op(ACT, lambda e: e.activation(Et[i][0:nk, :, q0:Tn], zin, AF.Exp), reads=[bank[zb[0]], bank[zb[1]]], writes=[b_E[i]])
                P.op(ACT, lambda e: e.activation(Lt[i][0:nk, :, q0:Tn], Et[i][0:nk, :, q0:Tn], AF.Ln, bias=one_t[0:nk, 0:1]),
                     reads=[b_E[i], b_eps], writes=[b_L[i]])
                if diag:
                    for h in range(2):
                        P.op(POOL, lambda e, h=h: e.tensor_tensor(Lt[i][0:nk, h, q0:q0 + nk], Lt[i][0:nk, h, q0:q0 + nk], m01[0:nk, 0:nk], ALU.mult),
                             reads=[b_L[i]] + b_const_all, writes=[b_L[i]])
                if bi_ == 0:
                    accslot[(c, 0)] = None
                if bi_ < nblk - 1:
                    a_new = rr_acc()
                    if bi_ == 0:
                        P.op(POOL, lambda e: e.memset(Acc[a_new][:], 0.0), writes=[b_Acc[a_new]])
                        P.op(POOL, lambda e: e.tensor_copy(Acc[a_new][0:nk, :, q0:Tn], Lt[i][0:nk, :, q0:Tn]), reads=[b_L[i]], writes=[b_Acc[a_new]])
                    else:
                        a_old = accslot[(c, bi_)]
                        if q0 > 0:
                            P.op(POOL, lambda e: e.memset(Acc[a_new][:, :, 0:q0], 0.0), writes=[b_Acc[a_new]])
                        P.op(DVE, lambda e: e.tensor_tensor(Acc[a_new][:, :, q0:Tn], Acc[a_old][:, :, q0:Tn], Lt[i][:, :, q0:Tn], ALU.add),
                             reads=[b_Acc[a_old], b_L[i]], writes=[b_Acc[a_new]])
                    accslot[(c, bi_ + 1)] = a_new

            accc = [0]

            def rr_acc():
                accc[0] += 1
                return accc[0] % 3

            def st_arg(u):
                c, bi_, (kbidx, k0, nk, q0, diag) = units[u]
                sl = slot_of[c]
                i = u % 2
                a_in = accslot[(c, bi_)]
                for h in range(2):
                    out = ps[0:nk, AB[h] * 512 + q0:AB[h] * 512 + Tn]
                    P.op(PE, lambda e, h=h, out=out: e.matmul(out, lhsT=kTh[sl][:, k0:k0 + nk], rhs=qz[h][:, c, q0:Tn], start=True, stop=False),
                         reads=[b_big[sl], b_qz], writes=[bank[AB[h]]])
                    P.op(PE, lambda e, h=h, out=out: e.matmul(out, lhsT=ntri[0:nk, 0:nk], rhs=Lt[i][0:nk, h, q0:Tn], start=False, stop=(a_in is None)),
                         reads=[b_L[i]] + b_const_all, writes=[bank[AB[h]]])
                    if a_in is not None:
                        P.op(PE, lambda e, h=h, out=out: e.matmul(out, lhsT=nones[:, 0:nk], rhs=Acc[a_in][:, h, q0:Tn], start=False, stop=True),
                             reads=[b_Acc[a_in]] + b_const_all, writes=[bank[AB[h]]])

            def st_a(u):
                c, bi_, (kbidx, k0, nk, q0, diag) = units[u]
                i = u % 2
                ain = ps[0:nk, AB[0] * 512:AB[0] * 512 + 1024].rearrange("p (h t) -> p h t", t=512)[:, :, q0:Tn]
                P.op(ACT, lambda e: e.activation(At[i][0:nk, :, q0:Tn], ain, AF.Exp), reads=[bank[AB[0]], bank[AB[1]]], writes=[b_A[i]])
                if diag:
                    for h in range(2):
                        P.op(POOL, lambda e, h=h: e.tensor_tensor(At[i][0:nk, h, q0:q0 + nk], At[i][0:nk, h, q0:q0 + nk], m01[0:nk, 0:nk], ALU.mult),
                             reads=[b_A[i]] + b_const_all, writes=[b_A[i]])

            def st_av(u):
                c, bi_, (kbidx, k0, nk, q0, diag) = units[u]
                sl = slot_of[c]
                i = u % 2
                if bi_ == 0:
                    P.op(PE, lambda e: e.matmul(ps[:, OB * 512:OB * 512 + Tn], lhsT=ones[:, :], rhs=zer[:, 0:Tn],
                                                start=True, stop=False, skip_group_check=True),
                         reads=b_const_all, writes=[bank[OB]])
                for h in range(2):
                    P.op(PE, lambda e, h=h: e.matmul(ps[64 * h:64 * h + 64, OB * 512 + q0:OB * 512 + Tn], lhsT=vh[sl][0:nk, kbidx, 64 * h:64 * h + 64],
                                                     rhs=At[i][0:nk, h, q0:Tn], start=False, stop=(bi_ == nblk - 1 and h == 1), skip_group_check=True),
                         reads=[b_big[2 + sl], b_A[i]], writes=[bank[OB]])
                if bi_ == nblk - 1 and c + 2 < 8:
                    load_c(c + 2)
                if bi_ == nblk - 1:
                    P.op(DVE, lambda e: e.tensor_copy(oT[:, c, 0:Tn], ps[:, OB * 512:OB * 512 + Tn]), reads=[bank[OB]], writes=[b_oT])

            for s in range(-2, nu + 1):
                if 0 <= s < nu:
                    st_arg(s)
                if filler is not None and s >= 0:
                    next(filler, None)
                if 0 <= s + 2 < nu:
                    st_z(s + 2)
                if 0 <= s - 1 < nu:
                    st_av(s - 1)
                if 0 <= s + 1 < nu:
                    st_EL(s + 1)
                if 0 <= s < nu:
                    st_a(s)

        one_t = sb("one_t", [128, 1])
        P.op(POOL, lambda e: e.memset(one_t[:], 1.0), writes=[b_eps])

        for kb in range(PAST // 128):
            xi = rr("xin", 2)
            P.dma(SP, lambda e, kb=kb, xi=xi: e.dma_start(out=xin[xi][:, :], in_=ck[kb * 128:(kb + 1) * 128, :]), "xin%d" % xi, writes=[b_xin[xi]])
            for hh in range(2):
                bi = nextbank()
                for j in range(4):
                    m = hh * 4 + j
                    P.op(PE, lambda e, j=j, m=m, xi=xi, bi=bi: e.transpose(ps[:, bi * 512 + j * 128:bi * 512 + (j + 1) * 128],
                                                                           xin[xi][:, m * 128:(m + 1) * 128], ident[:, :]),
                         reads=[b_xin[xi]] + b_const_all, writes=[bank[bi]])
                P.op(DVE, lambda e, hh=hh, kb=kb, bi=bi: e.tensor_copy(kTs[:, hh * 4:hh * 4 + 4, (kb % 4) * 128:(kb % 4 + 1) * 128],
                                                                      bk(bi).rearrange("p (j t) -> p j t", t=128)),
                     reads=[bank[bi]], writes=[b_kTs])
            if kb % 4 == 3:
                k0 = (kb - 3) * 128
                P.dma(SP, lambda e, k0=k0: e.dma_start(out=kscr[NP, :, :, k0:k0 + 512].rearrange("c p t -> p c t"), in_=kTs[:, :, :]),
                      "kTs", reads=[b_kTs], writes=[b_kscr[NP]])
        for kb in range(PAST // 128):
            P.dma(POOL, lambda e, kb=kb: e.dma_start(out=vscr[NP, :, kb * 128:(kb + 1) * 128, :].rearrange("c t d -> t c d"),
                                                     in_=cv[kb * 128:(kb + 1) * 128, :].rearrange("t (c d) -> t c d", d=128)),
                  "cvcast", writes=[b_vscr[NP]], group=True)

        pend = [None]
        tiles = [(s_, i) for s_ in range(NP) for i in range(NT)]
        nfr = None
        for ti, (s_, i) in enumerate(tiles):
            if i == 0:
                P.op(POOL, lambda e: e.memset(cuh[:], 0.0), writes=[b_cuh])
                P.op(POOL, lambda e: e.memset(uph[:], 0.0), writes=[b_uph])
            r0 = s_ * SEQ + i * T
            blocks = []
            for kb in range(4 * i + 3, -1, -1):
                j = kb - 4 * i
                blocks.append((kb, kb * 128, 128, 128 * j if j >= 0 else 0, j >= 0))
            last = (i == NT - 1)
            nff = None
            if ti + 1 < len(tiles):
                s2, i2 = tiles[ti + 1]
                r2 = s2 * SEQ + i2 * T
                nff = (lambda r2=r2, s2=s2: tile_front(T, xp[r2:r2 + T, :], s2))
            pend[0], nfr = process_tile(s_, T, xp[r0:r0 + T, :], yp[r0:r0 + T, :], nkp[r0:r0 + T, :], nvp[r0:r0 + T, :], i * T, blocks, last,
                                        ncp[2 * s_:2 * s_ + 2, :] if last else None, nfp[2 * s_:2 * s_ + 2, :] if last else None, pend[0],
                                        front=nfr, next_front_fn=nff)

        s_ = NP
        for r in range(2):
            P.op(POOL, lambda e, r=r: e.tensor_copy(cuh[:, :, r], vcols(("sc", r), 8)), reads=[b_vT[1]], writes=[b_cuh])
            P.op(POOL, lambda e, r=r: e.tensor_copy(uph[:, :, r], vcols(("sf", r), NFF)), reads=[b_vT[2]], writes=[b_uph])
        blocks = [(PAST // 128, PAST, SAMP, 0, True)] + [(kb, kb * 128, 128, 0, False) for kb in range(PAST // 128 - 1, -1, -1)]
        fin, _ = process_tile(s_, SAMP, xs, ys, nks, nvs, PAST, blocks, True, ncs, nfs, pend[0])
        fin[0]()
        fin[1]()

        P.emit(nc)
    return nc


_NC_CACHE = {}


def _prep_maps(inp, NP, SEQ, ncores):
    f = lambda a: np.ascontiguousarray(a, dtype=np.float32)
    shared = {
        "w_ada": f(inp["w_ada"][0]), "b_ada": f(inp["b_ada"][0]), "g1": f(inp["g_norm1"][0]), "w_in": f(inp["w_in"][0]),
        "w_conv": f(inp["w_conv"][0]), "b_conv": f(inp["b_conv"][0]), "w_ba": f(inp["w_branch_a"][0]),
        "w_bb": f(inp["w_branch_b"][0]), "w_out": f(inp["w_out"][0]), "g2": f(inp["g_norm2"][0]), "w_up": f(inp["w_up"][0]),
        "w_fconv": f(inp["w_fconv"][0]), "b_fconv": f(inp["b_fconv"][0]), "w_down": f(inp["w_down"][0]), "gfin": f(inp["g_final"]),
    }
    maps = []
    for c in range(ncores):
        m = dict(shared)
        m["xp"] = f(inp["x_prompt"][c * NP:(c + 1) * NP].reshape(NP * SEQ, D))
        m["xs"] = f(inp["x_sample"][c])
        m["ck"] = f(inp["cache_k"][0, c].reshape(-1, D))
        m["cv"] = f(inp["cache_v"][0, c].reshape(-1, D))
        m["sc"] = f(inp["state_conv"][0, c])
        m["sf"] = f(inp["state_ffn_conv"][0, c])
        m["cvec"] = f(np.concatenate([inp["c_prompt"][c * NP:(c + 1) * NP], inp["c_sample"][c:c + 1]], axis=0))
        maps.append(m)
    return maps


def run(inp, NP, SEQ, ncores, PAST=1024, SAMP=32):
    key = (SEQ, NP, PAST, SAMP)
    if key not in _NC_CACHE:
        _NC_CACHE[key] = build(SEQ, NP, PAST, SAMP)
    nc = _NC_CACHE[key]
    maps = _prep_maps(inp, NP, SEQ, ncores)
    res = run_bass_kernel_spmd(nc, maps, core_ids=list(range(ncores)))
    rs = res.results
    cat = lambda k: np.concatenate([r[k] for r in rs], axis=0)
    B = NP * ncores
    y_p = cat("yp").reshape(B, SEQ, D)
    y_s = cat("ys").reshape(ncores, SAMP, D)
    nk_p = cat("nkp").reshape(1, B, SEQ, NH, 64)
    nv_p = cat("nvp").reshape(1, B, SEQ, NH, 64)
    nc_p = cat("ncp").reshape(1, B, 2, D)
    nf_p = cat("nfp").reshape(1, B, 2, DFF)
    nk_s = cat("nks").reshape(1, ncores, SAMP, NH, 64)
    nv_s = cat("nvs").reshape(1, ncores, SAMP, NH, 64)
    nc_s = cat("ncs").reshape(1, ncores, 2, D)
    nf_s = cat("nfs").reshape(1, ncores, 2, DFF)
    return tuple(np.ascontiguousarray(a, dtype=np.float32) for a in (y_p, y_s, nk_p, nv_p, nc_p, nf_p, nk_s, nv_s, nc_s, nf_s))


def kernel(**inputs):
    return run(inputs, NP=2, SEQ=4096, ncores=8)
```
